# Optimizing a Trainium2 kernel written in Bass

```python
import math
import jax
import jax.numpy as jnp
from jax import lax
import numpy as np

D_MODEL = 2048
BATCH = 8
SEQ = 2048
DEPTH = 2
DEC_BATCH = 32
DEC_SEQ = 32
PAST_LEN = 4096

CHUNK = 64
BAND_CHUNKS = 8
BAND_PAST = BAND_CHUNKS * CHUNK
N_MEM = 256
EPS = 1e-6
NEG_INF = -1e30
Q_BLOCK = 128

BR_W = D_MODEL // 2
HD_A = 128
H_A = BR_W // HD_A
REL_CLIP = 128
DIFF_HD = 64
H_B = BR_W // (2 * DIFF_HD)
ROT_DIM = DIFF_HD // 4
ROPE_THETA = 500000.0
H_M = 4
HD_M = BR_W // H_M
N_BRANCH = 3
D_FF = ((8 * D_MODEL // 3 + 127) // 128) * 128
CONV_W = 3
SPLITS = (BR_W, 2 * BR_W, 3 * BR_W, 4 * BR_W, 5 * BR_W, 6 * BR_W, 7 * BR_W)
N_IN = 7 * BR_W + N_BRANCH * D_MODEL

kernel_name = 'hybrid_chunk_stream_encoder_step'


def rms_norm(x, g):
    xf = x.astype(jnp.float32)
    y = xf * lax.rsqrt(jnp.mean(xf * xf, axis=-1, keepdims=True) + EPS)
    return (y * g.astype(jnp.float32)).astype(x.dtype)


def rope_partial(x, pos):
    half = ROT_DIM // 2
    inv_freq = jnp.exp(jnp.arange(half, dtype=jnp.float32) * (-2.0 * math.log(ROPE_THETA) / ROT_DIM))
    ang = pos.astype(jnp.float32)[:, None] * inv_freq[None, :]
    shape = (ang.shape[0],) + (1,) * (x.ndim - 3) + (half,)
    cos = jnp.cos(ang).reshape(shape)
    sin = jnp.sin(ang).reshape(shape)
    xf = x.astype(jnp.float32)
    x1 = xf[..., :half]
    x2 = xf[..., half:ROT_DIM]
    out = jnp.concatenate([x1 * cos - x2 * sin, x2 * cos + x1 * sin, xf[..., ROT_DIM:]], axis=-1)
    return out.astype(x.dtype)


def mixer_inputs(h, pos, P, l):
    B, T, _ = h.shape
    z = h @ P['w_in'][l]
    qa, ka, va, qb, kb, vb, qm, gt = jnp.split(z, SPLITS, axis=-1)
    qa = rms_norm(qa.reshape(B, T, H_A, HD_A), P['a_q_norm_g'][l])
    ka = rms_norm(ka.reshape(B, T, H_A, HD_A), P['a_k_norm_g'][l])
    va = va.reshape(B, T, H_A, HD_A)
    qb = rope_partial(rms_norm(qb.reshape(B, T, H_B, 2, DIFF_HD), P['b_q_norm_g'][l]), pos)
    kb = rope_partial(rms_norm(kb.reshape(B, T, H_B, 2, DIFF_HD), P['b_k_norm_g'][l]), pos)
    kb = kb.reshape(B, T, H_B, 2 * DIFF_HD)
    vb = vb.reshape(B, T, H_B, 2 * DIFF_HD)
    qm = rms_norm(qm.reshape(B, T, H_M, HD_M), P['m_q_norm_g'][l])
    gates = jax.nn.sigmoid(gt.reshape(B, T, N_BRANCH, D_MODEL) + P['gate_b'][l])
    return qa, ka, va, qb, kb, vb, qm, gates


def band_attention(q, k, v, rel, valid, bias_tab):
    s = jnp.einsum('bqhd,bkhd->bhqk', q, k).astype(jnp.float32) * (HD_A ** -0.5)
    bias = bias_tab[:, jnp.clip(rel, -REL_CLIP, REL_CLIP) + REL_CLIP].astype(jnp.float32)
    s = jnp.where(valid, s + bias, NEG_INF)
    p = jax.nn.softmax(s, axis=-1)
    return jnp.einsum('bhqk,bkhd->bqhd', p.astype(v.dtype), v)


def chunk_band_prompt(q, k, v, bias_tab):
    B, T, H, D = q.shape
    band = BAND_PAST + CHUNK
    pad = jnp.zeros((B, BAND_PAST, H, D), k.dtype)
    kp = jnp.concatenate([pad, k], axis=1)
    vp = jnp.concatenate([pad.astype(v.dtype), v], axis=1)
    r = jnp.arange(band)
    rel = r[None, :] - BAND_PAST - jnp.arange(CHUNK)[:, None]

    def one_chunk(c):
        start = c * CHUNK
        qc = lax.dynamic_slice_in_dim(q, start, CHUNK, axis=1)
        kc = lax.dynamic_slice_in_dim(kp, start, band, axis=1)
        vc = lax.dynamic_slice_in_dim(vp, start, band, axis=1)
        valid = jnp.broadcast_to((start - BAND_PAST + r >= 0)[None, :], (CHUNK, band))
        return band_attention(qc, kc, vc, rel, valid, bias_tab)

    out = lax.map(one_chunk, jnp.arange(T // CHUNK))
    return jnp.swapaxes(out, 0, 1).reshape(B, T, H * D)


def diff_lambda(P, l):
    lam_init = 0.8 - 0.6 * math.exp(-0.3 * l)
    q1 = P['b_lam_q1'][l].astype(jnp.float32)
    k1 = P['b_lam_k1'][l].astype(jnp.float32)
    q2 = P['b_lam_q2'][l].astype(jnp.float32)
    k2 = P['b_lam_k2'][l].astype(jnp.float32)
    lam = jnp.exp(jnp.sum(q1 * k1)) - jnp.exp(jnp.sum(q2 * k2)) + lam_init
    return lam, lam_init


def diff_attention(q, k, v, valid, lam):
    s = jnp.einsum('bqhmd,bkhmd->bhmqk', q, k).astype(jnp.float32) * (DIFF_HD ** -0.5)
    s = jnp.where(valid, s, NEG_INF)
    p = jax.nn.softmax(s, axis=-1)
    w = p[:, :, 0] - lam * p[:, :, 1]
    return jnp.einsum('bhqk,bkhe->bqhe', w.astype(v.dtype), v)


def diff_prompt(q, k, v, lam):
    B, T = q.shape[:2]
    k_chunk = jnp.arange(T) // CHUNK

    def one_block(i):
        start = i * Q_BLOCK
        qb = lax.dynamic_slice_in_dim(q, start, Q_BLOCK, axis=1)
        q_chunk = (start + jnp.arange(Q_BLOCK)) // CHUNK
        valid = k_chunk[None, :] <= q_chunk[:, None]
        return diff_attention(qb, k, v, valid, lam)

    out = lax.map(one_block, jnp.arange(T // Q_BLOCK))
    return jnp.swapaxes(out, 0, 1).reshape(B, T, H_B, 2 * DIFF_HD)


def diff_post(o, lam_init, P, l):
    B, T = o.shape[:2]
    o = rms_norm(o, P['b_subln_g'][l]) * (1.0 - lam_init)
    return o.reshape(B, T, BR_W)


def memory_kv(mem, P, l):
    B, N, _ = mem.shape
    hm = rms_norm(mem, P['mem_norm_g'][l])
    mk, mv = jnp.split(hm @ P['w_mem_kv'][l], 2, axis=-1)
    mk = rms_norm(mk.reshape(B, N, H_M, HD_M), P['m_k_norm_g'][l])
    return mk, mv.reshape(B, N, H_M, HD_M)


def memory_attention(q, k, v):
    B, T = q.shape[:2]
    s = jnp.einsum('bqhd,bkhd->bhqk', q, k).astype(jnp.float32) * (HD_M ** -0.5)
    p = jax.nn.softmax(s, axis=-1)
    return jnp.einsum('bhqk,bkhd->bqhd', p.astype(v.dtype), v).reshape(B, T, BR_W)


def merge_branches(outs, gates, P, l):
    w_br = P['w_branch'][l]
    merged = sum(gates[:, :, n] * (o @ w_br[n]) for n, o in enumerate(outs))
    return merged @ P['w_out'][l]


def conv_ffn(x, prev, P, l):
    h = rms_norm(x, P['norm_ffn_g'][l])
    a, b = jnp.split(h @ P['w_ffn_up'][l], 2, axis=-1)
    T = a.shape[1]
    ap = jnp.concatenate([prev.astype(a.dtype), a], axis=1)
    w = P['ffn_conv_w'][l]
    c = P['ffn_conv_b'][l] + sum(ap[:, j:j + T] * w[j] for j in range(CONV_W))
    y = (jax.nn.gelu(c, approximate=False) * b) @ P['w_ffn_down'][l]
    return y, ap[:, T:]


def setup_inputs(seed: int = 0) -> dict:
    key = jax.random.key(seed)
    keys = jax.random.split(key, 34)

    def nrm(i, shape, scale):
        return jax.random.normal(keys[i], shape, jnp.float32) * scale

    def gain(i, shape):
        return 1.0 + nrm(i, shape, 0.02)

    a_len = min(BAND_PAST, PAST_LEN)
    return {
        'x_prompt': nrm(0, (BATCH, SEQ, D_MODEL), 1.0),
        'x_sample': nrm(1, (DEC_BATCH, DEC_SEQ, D_MODEL), 1.0),
        'cache_a_k': nrm(2, (DEPTH, DEC_BATCH, a_len, H_A, HD_A), 1.0),
        'cache_a_v': nrm(3, (DEPTH, DEC_BATCH, a_len, H_A, HD_A), 1.0),
        'cache_b_k': nrm(4, (DEPTH, DEC_BATCH, PAST_LEN, H_B, 2 * DIFF_HD), 1.0),
        'cache_b_v': nrm(5, (DEPTH, DEC_BATCH, PAST_LEN, H_B, 2 * DIFF_HD), 1.0),
        'cache_mem_k': nrm(6, (DEPTH, DEC_BATCH, N_MEM, H_M, HD_M), 1.0),
        'cache_mem_v': nrm(7, (DEPTH, DEC_BATCH, N_MEM, H_M, HD_M), 1.0),
        'state_ffn_conv': nrm(8, (DEPTH, DEC_BATCH, CONV_W - 1, D_FF), 1.0),
        'mem_prompt': nrm(9, (BATCH, N_MEM, D_MODEL), 1.0),
        'norm_mix_g': gain(10, (DEPTH, D_MODEL)),
        'w_in': nrm(11, (DEPTH, D_MODEL, N_IN), D_MODEL ** -0.5),
        'a_q_norm_g': gain(12, (DEPTH, HD_A)),
        'a_k_norm_g': gain(13, (DEPTH, HD_A)),
        'a_rel_bias': nrm(14, (DEPTH, H_A, 2 * REL_CLIP + 1), 0.5),
        'b_q_norm_g': gain(15, (DEPTH, DIFF_HD)),
        'b_k_norm_g': gain(16, (DEPTH, DIFF_HD)),
        'b_lam_q1': nrm(17, (DEPTH, DIFF_HD), 0.1),
        'b_lam_k1': nrm(18, (DEPTH, DIFF_HD), 0.1),
        'b_lam_q2': nrm(19, (DEPTH, DIFF_HD), 0.1),
        'b_lam_k2': nrm(20, (DEPTH, DIFF_HD), 0.1),
        'b_subln_g': gain(21, (DEPTH, 2 * DIFF_HD)),
        'm_q_norm_g': gain(22, (DEPTH, HD_M)),
        'm_k_norm_g': gain(23, (DEPTH, HD_M)),
        'mem_norm_g': gain(24, (DEPTH, D_MODEL)),
        'w_mem_kv': nrm(25, (DEPTH, D_MODEL, 2 * BR_W), D_MODEL ** -0.5),
        'gate_b': nrm(26, (DEPTH, N_BRANCH, D_MODEL), 0.01),
        'w_branch': nrm(27, (DEPTH, N_BRANCH, BR_W, D_MODEL), BR_W ** -0.5),
        'w_out': nrm(28, (DEPTH, D_MODEL, D_MODEL), D_MODEL ** -0.5),
        'norm_ffn_g': gain(29, (DEPTH, D_MODEL)),
        'w_ffn_up': nrm(30, (DEPTH, D_MODEL, 2 * D_FF), D_MODEL ** -0.5),
        'ffn_conv_w': nrm(31, (DEPTH, CONV_W, D_FF), 0.5),
        'ffn_conv_b': nrm(32, (DEPTH, D_FF), 0.01),
        'w_ffn_down': nrm(33, (DEPTH, D_FF, D_MODEL), D_FF ** -0.5),
    }


def reference(x_prompt, x_sample, cache_a_k, cache_a_v, cache_b_k, cache_b_v, cache_mem_k,
              cache_mem_v, state_ffn_conv, mem_prompt, norm_mix_g, w_in, a_q_norm_g, a_k_norm_g,
              a_rel_bias, b_q_norm_g, b_k_norm_g, b_lam_q1, b_lam_k1, b_lam_q2, b_lam_k2,
              b_subln_g, m_q_norm_g, m_k_norm_g, mem_norm_g, w_mem_kv, gate_b, w_branch, w_out,
              norm_ffn_g, w_ffn_up, ffn_conv_w, ffn_conv_b, w_ffn_down):
    P = {'norm_mix_g': norm_mix_g, 'w_in': w_in, 'a_q_norm_g': a_q_norm_g,
         'a_k_norm_g': a_k_norm_g, 'b_q_norm_g': b_q_norm_g, 'b_k_norm_g': b_k_norm_g,
         'b_lam_q1': b_lam_q1, 'b_lam_k1': b_lam_k1, 'b_lam_q2': b_lam_q2, 'b_lam_k2': b_lam_k2,
         'b_subln_g': b_subln_g, 'm_q_norm_g': m_q_norm_g, 'm_k_norm_g': m_k_norm_g,
         'mem_norm_g': mem_norm_g, 'w_mem_kv': w_mem_kv, 'gate_b': gate_b, 'w_branch': w_branch,
         'w_out': w_out, 'norm_ffn_g': norm_ffn_g, 'w_ffn_up': w_ffn_up,
         'ffn_conv_w': ffn_conv_w, 'ffn_conv_b': ffn_conv_b, 'w_ffn_down': w_ffn_down}

    Bp, Tp, _ = x_prompt.shape
    Bs, Ts, _ = x_sample.shape
    pos_p = jnp.arange(Tp, dtype=jnp.int32)
    pos_s = PAST_LEN + jnp.arange(Ts, dtype=jnp.int32)
    a_keep = min(BAND_PAST, Tp)

    a_len = cache_a_k.shape[2]
    key_pos_a = jnp.concatenate([PAST_LEN - a_len + jnp.arange(a_len, dtype=jnp.int32), pos_s])
    rel_s = key_pos_a[None, :] - pos_s[:, None]
    q_chunk_s = pos_s // CHUNK
    k_chunk_a = key_pos_a // CHUNK
    valid_a_s = (k_chunk_a[None, :] <= q_chunk_s[:, None]) & (k_chunk_a[None, :] >= q_chunk_s[:, None] - BAND_CHUNKS)
    key_pos_b = jnp.concatenate([jnp.arange(PAST_LEN, dtype=jnp.int32), pos_s])
    valid_b_s = (key_pos_b // CHUNK)[None, :] <= q_chunk_s[:, None]

    xp = x_prompt
    xs = x_sample
    ak_p, av_p, bk_p, bv_p, mk_p, mv_p, cv_p = [], [], [], [], [], [], []
    ak_s, av_s, bk_s, bv_s, cv_s = [], [], [], [], []
    for l in range(DEPTH):
        lam, lam_init = diff_lambda(P, l)

        h = rms_norm(xp, norm_mix_g[l])
        qa, ka, va, qb, kb, vb, qm, gates = mixer_inputs(h, pos_p, P, l)
        oa = chunk_band_prompt(qa, ka, va, a_rel_bias[l])
        ob = diff_post(diff_prompt(qb, kb.reshape(Bp, Tp, H_B, 2, DIFF_HD), vb, lam), lam_init, P, l)
        mk, mv = memory_kv(mem_prompt, P, l)
        om = memory_attention(qm, mk, mv)
        xp = xp + merge_branches((oa, ob, om), gates, P, l)
        f, conv_new = conv_ffn(xp, jnp.zeros((Bp, CONV_W - 1, D_FF), xp.dtype), P, l)
        xp = xp + f
        ak_p.append(ka[:, Tp - a_keep:])
        av_p.append(va[:, Tp - a_keep:])
        bk_p.append(kb)
        bv_p.append(vb)
        mk_p.append(mk)
        mv_p.append(mv)
        cv_p.append(conv_new)

        h = rms_norm(xs, norm_mix_g[l])
        qa, ka, va, qb, kb, vb, qm, gates = mixer_inputs(h, pos_s, P, l)
        ka_band = jnp.concatenate([cache_a_k[l].astype(ka.dtype), ka], axis=1)
        va_band = jnp.concatenate([cache_a_v[l].astype(va.dtype), va], axis=1)
        oa = band_attention(qa, ka_band, va_band, rel_s, valid_a_s, a_rel_bias[l]).reshape(Bs, Ts, BR_W)
        kb_all = jnp.concatenate([cache_b_k[l].astype(kb.dtype), kb], axis=1)
        vb_all = jnp.concatenate([cache_b_v[l].astype(vb.dtype), vb], axis=1)
        ob = diff_attention(qb, kb_all.reshape(Bs, PAST_LEN + Ts, H_B, 2, DIFF_HD), vb_all, valid_b_s, lam)
        ob = diff_post(ob, lam_init, P, l)
        om = memory_attention(qm, cache_mem_k[l].astype(qm.dtype), cache_mem_v[l].astype(qm.dtype))
        xs = xs + merge_branches((oa, ob, om), gates, P, l)
        f, conv_new = conv_ffn(xs, state_ffn_conv[l], P, l)
        xs = xs + f
        ak_s.append(ka)
        av_s.append(va)
        bk_s.append(kb)
        bv_s.append(vb)
        cv_s.append(conv_new)

    return (xp, xs,
            jnp.stack(ak_p), jnp.stack(av_p), jnp.stack(bk_p), jnp.stack(bv_p),
            jnp.stack(mk_p), jnp.stack(mv_p), jnp.stack(cv_p),
            jnp.stack(ak_s), jnp.stack(av_s), jnp.stack(bk_s), jnp.stack(bv_s), jnp.stack(cv_s))
```

```python
import math
import numpy as np
import ml_dtypes
import concourse.bass as bass
import concourse.mybir as mybir
from concourse.bass_utils import run_bass_kernel_spmd

F32 = mybir.dt.float32
BF16 = mybir.dt.bfloat16
AF = mybir.ActivationFunctionType
ALU = mybir.AluOpType
AX = mybir.AxisListType

D = 2048
BR = 1024
NIN = 13312
DFF = 5504
NF = 43
EPS = 1e-6
THETA = 500000.0
ENGS = ("pe", "act", "dve", "pool", "sp")
SAME_SYNC = ("act", "dve", "pool")


class Buf:
    __slots__ = ("name", "w", "rs", "rd")

    def __init__(self, name):
        self.name = name
        self.w = None
        self.rs = {}
        self.rd = []


class Op:
    __slots__ = ("eng", "fn", "waits", "signal", "sigval", "dma", "key", "dval", "epoch")


class Sched:
    def __init__(self):
        self.q = {e: [] for e in ENGS}
        self.pending = {e: [] for e in ENGS}
        self.bufs = []
        self.dma_cnt = {}
        self.last_dma = {}
        self.epoch = 0

    def buf(self, name):
        b = Buf(name)
        self.bufs.append(b)
        return b

    def bufs_n(self, name, n):
        return [self.buf("%s%d" % (name, i)) for i in range(n)]

    def add(self, eng, fn, r=(), w=(), key=None, ndma=1):
        self.nops = getattr(self, "nops", 0) + 1
        if getattr(self, "stop_at", None) is not None and self.nops > self.stop_at and getattr(self, "in_loop", False):
            raise _Stop()
        op = Op()
        op.eng = eng
        op.fn = fn
        op.signal = False
        op.sigval = 0
        op.dma = key is not None
        op.key = (key, self.epoch) if key is not None else None
        op.epoch = self.epoch
        op.waits = list(self.pending[eng])
        self.pending[eng] = []
        deps = []
        for b in r:
            if b.w is not None:
                deps.append(b.w)
        for b in w:
            if b.w is not None:
                deps.append(b.w)
            deps.extend(b.rs.values())
            deps.extend(b.rd)
        for d in deps:
            if d.dma:
                op.waits.append(d)
            elif d.eng == eng:
                if eng in SAME_SYNC:
                    d.signal = True
                    op.waits.append(d)
            else:
                d.signal = True
                op.waits.append(d)
        if op.dma:
            c = self.dma_cnt.get(op.key, 0) + 16 * ndma
            self.dma_cnt[op.key] = c
            op.dval = c
            self.last_dma[op.key] = op
        for b in r:
            if op.dma:
                b.rd.append(op)
            else:
                b.rs[eng] = op
        for b in w:
            b.w = op
            b.rs = {}
            b.rd = []
        self.q[eng].append(op)
        return op

    def barrier(self, new_epoch=False):
        lasts = []
        for e in ENGS:
            if self.q[e]:
                o = self.q[e][-1]
                if not o.dma:
                    o.signal = True
                    lasts.append(o)
        lasts.extend(self.last_dma.values())
        self.last_dma = {}
        for e in ENGS:
            for o in lasts:
                if (not o.dma) and o.eng == e and e == "pe":
                    continue
                self.pending[e].append(o)
        for b in self.bufs:
            b.w = None
            b.rs = {}
            b.rd = []
        if new_epoch:
            self.epoch += 1

    def emit(self, nc, block_ctx, sem_ctx):
        for e in ENGS:
            cnt = {}
            for o in self.q[e]:
                if o.signal and not o.dma:
                    c = cnt.get(o.epoch, 0) + 1
                    cnt[o.epoch] = c
                    o.sigval = c
                    assert c < 32000, "semaphore value too large"
        for k, v in self.dma_cnt.items():
            assert v < 32000, ("dma sem too large", k, v)
        semcache = {}

        def sem_for(kind, key):
            return self._semmap[(kind, key)]

        handles = {"pe": "tensor", "act": "scalar", "dve": "vector", "pool": "gpsimd", "sp": "sync"}
        final_keys = list(self.dma_cnt.items())

        def run_engine(e, eng):
            waited = {}
            for o in self.q[e]:
                for d in o.waits:
                    if d.dma:
                        sk = ("d", d.key)
                        val = d.dval
                    else:
                        sk = ("e", (d.eng, d.epoch))
                        val = d.sigval
                    if waited.get(sk, 0) >= val:
                        continue
                    eng.wait_ge(sem_for(*sk), val)
                    waited[sk] = val
                if o.dma:
                    o.fn(eng, sem_for("d", o.key))
                else:
                    inst = o.fn(eng)
                    if o.signal:
                        inst.then_inc(sem_for("e", (e, o.epoch)), 1)
            if e == "sp":
                for k, v in final_keys:
                    if waited.get(("d", k), 0) < v:
                        eng.wait_ge(sem_for("d", k), v)

        return run_engine, handles


class _Stop(Exception):
    pass


class Cfg:
    def __init__(self, seq=2048, past=4096, depth=2, ngroups=2, dbg_skip_ffn=False, dbg_br=(0, 1, 2), dbg_layers=None):
        self.dbg_layers = dbg_layers
        self.dbg_skip_ffn = dbg_skip_ffn
        self.dbg_br = tuple(dbg_br)
        self.seq = seq
        self.past = past
        self.depth = depth
        self.npt = seq // 128
        self.akeep = min(512, seq)
        self.nct = past // 128
        base = self.npt // ngroups
        rem = self.npt % ngroups
        sizes = [base] * ngroups
        for i in range(rem):
            sizes[(1 + i) % ngroups if ngroups > 1 else 0] += 1
        self.groups = []
        p0 = 0
        for g in range(ngroups):
            tl = [("p", p) for p in range(p0, p0 + sizes[g])]
            p0 += sizes[g]
            if g == 0:
                tl.append(("s", 0))
            self.groups.append(tl)
        self.ntmax = max(len(g) for g in self.groups)
        self.tokmax = self.ntmax * 128


def tok_blocks(nt):
    out = []
    t = 0
    while t < nt:
        n = min(4, nt - t)
        out.append((t, n))
        t += n
    return out


def build(cfg):
    nc = bass.Bass("TRN2", target_bir_lowering=False)
    L = cfg.depth
    SEQ = cfg.seq
    NPT = cfg.npt
    NCT = cfg.nct
    AK = cfg.akeep
    TOKM = cfg.tokmax
    NTM = cfg.ntmax

    def din(name, shape, dt=F32):
        return nc.dram_tensor(name, list(shape), dt, kind="ExternalInput").ap()

    def dout(name, shape, dt=F32):
        return nc.dram_tensor(name, list(shape), dt, kind="ExternalOutput").ap()

    def dscr(name, shape, dt=F32):
        return nc.dram_tensor(name, list(shape), dt, kind="Internal").ap()

    xp = din("xp", [SEQ, D])
    xs = din("xs", [128, D])
    cak = din("cak", [L, 4, 512, BR])
    cav = din("cav", [L, 4, 512, BR])
    cbk = din("cbk", [L, 4, cfg.past, BR])
    cbv = din("cbv", [L, 4, cfg.past, BR])
    cmk = din("cmk", [L, 4, 256, BR])
    cmv = din("cmv", [L, 4, 256, BR])
    scv = din("scv", [L, 8, DFF])
    mem = din("mem", [256, D])
    norm_mix_g = din("norm_mix_g", [L, D])
    w_in = din("w_in", [L, D, NIN])
    a_q_g = din("a_q_norm_g", [L, 128])
    a_k_g = din("a_k_norm_g", [L, 128])
    a_rel = din("a_rel_bias", [L, 8, 257])
    b_q_g = din("b_q_norm_g", [L, 64])
    b_k_g = din("b_k_norm_g", [L, 64])
    lq1 = din("b_lam_q1", [L, 64])
    lk1 = din("b_lam_k1", [L, 64])
    lq2 = din("b_lam_q2", [L, 64])
    lk2 = din("b_lam_k2", [L, 64])
    subln_g = din("b_subln_g", [L, 128])
    m_q_g = din("m_q_norm_g", [L, 256])
    m_k_g = din("m_k_norm_g", [L, 256])
    mem_g = din("mem_norm_g", [L, D])
    w_mem = din("w_mem_kv", [L, D, 2 * BR])
    gate_b = din("gate_b", [L, 3, D])
    w_br = din("w_branch", [L, 3, BR, D])
    w_out = din("w_out", [L, D, D])
    ffn_g = din("norm_ffn_g", [L, D])
    w_up = din("w_ffn_up", [L, D, 2 * DFF])
    conv_w = din("ffn_conv_w", [L, 3, DFF])
    conv_b = din("ffn_conv_b", [L, DFF])
    w_dn = din("w_ffn_down", [L, DFF, D])
    c_idf = din("c_idf", [128, 128])
    c_idb = din("c_idb", [128, 128], BF16)
    c_jb = din("c_jb", [128, 128], BF16)
    c_cos = din("c_cos", [SEQ + 128, 64])
    c_sin = din("c_sin", [SEQ + 128, 64])

    yp = dout("yp", [SEQ, D])
    ys = dout("ys", [128, D])
    akp = dout("akp", [L, AK, BR])
    avp = dout("avp", [L, AK, BR])
    bkp = dout("bkp", [L, SEQ, BR])
    bvp = dout("bvp", [L, SEQ, BR])
    mkp = dout("mkp", [L, 256, BR])
    mvp = dout("mvp", [L, 256, BR])
    cvp = dout("cvp", [L, 2, DFF])
    aks = dout("aks", [L, 128, BR])
    avs = dout("avs", [L, 128, BR])
    bks = dout("bks", [L, 128, BR])
    bvs = dout("bvs", [L, 128, BR])
    cvs = dout("cvs", [L, 8, DFF])
    dbgx = dout("dbgx", [128, D]) if getattr(cfg, 'dbg_dump', False) else None
    dbg_state = {'n': 0}

    s_kTa = dscr("s_kTa", [128, 8, SEQ], BF16)
    s_va = dscr("s_va", [128, NPT, 8, 129], BF16)
    s_kTb = dscr("s_kTb", [128, 8, SEQ], BF16)
    s_vb = dscr("s_vb", [128, NPT, 8, 129], BF16)
    s_mkT = dscr("s_mkT", [128, 8, 256], BF16)
    s_mv = dscr("s_mv", [128, 2, 4, 257], BF16)
    s_ext = dscr("s_ext", [8, 384], F32)

    S = Sched()
    from contextlib import ExitStack
    es = ExitStack()

    def sb(name, shape, dt):
        return es.enter_context(nc.sbuf_tensor(name, list(shape), dt))

    def pst(name, shape, dt):
        return es.enter_context(nc.psum_tensor(name, list(shape), dt))

    hT = sb("hT", [128, 16, TOKM], BF16)
    A1SZ = max(NF * TOKM, 24576, 24 * TOKM + 4 * NPT * 128 + NPT * 4 * 129)
    A1 = sb("A1", [128, A1SZ], BF16)
    W = sb("W", [128, 24576], BF16)
    a1o = 0
    oT = A1[:, a1o:a1o + 8 * TOKM].rearrange("p (k t) -> p k t", k=8)
    a1o += 8 * TOKM
    mT = A1[:, a1o:a1o + 16 * TOKM].rearrange("p (k t) -> p k t", k=16)
    a1o += 16 * TOKM
    NKT = NPT
    wkT = A1[:, a1o:a1o + 4 * NKT * 128].rearrange("p (h t) -> p h t", h=4)
    a1o += 4 * NKT * 128
    wV = A1[:, a1o:a1o + NKT * 4 * 129].rearrange("p (t h e) -> p t h e", t=NKT, h=4)
    wVflat = A1[:, a1o:a1o + NKT * 4 * 129]
    a1o += NKT * 4 * 129
    assert a1o <= A1SZ, (a1o, A1SZ)
    xts = [A1[:, i * 4096:(i + 1) * 4096].bitcast(F32) for i in range(2)]
    gtn = A1[:, 8192:12288].bitcast(F32)
    hbs = [A1[:, 12288 + i * 2048:12288 + (i + 1) * 2048] for i in range(2)]
    uT = A1[:, 0:NF * TOKM].rearrange("p (f t) -> p f t", f=NF)

    NF512 = 6
    F5 = [sb("F5_%d" % i, [128, 512], F32) for i in range(NF512)]
    B5 = [sb("B5_%d" % i, [128, 512], BF16) for i in range(4)]
    idf = sb("idf", [128, 128], F32)
    idb = sb("idb", [128, 128], BF16)
    jb = sb("jb", [128, 128], BF16)
    hank = sb("hank", [128, 4, 2, 128], F32)
    c0t = sb("c0t", [128, 8], F32)
    gka = sb("gka", [128, 128], F32)
    gqa = sb("gqa", [128, 1], F32)
    gqb = sb("gqb", [128, 64], F32)
    gkb = sb("gkb", [128, 64], F32)
    gkm = sb("gkm", [128, 256], F32)
    gqm = sb("gqm", [128, 2], F32)
    gsub = sb("gsub", [128, 128], F32)
    gtb = sb("gtb", [128, 3, 16], F32)
    cw = sb("cw", [128, 3, NF], F32)
    cb = sb("cb", [128, NF], F32)
    lamv = sb("lamv", [128, 4, 64], F32)
    lams = sb("lams", [128, 8], F32)
    st = [sb("st%d" % i, [128, 16], F32) for i in range(4)]
    cst = [sb("cs%d" % i, [128, 64], F32) for i in range(2)]
    snt = [sb("sn%d" % i, [128, 64], F32) for i in range(2)]
    ET = [sb("ET%d" % i, [128, 256], BF16) for i in range(3)]
    EPP = [sb("EPP%d" % i, [128, 256], BF16) for i in range(4)]
    EP = [EPP[i][:, 0:128] for i in range(4)]
    SP_ = [sb("SPf%d" % i, [128, 128], F32) for i in range(2)]
    ON = [sb("ON%d" % i, [128, 256], F32) for i in range(2)]
    OB = [sb("OB%d" % i, [128, 256], BF16) for i in range(2)]
    qTt = [sb("qT%d" % i, [128, 4, 128], BF16) for i in range(2)]
    qTz = [sb("qTz%d" % i, [128, 4, 128], BF16) for i in range(2)]
    cV = [sb("cV%d" % i, [128, 4, 129], BF16) for i in range(2)]
    cVm = sb("cVm", [128, 2, 4, 257], BF16)
    mkT = sb("mkT", [128, 8, 256], BF16)
    skT = sb("skT", [128, 4, 128], BF16)
    sVn = sb("sVn", [128, 4, 129], BF16)
    acar = sb("acar", [128, NF, 2], F32)
    alast = sb("alast", [128, NF, 10], F32)
    abuf = sb("abuf", [128, 516], F32)
    shalo = sb("shalo", [128, NF, 8], F32)
    s8 = sb("s8", [8, 128], F32)
    o10 = sb("o10", [16, 128], F32)
    epsc = sb("epsc", [128, 1], F32)

    PS = [pst("ps%d" % i, [128, 512], F32) for i in range(8)]

    b_hT = S.buf("hT")
    b_oT = S.buf("oT")
    b_mT = S.buf("mT")
    b_wkT = S.buf("wkT")
    b_wV = S.buf("wV")
    b_uT = S.buf("uT")
    b_xt = S.bufs_n("xt", 2)
    b_gtn = S.buf("gtn")
    b_hb = S.bufs_n("hb", 2)
    b_F5 = S.bufs_n("F5", NF512)
    b_B5 = S.bufs_n("B5", 4)
    b_const = S.buf("const")
    b_lay = S.buf("layerparams")
    b_hank = S.buf("hank")
    b_st = S.bufs_n("st", 4)
    b_cs = S.bufs_n("cs", 2)
    b_ET = S.bufs_n("ET", 3)
    b_EP = S.bufs_n("EP", 4)
    b_EPb = S.bufs_n("EPb", 4)
    b_SPf = S.bufs_n("SPf", 2)
    b_ON = S.bufs_n("ON", 2)
    b_OB = S.bufs_n("OB", 2)
    b_qT = S.bufs_n("qT", 2)
    b_cV = S.bufs_n("cV", 2)
    b_cVm = S.buf("cVm")
    b_mkT = S.buf("mkT")
    b_skT = S.buf("skT")
    b_sVn = S.buf("sVn")
    b_acar = S.buf("acar")
    b_alast = S.buf("alast")
    b_abuf = S.buf("abuf")
    b_shalo = S.buf("shalo")
    b_s8 = S.buf("s8")
    b_o10 = S.buf("o10")
    b_PS = S.bufs_n("PS", 8)
    b_W = S.bufs_n("W", 3)
    b_dram = S.buf("dram_scr")
    b_dext = S.buf("dram_ext")

    rr = {"F5": 0, "B5": 0, "st": 0, "ET": 0, "SPf": 0, "ON": 0, "OB": 0, "qT": 0, "cV": 0, "cs": 0}

    def nxt(name, n):
        i = rr[name]
        rr[name] = (i + 1) % n
        return i

    def act(fn, r=(), w=()):
        return S.add("act", fn, r, w)

    def dve(fn, r=(), w=()):
        return S.add("dve", fn, r, w)

    def pe(fn, r=(), w=()):
        return S.add("pe", fn, r, w)

    def dma(q, outs_ins, r=(), w=(), key=None, **kw):
        pairs = list(outs_ins)

        def fn(eng, sem, pairs=pairs, kw=kw):
            for (o, i) in pairs:
                eng.dma_start(out=o, in_=i, **kw).then_inc(sem, 16)
        assert key is not None
        return S.add(q, fn, r, w, key=key, ndma=len(pairs))

    def A_exp(out, in_, bias=None, scale=1.0):
        if bias is None:
            return lambda e: e.activation(out=out, in_=in_, func=AF.Exp, scale=scale)
        return lambda e: e.activation(out=out, in_=in_, func=AF.Exp, bias=bias, scale=scale)

    dma("sp", [(idf[:], c_idf[:, :]), (idb[:], c_idb[:, :]), (jb[:], c_jb[:, :])], w=[b_const], key="const")
    dve(lambda e: e.memset(epsc[:], EPS), w=[b_const])
    for i in range(4):
        dve(lambda e, i=i: e.memset(EPP[i][:], 0.0), w=[b_EP[i]])
    dve(lambda e: e.memset(acar[:], 0.0), w=[b_acar])
    dve(lambda e: e.memset(alast[:], 0.0), w=[b_alast])

    LFIRST = cfg.dbg_layers[0] if cfg.dbg_layers is not None else 0

    def x_src(l, tile):
        if tile[0] == "p":
            base = xp if l == LFIRST else yp
            return base[tile[1] * 128:(tile[1] + 1) * 128, :]
        return (xs if l == LFIRST else ys)[:, :]

    def y_dst(tile):
        if tile[0] == "p":
            return yp[tile[1] * 128:(tile[1] + 1) * 128, :]
        return ys[:, :]

    def pos_row(tile):
        return tile[1] * 128 if tile[0] == "p" else SEQ

    def norm_stage(tiles, src_fn, g_dram_row, dstT, b_dst, psbase=0):
        dma("sp", [(gtn, g_dram_row.partition_broadcast(128))], w=[b_gtn], key="gtn")
        for ti, tile in enumerate(tiles):
            i = ti % 2
            dma("sp", [(xts[i], src_fn(tile))], w=[b_xt[i]], key="xt%d" % i)
            if dbgx is not None and dstT is hT:
                dbg_state['n'] += 1
                if dbg_state['n'] == cfg.dbg_dump:
                    dma("sp", [(dbgx[:, :], xts[i])], r=[b_xt[i]], key="dbgx")
                    if getattr(cfg, 'dbg_stop', False):
                        raise _Stop()
            si = nxt("st", 4)
            act(lambda e, i=i, si=si: e.activation(out=hbs[i], in_=xts[i], func=AF.Square, accum_out=st[si][:, 0:1]),
                r=[b_xt[i]], w=[b_hb[i], b_st[si]])
            act(lambda e, si=si: e.activation(out=st[si][:, 1:2], in_=st[si][:, 0:1], func=AF.Ln, bias=epsc[:], scale=1.0 / D),
                r=[b_st[si], b_const], w=[b_st[si]])
            act(lambda e, si=si: e.activation(out=st[si][:, 2:3], in_=st[si][:, 1:2], func=AF.Exp, scale=-0.5),
                r=[b_st[si]], w=[b_st[si]])
            dve(lambda e, i=i, si=si: e.scalar_tensor_tensor(out=hbs[i], in0=xts[i], scalar=st[si][:, 2:3], in1=gtn,
                                                            op0=ALU.mult, op1=ALU.mult),
                r=[b_xt[i], b_st[si], b_gtn], w=[b_hb[i]])
            for half in range(2):
                pi = psbase + (ti % 2) * 2 + half
                pv = PS[pi][:].bitcast(BF16)

                def tr(e, i=i, half=half, pv=pv):
                    ins = None
                    for k in range(8):
                        kc = half * 8 + k
                        ins = e.transpose(out=pv[:, k * 128:(k + 1) * 128], in_=hbs[i][:, kc * 128:(kc + 1) * 128], identity=idb[:])
                    return ins
                pe(tr, r=[b_hb[i], b_const], w=[b_PS[pi]])
                act(lambda e, half=half, pv=pv, ti=ti: e.activation(
                    out=dstT[:, half * 8:(half + 1) * 8, ti * 128:(ti + 1) * 128],
                    in_=pv[:, 0:1024].rearrange("p (k t) -> p k t", k=8), func=AF.Copy),
                    r=[b_PS[pi]], w=[b_dst])

    def load_w_block(slot, src_ap_3d, ncols, q="pool"):
        base = slot * 8192 if ncols == 512 else None
        dst = W[:, base:base + 16 * ncols].rearrange("p (k n) -> p k n", k=16)
        dma(q, [(dst, src_ap_3d.rearrange("(k p) n -> p k n", p=128))], w=[b_W[slot]], key="W%d" % slot)
        return dst

    wslot = [0]

    def next_slot(n=3):
        s = wslot[0]
        wslot[0] = (s + 1) % n
        return s

    zps = [0]

    def dense_tm(slot, wv, lhs_src, ti, ps_i, nk=16):
        def fn(e):
            ins = None
            for kc in range(nk):
                ins = e.matmul(PS[ps_i][:, :], lhsT=lhs_src[:, kc, ti * 128:(ti + 1) * 128], rhs=wv[:, kc, :],
                               start=(kc == 0), stop=(kc == nk - 1))
            return ins
        return fn


    rt = [sb("rt%d" % i, [128, 64], F32) for i in range(4)]
    b_rt = S.buf("rt")
    smV = [sb("smV%d" % i, [128, 2, 257], BF16) for i in range(2)]
    b_smV = S.bufs_n("smV", 2)
    mskt = sb("mskt", [128, 128], F32)
    c_msk = din("c_msk", [128, 128])
    dma("sp", [(mskt[:], c_msk[:, :])], w=[b_const], key="const")
    for i in range(2):
        dve(lambda e, i=i: e.memset(cV[i][:], 1.0), w=[b_cV[i]])
        dve(lambda e, i=i: e.memset(smV[i][:], 1.0), w=[b_smV[i]])
    ctr = {"s": 0, "o": 0, "t": 0, "ep": 0, "smv": 0}

    def rope(fi, tile):
        ci = nxt("cs", 2)
        r0 = pos_row(tile)
        dma("sp", [(cst[ci][:], c_cos[r0:r0 + 128, :]), (snt[ci][:], c_sin[r0:r0 + 128, :])], w=[b_cs[ci]], key="cs%d" % ci)
        v = F5[fi][:].rearrange("p (s d) -> p s d", s=8)
        x1 = v[:, :, 0:8]
        x2 = v[:, :, 8:16]
        C = cst[ci][:].rearrange("p (s d) -> p s d", s=8)
        Sn = snt[ci][:].rearrange("p (s d) -> p s d", s=8)
        T = [rt[i][:].rearrange("p (s d) -> p s d", s=8) for i in range(4)]
        dve(lambda e: e.tensor_tensor(out=T[0], in0=x1, in1=C, op=ALU.mult), r=[b_F5[fi], b_cs[ci]], w=[b_rt])
        dve(lambda e: e.tensor_tensor(out=T[1], in0=x2, in1=Sn, op=ALU.mult), r=[b_F5[fi], b_cs[ci]], w=[b_rt])
        dve(lambda e: e.tensor_tensor(out=T[2], in0=x2, in1=C, op=ALU.mult), r=[b_F5[fi], b_cs[ci]], w=[b_rt])
        dve(lambda e: e.tensor_tensor(out=T[3], in0=x1, in1=Sn, op=ALU.mult), r=[b_F5[fi], b_cs[ci]], w=[b_rt])
        dve(lambda e: e.tensor_tensor(out=x1, in0=T[0], in1=T[1], op=ALU.subtract), r=[b_rt], w=[b_F5[fi]])
        dve(lambda e: e.tensor_tensor(out=x2, in0=T[2], in1=T[3], op=ALU.add), r=[b_rt], w=[b_F5[fi]])

    def s_bank():
        i = 4 + (ctr["s"] % 2)
        ctr["s"] += 1
        return i

    def t_bank():
        i = 2 + (ctr["t"] % 2)
        ctr["t"] += 1
        return i

    def cache_group(kd, vd, nj):
        fi = nxt("F5", NF512)
        dma("sp", [(F5[fi][:, 0:nj * 128].rearrange("p (j d) -> p j d", j=nj), kd)], w=[b_F5[fi]], key="F5_%d" % fi)
        ci = nxt("cV", 2)
        dma("pool", [(cV[ci][:, 0:nj, 0:128], vd)], w=[b_cV[ci]], key="cV%d" % ci)
        pt = t_bank()

        def tr(e, fi=fi, pt=pt, nj=nj):
            ins = None
            for k in range(nj):
                ins = e.transpose(out=PS[pt][:, k * 128:(k + 1) * 128], in_=F5[fi][:, k * 128:(k + 1) * 128], identity=idf[:])
            return ins
        pe(tr, r=[b_F5[fi], b_const], w=[b_PS[pt]])
        bi = nxt("B5", 4)
        act(lambda e, bi=bi, pt=pt, nj=nj: e.activation(out=B5[bi][:, 0:nj * 128], in_=PS[pt][:, 0:nj * 128], func=AF.Copy), r=[b_PS[pt]], w=[b_B5[bi]])
        return B5[bi][:, 0:nj * 128].rearrange("p (j d) -> p j d", j=nj), b_B5[bi], cV[ci], b_cV[ci]

    def qrange(b, rev):
        return slice(96 - 32 * b, 128 - 32 * b) if rev else slice(32 * b, 32 * b + 32)

    def finish_to_oT(ob_ap, b_ob, chunk, ti, rev):
        pt = t_bank()
        if rev:
            pe(lambda e: e.matmul(PS[pt][:, 0:128], lhsT=ob_ap, rhs=jb[:], start=True, stop=True), r=[b_ob, b_const], w=[b_PS[pt]])
            act(lambda e: e.activation(out=oT[:, chunk, ti * 128:(ti + 1) * 128], in_=PS[pt][:, 0:128], func=AF.Copy), r=[b_PS[pt]], w=[b_oT])
        else:
            pvb = PS[pt][:].bitcast(BF16)
            pe(lambda e: e.transpose(out=pvb[:, 0:128], in_=ob_ap, identity=idb[:]), r=[b_ob, b_const], w=[b_PS[pt]])
            act(lambda e: e.activation(out=oT[:, chunk, ti * 128:(ti + 1) * 128], in_=pvb[:, 0:128], func=AF.Copy), r=[b_PS[pt]], w=[b_oT])


    def run_stages(stages, after_first=None):
        n = len(stages)
        if n == 0:
            if after_first:
                after_first()
            return

        def do_st(s):
            if s.get("pre"):
                s["pre"]()
            s["st"]()
        do_st(stages[0])
        if after_first:
            after_first()
        for j in range(n):
            if j + 1 < n:
                do_st(stages[j + 1])
            stages[j]["mid"]()
            stages[j]["pv"]()

    def split_cache_loads(kd, vd, nj):
        fi = nxt("F5", NF512)
        dma("sp", [(F5[fi][:, 0:nj * 128].rearrange("p (j d) -> p j d", j=nj), kd)], w=[b_F5[fi]], key="F5_%d" % fi)
        ci = nxt("cV", 2)
        dma("pool", [(cV[ci][:, 0:nj, 0:128], vd)], w=[b_cV[ci]], key="cV%d" % ci)
        return {"fi": fi, "ci": ci, "nj": nj}

    def cache_xpose(ld):
        fi, nj = ld["fi"], ld["nj"]
        pt = t_bank()

        def tr(e, fi=fi, pt=pt, nj=nj):
            ins = None
            for k in range(nj):
                ins = e.transpose(out=PS[pt][:, k * 128:(k + 1) * 128], in_=F5[fi][:, k * 128:(k + 1) * 128], identity=idf[:])
            return ins
        pe(tr, r=[b_F5[fi], b_const], w=[b_PS[pt]])
        bi = nxt("B5", 4)
        act(lambda e, bi=bi, pt=pt, nj=nj: e.activation(out=B5[bi][:, 0:nj * 128], in_=PS[pt][:, 0:nj * 128], func=AF.Copy), r=[b_PS[pt]], w=[b_B5[bi]])
        ld["kTc"] = B5[bi][:, 0:nj * 128].rearrange("p (j d) -> p j d", j=nj)
        ld["bk"] = b_B5[bi]
        ld["vc"] = cV[ld["ci"]]
        ld["bv"] = b_cV[ld["ci"]]

    def attn_a(l, hg, ti, tile, qi):
        pend = [None]
        for hh in range(4):
            h = hg * 4 + hh
            po = 6 + (ctr["o"] % 2)
            ctr["o"] += 1
            q = qTt[qi]
            stages = []
            if tile[0] == "p":
                t = tile[1]
                js = list(range(max(0, t - 4), t + 1))
                for idx, j in enumerate(js):
                    dd = t - j
                    s = {}

                    def f_st(s=s, j=j, hh=hh):
                        ps = s_bank()
                        s["ps"] = ps
                        pe(lambda e, ps=ps: e.matmul(PS[ps][:, 0:128], lhsT=wkT[:, hh, j * 128:(j + 1) * 128], rhs=q[:, hh, :], start=True, stop=True),
                           r=[b_wkT, b_qT[qi]], w=[b_PS[ps]])

                    def f_mid(s=s, dd=dd, hh=hh, h=h):
                        ps = s["ps"]
                        ei = nxt("ET", 3)
                        s["ei"] = ei
                        if dd >= 2:
                            act(A_exp(ET[ei][:, 0:128], PS[ps][:, 0:128], bias=c0t[:, h:h + 1]), r=[b_PS[ps], b_lay], w=[b_ET[ei]])
                            if dd == 4:
                                dve(lambda e, ei=ei: e.memset(ET[ei][0:64, 0:64], 0.0), w=[b_ET[ei]])
                        else:
                            sp_ = nxt("SPf", 2)
                            dve(lambda e, sp_=sp_, ps=ps: e.tensor_tensor(out=SP_[sp_][:], in0=PS[ps][:, 0:128], in1=hank[:, hh, dd, :], op=ALU.add),
                                r=[b_PS[ps], b_hank], w=[b_SPf[sp_]])
                            act(A_exp(ET[ei][:, 0:128], SP_[sp_][:]), r=[b_SPf[sp_]], w=[b_ET[ei]])
                            if dd == 0:
                                dve(lambda e, ei=ei: e.memset(ET[ei][64:128, 64:128], 0.0), w=[b_ET[ei]])

                    def f_pv(s=s, j=j, hh=hh, idx=idx, n=len(js), po=po):
                        ei = s["ei"]
                        pe(lambda e, ei=ei: e.matmul(PS[po][:, 0:129], lhsT=ET[ei][:, 0:128], rhs=wV[:, j, hh, :], start=(idx == 0), stop=(idx == n - 1)),
                           r=[b_ET[ei], b_wV], w=[b_PS[po]])
                    s.update(st=f_st, mid=f_mid, pv=f_pv)
                    stages.append(s)
            else:
                lds = [None] * 4

                def mk_ld(b, h=h):
                    return split_cache_loads(cak[l, b, :, h * 128:(h + 1) * 128].rearrange("(j p) d -> p j d", p=128),
                                             cav[l, b, :, h * 128:(h + 1) * 128].rearrange("(j p) d -> p j d", p=128), 4)
                lds[0] = mk_ld(0)
                cache_xpose(lds[0])
                cnt = [0]
                for b in range(4):
                    qr = qrange(b, True)
                    for jj in range(4):
                        s = {}
                        pre = None
                        if jj == 1 and b + 1 < 4:
                            def pre(b=b):
                                lds[b + 1] = mk_ld(b + 1)
                        elif jj == 3 and b + 1 < 4:
                            def pre(b=b):
                                cache_xpose(lds[b + 1])

                        def f_st(s=s, b=b, jj=jj, qr=qr, hh=hh):
                            ps = s_bank()
                            s["ps"] = ps
                            ld = lds[b]
                            pe(lambda e, ps=ps, kTc=ld["kTc"]: e.matmul(PS[ps][:, 0:32], lhsT=kTc[:, jj, :], rhs=q[:, hh, qr], start=True, stop=True),
                               r=[ld["bk"], b_qT[qi]], w=[b_PS[ps]])

                        def f_mid(s=s, jj=jj, qr=qr, hh=hh, h=h):
                            ps = s["ps"]
                            ep = ctr["ep"] % 4
                            ctr["ep"] += 1
                            s["ep"] = ep
                            dve(lambda e, ep=ep: e.memset(EP[ep][:], 0.0), w=[b_EP[ep]])
                            if jj < 3:
                                act(A_exp(EP[ep][:, qr], PS[ps][:, 0:32], bias=c0t[:, h:h + 1]), r=[b_PS[ps], b_lay], w=[b_EP[ep]])
                            else:
                                sp_ = nxt("SPf", 2)
                                dve(lambda e, sp_=sp_, ps=ps: e.tensor_tensor(out=SP_[sp_][:, 0:32], in0=PS[ps][:, 0:32], in1=hank[:, hh, 1, 96:128], op=ALU.add),
                                    r=[b_PS[ps], b_hank], w=[b_SPf[sp_]])
                                act(A_exp(EP[ep][:, qr], SP_[sp_][:, 0:32]), r=[b_SPf[sp_]], w=[b_EP[ep]])

                        def f_pv(s=s, b=b, jj=jj, first=(b == 0 and jj == 0), po=po):
                            ep = s["ep"]
                            ld = lds[b]
                            pe(lambda e, ep=ep, vc=ld["vc"]: e.matmul(PS[po][:, 0:129], lhsT=EP[ep][:, :], rhs=vc[:, jj, :], start=first, stop=False),
                               r=[b_EP[ep], ld["bv"]], w=[b_PS[po]])
                        s.update(st=f_st, mid=f_mid, pv=f_pv, pre=pre)
                        stages.append(s)
                s = {}

                def f_st(s=s, hh=hh):
                    ps = s_bank()
                    s["ps"] = ps
                    pe(lambda e, ps=ps: e.matmul(PS[ps][:, 0:128], lhsT=skT[:, hh, :], rhs=q[:, hh, :], start=True, stop=True),
                       r=[b_skT, b_qT[qi]], w=[b_PS[ps]])

                def f_mid(s=s, hh=hh):
                    ps = s["ps"]
                    sp_ = nxt("SPf", 2)
                    dve(lambda e, sp_=sp_, ps=ps: e.tensor_tensor(out=SP_[sp_][:], in0=PS[ps][:, 0:128], in1=hank[:, hh, 0, :], op=ALU.add),
                        r=[b_PS[ps], b_hank], w=[b_SPf[sp_]])
                    act(A_exp(SP_[sp_][:], SP_[sp_][:]), r=[b_SPf[sp_]], w=[b_SPf[sp_]])
                    ei = nxt("ET", 3)
                    s["ei"] = ei
                    dve(lambda e, sp_=sp_, ei=ei: e.tensor_tensor(out=ET[ei][:, 0:128], in0=SP_[sp_][:], in1=mskt[:], op=ALU.mult), r=[b_SPf[sp_], b_const], w=[b_ET[ei]])

                def f_pv(s=s, hh=hh, po=po):
                    ei = s["ei"]
                    pe(lambda e, ei=ei: e.matmul(PS[po][:, 0:129], lhsT=ET[ei][:, 0:128], rhs=sVn[:, hh, :], start=False, stop=True),
                       r=[b_ET[ei], b_sVn], w=[b_PS[po]])
                s.update(st=f_st, mid=f_mid, pv=f_pv)
                stages.append(s)
            run_stages(stages, after_first=pend[0])
            si = nxt("st", 4)
            oi = nxt("OB", 2)
            dve(lambda e, si=si, po=po: e.reciprocal(out=st[si][:, 0:1], in_=PS[po][:, 128:129]), r=[b_PS[po]], w=[b_st[si]])
            dve(lambda e, si=si, po=po, oi=oi: e.tensor_scalar(out=OB[oi][:, 0:128], in0=PS[po][:, 0:128], scalar1=st[si][:, 0:1], scalar2=None, op0=ALU.mult),
                r=[b_PS[po], b_st[si]], w=[b_OB[oi]])
            pend[0] = (lambda oi=oi, hh=hh: finish_to_oT(OB[oi][:, 0:128], b_OB[oi], hg * 4 + hh, ti, True))
        pend[0]()

    def attn_b(l, hg, ti, tile, qi, neg_lam, lam_init):
        q = qTt[qi]
        qz = qTz[qi]
        pend = [None]
        for hh in range(4):
            h = hg * 4 + hh
            p0, p1 = 6, 7
            stages = []

            def st_mm(kt_ap, qcols, N, ps, hh=hh):
                def fn(e):
                    e.matmul(PS[ps][:, 0:N], lhsT=kt_ap, rhs=q[:, hh, qcols], start=True, stop=True)
                    return e.matmul(PS[ps][:, 128:128 + N], lhsT=kt_ap, rhs=qz[:, hh, qcols], start=True, stop=True)
                return fn

            def pv_mm(e0, e1, v_ap, first, last):
                def fn(e):
                    e.matmul(PS[p0][:, 0:129], lhsT=e0, rhs=v_ap, start=first, stop=last)
                    return e.matmul(PS[p1][:, 0:129], lhsT=e1, rhs=v_ap, start=first, stop=last)
                return fn
            if tile[0] == "p":
                t = tile[1]
                for j in range(t + 1):
                    s = {}

                    def f_st(s=s, j=j, hh=hh, st_mm=st_mm):
                        ps = s_bank()
                        s["ps"] = ps
                        pe(st_mm(wkT[:, hh, j * 128:(j + 1) * 128], slice(0, 128), 128, ps), r=[b_wkT, b_qT[qi]], w=[b_PS[ps]])

                    def f_mid(s=s, j=j, t=t):
                        ps = s["ps"]
                        ei = nxt("ET", 3)
                        s["ei"] = ei
                        act(A_exp(ET[ei][:, 0:256], PS[ps][:, 0:256]), r=[b_PS[ps]], w=[b_ET[ei]])
                        if j == t:
                            dve(lambda e, ei=ei: e.memset(ET[ei][64:128, :].rearrange("p (m q) -> p m q", m=2)[:, :, 0:64], 0.0), w=[b_ET[ei]])

                    def f_pv(s=s, j=j, t=t, hh=hh, pv_mm=pv_mm):
                        ei = s["ei"]
                        pe(pv_mm(ET[ei][:, 0:128], ET[ei][:, 128:256], wV[:, j, hh, :], j == 0, j == t), r=[b_ET[ei], b_wV], w=[b_PS[p0], b_PS[p1]])
                    s.update(st=f_st, mid=f_mid, pv=f_pv)
                    stages.append(s)
            else:
                groups = [(b, jg) for b in range(4) for jg in range(NCT // 4)]
                lds = [None] * len(groups)

                def mk_ld(g, h=h):
                    b, jg = groups[g]
                    rows = slice(jg * 512, (jg + 1) * 512)
                    return split_cache_loads(cbk[l, b, rows, h * 128:(h + 1) * 128].rearrange("(j p) d -> p j d", p=128),
                                             cbv[l, b, rows, h * 128:(h + 1) * 128].rearrange("(j p) d -> p j d", p=128), 4)
                lds[0] = mk_ld(0)
                cache_xpose(lds[0])
                for g, (b, jg) in enumerate(groups):
                    qr = qrange(b, False)
                    for jj in range(4):
                        s = {}
                        pre = None
                        if jj == 1 and g + 1 < len(groups):
                            def pre(g=g):
                                lds[g + 1] = mk_ld(g + 1)
                        elif jj == 3 and g + 1 < len(groups):
                            def pre(g=g):
                                cache_xpose(lds[g + 1])

                        def f_st(s=s, g=g, jj=jj, qr=qr, st_mm=st_mm):
                            ps = s_bank()
                            s["ps"] = ps
                            ld = lds[g]
                            pe(st_mm(ld["kTc"][:, jj, :], qr, 32, ps), r=[ld["bk"], b_qT[qi]], w=[b_PS[ps]])

                        def f_mid(s=s, qr=qr):
                            ps = s["ps"]
                            ep = ctr["ep"] % 4
                            ctr["ep"] += 1
                            s["ep"] = ep
                            dve(lambda e, ep=ep: e.memset(EPP[ep][:], 0.0), w=[b_EP[ep]])
                            act(A_exp(EPP[ep][:].rearrange("p (m q) -> p m q", m=2)[:, :, qr],
                                      PS[ps][:, 0:256].rearrange("p (m q) -> p m q", m=2)[:, :, 0:32]), r=[b_PS[ps]], w=[b_EP[ep]])

                        def f_pv(s=s, g=g, jj=jj, first=(g == 0 and jj == 0), pv_mm=pv_mm):
                            ep = s["ep"]
                            ld = lds[g]
                            pe(pv_mm(EPP[ep][:, 0:128], EPP[ep][:, 128:256], ld["vc"][:, jj, :], first, False), r=[b_EP[ep], ld["bv"]], w=[b_PS[p0], b_PS[p1]])
                        s.update(st=f_st, mid=f_mid, pv=f_pv, pre=pre)
                        stages.append(s)
                s = {}

                def f_st(s=s, hh=hh, st_mm=st_mm):
                    ps = s_bank()
                    s["ps"] = ps
                    pe(st_mm(skT[:, hh, :], slice(0, 128), 128, ps), r=[b_skT, b_qT[qi]], w=[b_PS[ps]])

                def f_mid(s=s):
                    ps = s["ps"]
                    fi = nxt("F5", NF512)
                    act(A_exp(F5[fi][:, 0:256], PS[ps][:, 0:256]), r=[b_PS[ps]], w=[b_F5[fi]])
                    ei = nxt("ET", 3)
                    s["ei"] = ei
                    dve(lambda e, fi=fi, ei=ei: e.tensor_tensor(out=ET[ei][:, 0:256].rearrange("p (m q) -> p m q", m=2),
                                                                in0=F5[fi][:, 0:256].rearrange("p (m q) -> p m q", m=2),
                                                                in1=mskb3, op=ALU.mult), r=[b_F5[fi], b_const], w=[b_ET[ei]])

                def f_pv(s=s, hh=hh, pv_mm=pv_mm):
                    ei = s["ei"]
                    pe(pv_mm(ET[ei][:, 0:128], ET[ei][:, 128:256], sVn[:, hh, :], False, True), r=[b_ET[ei], b_sVn], w=[b_PS[p0], b_PS[p1]])
                s.update(st=f_st, mid=f_mid, pv=f_pv)
                stages.append(s)
            run_stages(stages, after_first=pend[0])
            si = nxt("st", 4)
            oi = nxt("ON", 2)
            ob = nxt("OB", 2)
            f2 = nxt("F5", NF512)
            dve(lambda e, si=si: e.reciprocal(out=st[si][:, 0:1], in_=PS[p0][:, 128:129]), r=[b_PS[p0]], w=[b_st[si]])
            dve(lambda e, si=si: e.reciprocal(out=st[si][:, 1:2], in_=PS[p1][:, 128:129]), r=[b_PS[p1]], w=[b_st[si]])
            dve(lambda e, si=si: e.tensor_tensor(out=st[si][:, 2:3], in0=st[si][:, 1:2], in1=neg_lam, op=ALU.mult), r=[b_st[si], b_lay], w=[b_st[si]])
            dve(lambda e, si=si, oi=oi: e.tensor_scalar(out=ON[oi][:, 0:128], in0=PS[p0][:, 0:128], scalar1=st[si][:, 0:1], scalar2=None, op0=ALU.mult),
                r=[b_PS[p0], b_st[si]], w=[b_ON[oi]])
            dve(lambda e, si=si, oi=oi: e.scalar_tensor_tensor(out=ON[oi][:, 0:128], in0=PS[p1][:, 0:128], scalar=st[si][:, 2:3], in1=ON[oi][:, 0:128],
                                                               op0=ALU.mult, op1=ALU.add), r=[b_PS[p1], b_st[si], b_ON[oi]], w=[b_ON[oi]])
            dve(lambda e, oi=oi, f2=f2: e.tensor_tensor(out=F5[f2][:, 0:128], in0=ON[oi][:, 0:128], in1=ON[oi][:, 0:128], op=ALU.mult), r=[b_ON[oi]], w=[b_F5[f2]])
            dve(lambda e, si=si, f2=f2: e.tensor_reduce(out=st[si][:, 3:4], in_=F5[f2][:, 0:128], axis=AX.X, op=ALU.add), r=[b_F5[f2]], w=[b_st[si]])
            act(lambda e, si=si: e.activation(out=st[si][:, 4:5], in_=st[si][:, 3:4], func=AF.Ln, bias=epsc[:], scale=1.0 / 128), r=[b_st[si], b_const], w=[b_st[si]])
            act(lambda e, si=si: e.activation(out=st[si][:, 5:6], in_=st[si][:, 4:5], func=AF.Exp, scale=-0.5), r=[b_st[si]], w=[b_st[si]])
            dve(lambda e, si=si, oi=oi, ob=ob: e.scalar_tensor_tensor(out=OB[ob][:, 0:128], in0=ON[oi][:, 0:128], scalar=st[si][:, 5:6], in1=gsub[:],
                                                                      op0=ALU.mult, op1=ALU.mult), r=[b_ON[oi], b_st[si], b_lay], w=[b_OB[ob]])
            pend[0] = (lambda ob=ob, hh=hh: finish_to_oT(OB[ob][:, 0:128], b_OB[ob], hg * 4 + hh, ti, False))
        pend[0]()

    def attn_m(l, hg, ti, tile, qi):
        q = qTt[qi]
        pend = [None]
        for hl in range(2):
            h = hg * 2 + hl
            po = 6 + (ctr["o"] % 2)
            ctr["o"] += 1
            stages = []
            if tile[0] == "p":
                for jj in range(2):
                    s = {}

                    def f_st(s=s, jj=jj, h=h, hl=hl):
                        ps = s_bank()
                        s["ps"] = ps

                        def smm(e, ps=ps):
                            e.matmul(PS[ps][:, 0:128], lhsT=mkT[:, h * 2, jj * 128:(jj + 1) * 128], rhs=q[:, hl * 2, :], start=True, stop=False)
                            return e.matmul(PS[ps][:, 0:128], lhsT=mkT[:, h * 2 + 1, jj * 128:(jj + 1) * 128], rhs=q[:, hl * 2 + 1, :], start=False, stop=True)
                        pe(smm, r=[b_mkT, b_qT[qi]], w=[b_PS[ps]])

                    def f_mid(s=s):
                        ps = s["ps"]
                        ei = nxt("ET", 3)
                        s["ei"] = ei
                        act(A_exp(ET[ei][:, 0:128], PS[ps][:, 0:128]), r=[b_PS[ps]], w=[b_ET[ei]])

                    def f_pv(s=s, jj=jj, h=h, po=po):
                        ei = s["ei"]
                        pe(lambda e, ei=ei: e.matmul(PS[po][:, 0:257], lhsT=ET[ei][:, 0:128], rhs=cVm[:, jj, h, :], start=(jj == 0), stop=(jj == 1)),
                           r=[b_ET[ei], b_cVm], w=[b_PS[po]])
                    s.update(st=f_st, mid=f_mid, pv=f_pv)
                    stages.append(s)
            else:
                lds = [None] * 4

                def mk_ld(b, h=h):
                    fi = nxt("F5", NF512)
                    dma("sp", [(F5[fi][:].rearrange("p (j d) -> p j d", j=2), cmk[l, b, :, h * 256:(h + 1) * 256].rearrange("(j p) d -> p j d", p=128))],
                        w=[b_F5[fi]], key="F5_%d" % fi)
                    mi = ctr["smv"] % 2
                    ctr["smv"] += 1
                    dma("pool", [(smV[mi][:, :, 0:256], cmv[l, b, :, h * 256:(h + 1) * 256].rearrange("(j p) d -> p j d", p=128))], w=[b_smV[mi]], key="smV%d" % mi)
                    return {"fi": fi, "mi": mi}

                def xp_ld(ld):
                    fi = ld["fi"]
                    pt = t_bank()

                    def tr(e, fi=fi, pt=pt):
                        ins = None
                        for k in range(4):
                            ins = e.transpose(out=PS[pt][:, k * 128:(k + 1) * 128], in_=F5[fi][:, k * 128:(k + 1) * 128], identity=idf[:])
                        return ins
                    pe(tr, r=[b_F5[fi], b_const], w=[b_PS[pt]])
                    bi = nxt("B5", 4)
                    act(lambda e, bi=bi, pt=pt: e.activation(out=B5[bi][:], in_=PS[pt][:, :], func=AF.Copy), r=[b_PS[pt]], w=[b_B5[bi]])
                    ld["kv"] = B5[bi][:].rearrange("p (j c t) -> p j c t", j=2, c=2)
                    ld["bk"] = b_B5[bi]
                lds[0] = mk_ld(0)
                xp_ld(lds[0])
                for b in range(4):
                    qr = qrange(b, False)
                    for jj in range(2):
                        s = {}
                        pre = None
                        if jj == 1 and b + 1 < 4:
                            def pre(b=b):
                                lds[b + 1] = mk_ld(b + 1)
                        elif jj == 0 and b > 0:
                            def pre(b=b):
                                xp_ld(lds[b])

                        def f_st(s=s, b=b, jj=jj, qr=qr, hl=hl):
                            ps = s_bank()
                            s["ps"] = ps
                            ld = lds[b]

                            def smm(e, ps=ps, kv=ld["kv"]):
                                e.matmul(PS[ps][:, 0:32], lhsT=kv[:, jj, 0, :], rhs=q[:, hl * 2, qr], start=True, stop=False)
                                return e.matmul(PS[ps][:, 0:32], lhsT=kv[:, jj, 1, :], rhs=q[:, hl * 2 + 1, qr], start=False, stop=True)
                            pe(smm, r=[ld["bk"], b_qT[qi]], w=[b_PS[ps]])

                        def f_mid(s=s, qr=qr):
                            ps = s["ps"]
                            ep = ctr["ep"] % 4
                            ctr["ep"] += 1
                            s["ep"] = ep
                            dve(lambda e, ep=ep: e.memset(EP[ep][:], 0.0), w=[b_EP[ep]])
                            act(A_exp(EP[ep][:, qr], PS[ps][:, 0:32]), r=[b_PS[ps]], w=[b_EP[ep]])

                        def f_pv(s=s, b=b, jj=jj, first=(b == 0 and jj == 0), last=(b == 3 and jj == 1), po=po):
                            ep = s["ep"]
                            mi = lds[b]["mi"]
                            pe(lambda e, ep=ep, mi=mi: e.matmul(PS[po][:, 0:257], lhsT=EP[ep][:, :], rhs=smV[mi][:, jj, :], start=first, stop=last),
                               r=[b_EP[ep], b_smV[mi]], w=[b_PS[po]])
                        s.update(st=f_st, mid=f_mid, pv=f_pv, pre=pre)
                        stages.append(s)
            run_stages(stages, after_first=pend[0])
            si = nxt("st", 4)
            oi = nxt("OB", 2)
            dve(lambda e, si=si, po=po: e.reciprocal(out=st[si][:, 0:1], in_=PS[po][:, 256:257]), r=[b_PS[po]], w=[b_st[si]])
            dve(lambda e, si=si, po=po, oi=oi: e.tensor_scalar(out=OB[oi][:, 0:256], in0=PS[po][:, 0:256], scalar1=st[si][:, 0:1], scalar2=None, op0=ALU.mult),
                r=[b_PS[po], b_st[si]], w=[b_OB[oi]])
            def fin(oi=oi, hl=hl):
                for dc in range(2):
                    finish_to_oT(OB[oi][:, dc * 128:(dc + 1) * 128], b_OB[oi], hg * 4 + hl * 2 + dc, ti, False)
            pend[0] = fin
        pend[0]()

    mskb = sb("mskb", [128, 128], F32)
    c_mskb = din("c_mskb", [128, 128])
    dma("sp", [(mskb[:], c_mskb[:, :])], w=[b_const], key="const")
    mskb3 = mskb[:].unsqueeze(1).to_broadcast([128, 2, 128])


    def skewed_dense(slot, wv, lhs_src, rbufs):
        pis = {}

        def issue(ti):
            pi = zps[0] % 2
            zps[0] += 1
            pis[ti] = pi
            pe(dense_tm(slot, wv, lhs_src, ti, pi), r=rbufs, w=[b_PS[pi]])
        return pis, issue

    inv_sqrt = {"a": 128 ** -0.5, "b": 64 ** -0.5, "m": 256 ** -0.5}

    S.stop_at = getattr(cfg, 'dbg_stop_at', None)
    S.in_loop = True
    try:
        for l in (cfg.dbg_layers if cfg.dbg_layers is not None else range(L)):
            lam_init = 0.8 - 0.6 * math.exp(-0.3 * l)
            prm = [
                (gka[:], a_k_g[l, :].partition_broadcast(128)),
                (gqb[:], b_q_g[l, :].partition_broadcast(128)),
                (gkb[:], b_k_g[l, :].partition_broadcast(128)),
                (gkm[:], m_k_g[l, :].partition_broadcast(128)),
                (gsub[:], subln_g[l, :].partition_broadcast(128)),
                (lamv[:, 0, :], lq1[l, :].partition_broadcast(128)),
                (lamv[:, 1, :], lk1[l, :].partition_broadcast(128)),
                (lamv[:, 2, :], lq2[l, :].partition_broadcast(128)),
                (lamv[:, 3, :], lk2[l, :].partition_broadcast(128)),
            ]
            dma("sp", prm, w=[b_lay], key="lay")
            dma("sp", [(gqa[:], a_q_g[l, :].rearrange("(p o) -> p o", o=1)),
                       (gqm[:, 0:1], m_q_g[l, 0:128].rearrange("(p o) -> p o", o=1)),
                       (gqm[:, 1:2], m_q_g[l, 128:256].rearrange("(p o) -> p o", o=1))]
                + [(c0t[:, h:h + 1], a_rel[l, h, 0:1].partition_broadcast(128)) for h in range(8)], w=[b_lay], key="lay2")
            fa = nxt("F5", NF512)
            fb = nxt("F5", NF512)
            dma("sp", [(F5[fa][0:48, 0:128], gate_b[l, :, :].rearrange("n (c p) -> (n c) p", p=128)),
                       (F5[fa][0:8, 128 + 127:128 + 384], a_rel[l, :, :])], w=[b_F5[fa]], key="F5_%d" % fa)
            dma("sp", [(F5[fb][0:NF, j * 128:(j + 1) * 128], conv_w[l, j, :].rearrange("(f p) -> f p", p=128)) for j in range(3)]
                + [(F5[fb][0:NF, 384:512], conv_b[l, :].rearrange("(f p) -> f p", p=128))], w=[b_F5[fb]], key="F5_%d" % fb)
            dve(lambda e, fa=fa: e.tensor_scalar(out=F5[fa][0:8, 128:128 + 127], in0=F5[fa][0:8, 128 + 128:128 + 255], scalar1=0.0,
                                                 scalar2=F5[fa][0:8, 128 + 127:128 + 128], op0=ALU.mult, op1=ALU.add), r=[b_F5[fa]], w=[b_F5[fa]])
            dma("sp", [(s_ext[:, :], F5[fa][0:8, 128:512])], r=[b_F5[fa]], w=[b_dext], key="ext")
            pe(lambda e, fa=fa: e.transpose(out=PS[2][:, 0:48], in_=F5[fa][0:48, 0:128], identity=idf[0:48, 0:48]), r=[b_F5[fa], b_const], w=[b_PS[2]])
            act(lambda e: e.activation(out=gtb[:].rearrange("p n c -> p (n c)"), in_=PS[2][:, 0:48], func=AF.Copy), r=[b_PS[2]], w=[b_lay])

            def trc(e, fb=fb):
                ins = None
                for j in range(4):
                    ins = e.transpose(out=PS[3][:, j * 64:j * 64 + NF], in_=F5[fb][0:NF, j * 128:(j + 1) * 128], identity=idf[0:NF, 0:NF])
                return ins
            pe(trc, r=[b_F5[fb], b_const], w=[b_PS[3]])
            act(lambda e: e.activation(out=cw[:], in_=PS[3][:, 0:192].rearrange("p (j f) -> p j f", j=3)[:, :, 0:NF], func=AF.Copy), r=[b_PS[3]], w=[b_lay])
            act(lambda e: e.activation(out=cb[:], in_=PS[3][:, 192:192 + NF], func=AF.Copy), r=[b_PS[3]], w=[b_lay])
            dve(lambda e: e.tensor_tensor(out=lamv[:, 0, :], in0=lamv[:, 0, :], in1=lamv[:, 1, :], op=ALU.mult), r=[b_lay], w=[b_lay])
            dve(lambda e: e.tensor_tensor(out=lamv[:, 2, :], in0=lamv[:, 2, :], in1=lamv[:, 3, :], op=ALU.mult), r=[b_lay], w=[b_lay])
            dve(lambda e: e.tensor_reduce(out=lams[:, 0:1], in_=lamv[:, 0, :], axis=AX.X, op=ALU.add), r=[b_lay], w=[b_lay])
            dve(lambda e: e.tensor_reduce(out=lams[:, 1:2], in_=lamv[:, 2, :], axis=AX.X, op=ALU.add), r=[b_lay], w=[b_lay])
            act(lambda e: e.activation(out=lams[:, 2:4], in_=lams[:, 0:2], func=AF.Exp), r=[b_lay], w=[b_lay])
            dve(lambda e, li=lam_init: e.scalar_tensor_tensor(out=lams[:, 4:5], in0=lams[:, 3:4], scalar=-li, in1=lams[:, 2:3],
                                                 op0=ALU.add, op1=ALU.subtract), r=[b_lay], w=[b_lay])
            neg_lam = lams[:, 4:5]
            dve(lambda e, li=lam_init: e.tensor_scalar(out=gsub[:], in0=gsub[:], scalar1=1.0 - li, scalar2=None, op0=ALU.mult), r=[b_lay], w=[b_lay])

            for gi, tiles in enumerate(cfg.groups):
                NT = len(tiles)
                has_s = any(t[0] == "s" for t in tiles)
                ptiles = [t for t in tiles if t[0] == "p"]
                pt0 = ptiles[0][1]
                npl = len(ptiles)
                tbs = tok_blocks(NT)

                norm_stage(tiles, lambda t, l=l: x_src(l, t), norm_mix_g[l, :], hT, b_hT)
                S.barrier()

                if gi == 0:
                    hmT = A1[:, 16384:20480].rearrange("p (k t) -> p k t", k=16)
                    norm_stage([("m", 0), ("m", 1)], lambda t: mem[t[1] * 128:(t[1] + 1) * 128, :], mem_g[l, :], hmT, b_mT, psbase=4)
                    for cbk_i in range(4):
                        slot = next_slot()
                        wv = load_w_block(slot, w_mem[l, :, cbk_i * 512:(cbk_i + 1) * 512], 512)
                        for mt in range(2):
                            pi = zps[0] % 2
                            zps[0] += 1
                            pe(dense_tm(slot, wv, hmT, mt, pi), r=[b_mT, b_W[slot]], w=[b_PS[pi]])
                            fi = nxt("F5", NF512)
                            act(lambda e, pi=pi, fi=fi: e.activation(out=F5[fi][:], in_=PS[pi][:, :], func=AF.Copy),
                                r=[b_PS[pi]], w=[b_F5[fi]])
                            rows = slice(mt * 128, (mt + 1) * 128)
                            if cbk_i < 2:
                                si = nxt("st", 4)
                                f2 = nxt("F5", NF512)
                                dve(lambda e, fi=fi, f2=f2: e.tensor_tensor(out=F5[f2][:], in0=F5[fi][:], in1=F5[fi][:], op=ALU.mult),
                                    r=[b_F5[fi]], w=[b_F5[f2]])
                                dve(lambda e, f2=f2, si=si: e.tensor_reduce(out=st[si][:, 0:2], in_=F5[f2][:].rearrange("p (h d) -> p h d", h=2),
                                                                            axis=AX.X, op=ALU.add), r=[b_F5[f2]], w=[b_st[si]])
                                act(lambda e, si=si: e.activation(out=st[si][:, 2:4], in_=st[si][:, 0:2], func=AF.Ln, bias=epsc[:], scale=1.0 / 256),
                                    r=[b_st[si], b_const], w=[b_st[si]])
                                act(lambda e, si=si: e.activation(out=st[si][:, 4:6], in_=st[si][:, 2:4], func=AF.Exp, scale=-0.5),
                                    r=[b_st[si]], w=[b_st[si]])
                                for hh in range(2):
                                    dve(lambda e, fi=fi, si=si, hh=hh: e.scalar_tensor_tensor(
                                        out=F5[fi][:, hh * 256:(hh + 1) * 256], in0=F5[fi][:, hh * 256:(hh + 1) * 256],
                                        scalar=st[si][:, 4 + hh:5 + hh], in1=gkm[:], op0=ALU.mult, op1=ALU.mult),
                                        r=[b_F5[fi], b_st[si], b_lay], w=[b_F5[fi]])
                                dma("sp", [(mkp[l, rows, cbk_i * 512:(cbk_i + 1) * 512], F5[fi][:])], r=[b_F5[fi]], key="F5_%d" % fi)
                                pi2 = 2 + (zps[0] % 2)

                                def trm(e, fi=fi, pi2=pi2):
                                    ins = None
                                    for k in range(4):
                                        ins = e.transpose(out=PS[pi2][:, k * 128:(k + 1) * 128], in_=F5[fi][:, k * 128:(k + 1) * 128], identity=idf[:])
                                    return ins
                                pe(trm, r=[b_F5[fi], b_const], w=[b_PS[pi2]])
                                act(lambda e, pi2=pi2, cbk_i=cbk_i, mt=mt: e.activation(
                                    out=mkT[:, cbk_i * 4:(cbk_i + 1) * 4, mt * 128:(mt + 1) * 128],
                                    in_=PS[pi2][:, :].rearrange("p (k t) -> p k t", k=4), func=AF.Copy),
                                    r=[b_PS[pi2]], w=[b_mkT])
                            else:
                                hb0 = (cbk_i - 2) * 2
                                dma("sp", [(mvp[l, rows, (cbk_i - 2) * 512:(cbk_i - 1) * 512], F5[fi][:])], r=[b_F5[fi]], key="F5_%d" % fi)
                                dve(lambda e, fi=fi, mt=mt, hb0=hb0: e.tensor_copy(
                                    out=cVm[:, mt, hb0:hb0 + 2, 0:256], in_=F5[fi][:].rearrange("p (h d) -> p h d", h=2)),
                                    r=[b_F5[fi]], w=[b_cVm])
                    dve(lambda e: e.memset(cVm[:, :, :, 256:257], 1.0), w=[b_cVm])
                    dma("sp", [(s_mkT[:, :, :], mkT[:]), (s_mv[:, :, :, :], cVm[:])], r=[b_mkT, b_cVm], w=[b_dram], key="carry_m")
                    S.barrier()
                else:
                    dma("sp", [(mkT[:], s_mkT[:, :, :]), (cVm[:], s_mv[:, :, :, :])], r=[b_dram], w=[b_mkT, b_cVm], key="carry_m")

                dve(lambda e: e.memset(wVflat, 1.0), w=[b_wV])
                for br in cfg.dbg_br:
                    bname = "abm"[br]
                    qoff = {"a": 0, "b": 3072, "m": 6144}[bname]
                    for hg in range(2):
                        if bname == "a":
                            dma("sp", [(hank[:, hh, x, :], bass.AP(s_ext.tensor, (hg * 4 + hh) * 384 + (128 if x == 0 else 0), [[1, 128], [1, 128]]))
                                       for hh in range(4) for x in range(2)], r=[b_dext], w=[b_hank], key="hank")
                        if bname in ("a", "b"):
                            koff = qoff + 1024 + hg * 512
                            voff = qoff + 2048 + hg * 512
                            s_kT = s_kTa if bname == "a" else s_kTb
                            s_v = s_va if bname == "a" else s_vb
                            kout_p = akp if bname == "a" else bkp
                            vout_p = avp if bname == "a" else bvp
                            kout_s = aks if bname == "a" else bks
                            vout_s = avs if bname == "a" else bvs
                            if bname == "a":
                                kt_lo = max(0, pt0 - 4)
                            else:
                                kt_lo = 0
                            if pt0 > kt_lo:
                                dma("sp", [(wkT[:, :, kt_lo * 128:pt0 * 128], s_kT[:, hg * 4:(hg + 1) * 4, kt_lo * 128:pt0 * 128]),
                                           (wV[:, kt_lo:pt0, :, :], s_v[:, kt_lo:pt0, hg * 4:(hg + 1) * 4, :])],
                                    r=[b_dram], w=[b_wkT, b_wV], key="carry_ld")
                            slot = next_slot()
                            wv = load_w_block(slot, w_in[l, :, koff:koff + 512], 512)
                            pis, issue = skewed_dense(slot, wv, hT, [b_hT, b_W[slot]])
                            issue(0)
                            for ti, tile in enumerate(tiles):
                                if ti + 1 < NT:
                                    issue(ti + 1)
                                pi = pis[ti]
                                fi = nxt("F5", NF512)
                                f2 = nxt("F5", NF512)
                                si = nxt("st", 4)
                                act(lambda e, pi=pi, fi=fi: e.activation(out=F5[fi][:], in_=PS[pi][:, :], func=AF.Copy), r=[b_PS[pi]], w=[b_F5[fi]])
                                nsub = 4 if bname == "a" else 8
                                dsub = 512 // nsub
                                dve(lambda e, fi=fi, f2=f2: e.tensor_tensor(out=F5[f2][:], in0=F5[fi][:], in1=F5[fi][:], op=ALU.mult),
                                    r=[b_F5[fi]], w=[b_F5[f2]])
                                dve(lambda e, f2=f2, si=si, nsub=nsub: e.tensor_reduce(out=st[si][:, 0:nsub], in_=F5[f2][:].rearrange("p (h d) -> p h d", h=nsub),
                                                                                       axis=AX.X, op=ALU.add), r=[b_F5[f2]], w=[b_st[si]])
                                act(lambda e, si=si, nsub=nsub, dsub=dsub: e.activation(out=st[si][:, 8:8 + nsub], in_=st[si][:, 0:nsub], func=AF.Ln,
                                                                                        bias=epsc[:], scale=1.0 / dsub), r=[b_st[si], b_const], w=[b_st[si]])
                                act(lambda e, si=si, nsub=nsub: e.activation(out=st[si][:, 0:nsub], in_=st[si][:, 8:8 + nsub], func=AF.Exp, scale=-0.5),
                                    r=[b_st[si]], w=[b_st[si]])
                                gt_ = gka if bname == "a" else gkb
                                for hh in range(nsub):
                                    dve(lambda e, fi=fi, si=si, hh=hh, dsub=dsub, gt_=gt_: e.scalar_tensor_tensor(
                                        out=F5[fi][:, hh * dsub:(hh + 1) * dsub], in0=F5[fi][:, hh * dsub:(hh + 1) * dsub],
                                        scalar=st[si][:, hh:hh + 1], in1=gt_[:], op0=ALU.mult, op1=ALU.mult),
                                        r=[b_F5[fi], b_st[si], b_lay], w=[b_F5[fi]])
                                if bname == "b":
                                    rope(fi, tile)
                                if tile[0] == "s":
                                    dma("sp", [(kout_s[l, :, hg * 512:(hg + 1) * 512], F5[fi][:])], r=[b_F5[fi]], key="F5_%d" % fi)
                                else:
                                    prow = tile[1] * 128
                                    if bname == "b":
                                        dma("sp", [(kout_p[l, prow:prow + 128, hg * 512:(hg + 1) * 512], F5[fi][:])], r=[b_F5[fi]], key="F5_%d" % fi)
                                    elif prow >= SEQ - AK:
                                        r0 = prow - (SEQ - AK)
                                        dma("sp", [(kout_p[l, r0:r0 + 128, hg * 512:(hg + 1) * 512], F5[fi][:])], r=[b_F5[fi]], key="F5_%d" % fi)
                                pi2 = 2 + (zps[0] % 2)

                                def trk(e, fi=fi, pi2=pi2):
                                    ins = None
                                    for k in range(4):
                                        ins = e.transpose(out=PS[pi2][:, k * 128:(k + 1) * 128], in_=F5[fi][:, k * 128:(k + 1) * 128], identity=idf[:])
                                    return ins
                                pe(trk, r=[b_F5[fi], b_const], w=[b_PS[pi2]])
                                if tile[0] == "s":
                                    act(lambda e, pi2=pi2: e.activation(out=skT[:], in_=PS[pi2][:, :].rearrange("p (k t) -> p k t", k=4), func=AF.Copy),
                                        r=[b_PS[pi2]], w=[b_skT])
                                else:
                                    act(lambda e, pi2=pi2, kt=tile[1]: e.activation(out=wkT[:, :, kt * 128:(kt + 1) * 128],
                                                                                    in_=PS[pi2][:, :].rearrange("p (k t) -> p k t", k=4), func=AF.Copy),
                                        r=[b_PS[pi2]], w=[b_wkT])
                            slot = next_slot()
                            wv = load_w_block(slot, w_in[l, :, voff:voff + 512], 512)
                            pis, issue = skewed_dense(slot, wv, hT, [b_hT, b_W[slot]])
                            issue(0)
                            for ti, tile in enumerate(tiles):
                                if ti + 1 < NT:
                                    issue(ti + 1)
                                pi = pis[ti]
                                fi = nxt("F5", NF512)
                                act(lambda e, pi=pi, fi=fi: e.activation(out=F5[fi][:], in_=PS[pi][:, :], func=AF.Copy), r=[b_PS[pi]], w=[b_F5[fi]])
                                if tile[0] == "s":
                                    dma("sp", [(vout_s[l, :, hg * 512:(hg + 1) * 512], F5[fi][:])], r=[b_F5[fi]], key="F5_%d" % fi)
                                    dve(lambda e, fi=fi: e.tensor_copy(out=sVn[:, :, 0:128], in_=F5[fi][:].rearrange("p (h d) -> p h d", h=4)),
                                        r=[b_F5[fi]], w=[b_sVn])
                                    dve(lambda e: e.memset(sVn[:, :, 128:129], 1.0), w=[b_sVn])
                                else:
                                    prow = tile[1] * 128
                                    if bname == "b":
                                        dma("sp", [(vout_p[l, prow:prow + 128, hg * 512:(hg + 1) * 512], F5[fi][:])], r=[b_F5[fi]], key="F5_%d" % fi)
                                    elif prow >= SEQ - AK:
                                        r0 = prow - (SEQ - AK)
                                        dma("sp", [(vout_p[l, r0:r0 + 128, hg * 512:(hg + 1) * 512], F5[fi][:])], r=[b_F5[fi]], key="F5_%d" % fi)
                                    dve(lambda e, fi=fi, kt=tile[1]: e.tensor_copy(out=wV[:, kt, :, 0:128], in_=F5[fi][:].rearrange("p (h d) -> p h d", h=4)),
                                        r=[b_F5[fi]], w=[b_wV])
                            if gi + 1 < len(cfg.groups) and npl > 0:
                                dma("sp", [(s_kT[:, hg * 4:(hg + 1) * 4, pt0 * 128:(pt0 + npl) * 128], wkT[:, :, pt0 * 128:(pt0 + npl) * 128]),
                                           (s_v[:, pt0:pt0 + npl, hg * 4:(hg + 1) * 4, :], wV[:, pt0:pt0 + npl, :, :])],
                                    r=[b_wkT, b_wV], w=[b_dram], key="carry_st")
                        slot = next_slot()
                        wv = load_w_block(slot, w_in[l, :, qoff + hg * 512:qoff + (hg + 1) * 512], 512)
                        pis, issue = skewed_dense(slot, wv, hT, [b_hT, b_W[slot]])
                        issue(0)
                        for ti, tile in enumerate(tiles):
                            if ti + 1 < NT:
                                issue(ti + 1)
                            pi = pis[ti]
                            fi = nxt("F5", NF512)
                            f2 = nxt("F5", NF512)
                            si = nxt("st", 4)
                            act(lambda e, pi=pi, fi=fi: e.activation(out=F5[fi][:], in_=PS[pi][:, :], func=AF.Copy), r=[b_PS[pi]], w=[b_F5[fi]])
                            nsub = {"a": 4, "b": 8, "m": 2}[bname]
                            dsub = 512 // nsub
                            sc = inv_sqrt[bname]
                            dve(lambda e, fi=fi, f2=f2: e.tensor_tensor(out=F5[f2][:], in0=F5[fi][:], in1=F5[fi][:], op=ALU.mult),
                                r=[b_F5[fi]], w=[b_F5[f2]])
                            dve(lambda e, f2=f2, si=si, nsub=nsub: e.tensor_reduce(out=st[si][:, 0:nsub], in_=F5[f2][:].rearrange("p (h d) -> p h d", h=nsub),
                                                                                   axis=AX.X, op=ALU.add), r=[b_F5[f2]], w=[b_st[si]])
                            act(lambda e, si=si, nsub=nsub, dsub=dsub: e.activation(out=st[si][:, 8:8 + nsub], in_=st[si][:, 0:nsub], func=AF.Ln,
                                                                                    bias=epsc[:], scale=1.0 / dsub), r=[b_st[si], b_const], w=[b_st[si]])
                            act(lambda e, si=si, nsub=nsub: e.activation(out=st[si][:, 0:nsub], in_=st[si][:, 8:8 + nsub], func=AF.Exp, scale=-0.5),
                                r=[b_st[si]], w=[b_st[si]])
                            bi = nxt("B5", 4)
                            qi = nxt("qT", 2)
                            pi2 = 2 + (zps[0] % 2)
                            if bname == "b":
                                for hh in range(nsub):
                                    dve(lambda e, fi=fi, si=si, hh=hh, dsub=dsub: e.scalar_tensor_tensor(
                                        out=F5[fi][:, hh * dsub:(hh + 1) * dsub], in0=F5[fi][:, hh * dsub:(hh + 1) * dsub],
                                        scalar=st[si][:, hh:hh + 1], in1=gqb[:], op0=ALU.mult, op1=ALU.mult),
                                        r=[b_F5[fi], b_st[si], b_lay], w=[b_F5[fi]])
                                rope(fi, tile)
                                dve(lambda e, fi=fi, bi=bi, sc=sc: e.tensor_scalar(out=B5[bi][:], in0=F5[fi][:], scalar1=sc, scalar2=None, op0=ALU.mult),
                                    r=[b_F5[fi]], w=[b_B5[bi]])
                            else:
                                for hh in range(nsub):
                                    dve(lambda e, fi=fi, bi=bi, si=si, hh=hh, dsub=dsub, sc=sc: e.tensor_scalar(
                                        out=B5[bi][:, hh * dsub:(hh + 1) * dsub], in0=F5[fi][:, hh * dsub:(hh + 1) * dsub],
                                        scalar1=st[si][:, hh:hh + 1], scalar2=sc, op0=ALU.mult, op1=ALU.mult),
                                        r=[b_F5[fi], b_st[si]], w=[b_B5[bi]])
                            if bname == "a":
                                def trq(e, bi=bi, pi2=pi2):
                                    ins = None
                                    for k in range(4):
                                        ins = e.matmul(PS[pi2][:, k * 128:(k + 1) * 128], lhsT=B5[bi][:, k * 128:(k + 1) * 128], rhs=jb[:], start=True, stop=True)
                                    return ins
                                pe(trq, r=[b_B5[bi], b_const], w=[b_PS[pi2]])
                                act(lambda e, pi2=pi2, qi=qi: e.activation(out=qTt[qi][:], in_=PS[pi2][:, :].rearrange("p (k t) -> p k t", k=4),
                                                                           func=AF.Identity, scale=gqa[:, 0:1]), r=[b_PS[pi2], b_lay], w=[b_qT[qi]])
                            else:
                                pvb = PS[pi2][:].bitcast(BF16)

                                def trq(e, bi=bi, pvb=pvb):
                                    ins = None
                                    for k in range(4):
                                        ins = e.transpose(out=pvb[:, k * 128:(k + 1) * 128], in_=B5[bi][:, k * 128:(k + 1) * 128], identity=idb[:])
                                    return ins
                                pe(trq, r=[b_B5[bi], b_const], w=[b_PS[pi2]])
                                if bname == "b":
                                    act(lambda e, pvb=pvb, qi=qi: e.activation(out=qTt[qi][0:64], in_=pvb[0:64, 0:512].rearrange("p (k t) -> p k t", k=4), func=AF.Copy),
                                        r=[b_PS[pi2]], w=[b_qT[qi]])
                                    act(lambda e, pvb=pvb, qi=qi: e.activation(out=qTz[qi][64:128], in_=pvb[64:128, 0:512].rearrange("p (k t) -> p k t", k=4), func=AF.Copy),
                                        r=[b_PS[pi2]], w=[b_qT[qi]])
                                    dve(lambda e, qi=qi: e.memset(qTt[qi][64:128], 0.0), w=[b_qT[qi]])
                                    dve(lambda e, qi=qi: e.memset(qTz[qi][0:64], 0.0), w=[b_qT[qi]])
                                else:
                                    for k in range(4):
                                        act(lambda e, pvb=pvb, qi=qi, k=k: e.activation(out=qTt[qi][:, k, :], in_=pvb[:, k * 128:(k + 1) * 128],
                                                                                       func=AF.Identity, scale=gqm[:, (k % 2):(k % 2) + 1]),
                                            r=[b_PS[pi2], b_lay], w=[b_qT[qi]])
                            if bname == "a":
                                attn_a(l, hg, ti, tile, qi)
                            elif bname == "b":
                                attn_b(l, hg, ti, tile, qi, neg_lam, lam_init)
                            else:
                                attn_m(l, hg, ti, tile, qi)
                    for cp in range(8):
                        slot = next_slot()
                        gcol = 7168 + br * 2048 + cp * 256
                        wg = W[:, slot * 8192:slot * 8192 + 4096].rearrange("p (k n) -> p k n", k=16)
                        wb = W[:, slot * 8192 + 4096:slot * 8192 + 6144].rearrange("p (k n) -> p k n", k=8)
                        dma("pool", [(wg, w_in[l, :, gcol:gcol + 256].rearrange("(k p) n -> p k n", p=128)),
                                     (wb, w_br[l, br, :, cp * 256:(cp + 1) * 256].rearrange("(k p) n -> p k n", p=128))],
                            w=[b_W[slot]], key="W%d" % slot)
                        for cc in range(2):
                            c = cp * 2 + cc
                            for (t0, ntb) in tbs:
                                N = ntb * 128
                                cols = slice(t0 * 128, t0 * 128 + N)
                                pg = 4 + (zps[0] % 2) * 2
                                zps[0] += 1

                                def mmg(e, wg=wg, cc=cc, cols=cols, pg=pg, N=N):
                                    ins = None
                                    for kc in range(16):
                                        ins = e.matmul(PS[pg][:, 0:N], lhsT=wg[:, kc, cc * 128:(cc + 1) * 128], rhs=hT[:, kc, cols], start=(kc == 0), stop=(kc == 15))
                                    return ins

                                def mmp(e, wb=wb, cc=cc, cols=cols, pg=pg, N=N):
                                    ins = None
                                    for kc in range(8):
                                        ins = e.matmul(PS[pg + 1][:, 0:N], lhsT=wb[:, kc, cc * 128:(cc + 1) * 128], rhs=oT[:, kc, cols], start=(kc == 0), stop=(kc == 7))
                                    return ins
                                pe(mmg, r=[b_hT, b_W[slot]], w=[b_PS[pg]])
                                pe(mmp, r=[b_oT, b_W[slot]], w=[b_PS[pg + 1]])
                                fi = nxt("F5", NF512)
                                act(lambda e, fi=fi, pg=pg, N=N, c=c, br=br: e.activation(out=F5[fi][:, 0:N], in_=PS[pg][:, 0:N], func=AF.Sigmoid,
                                                                                          bias=gtb[:, br, c:c + 1]), r=[b_PS[pg], b_lay], w=[b_F5[fi]])
                                if br == cfg.dbg_br[0]:
                                    dve(lambda e, fi=fi, pg=pg, N=N, c=c, cols=cols: e.tensor_tensor(out=mT[:, c, cols], in0=F5[fi][:, 0:N], in1=PS[pg + 1][:, 0:N], op=ALU.mult),
                                        r=[b_F5[fi], b_PS[pg + 1]], w=[b_mT])
                                else:
                                    dve(lambda e, fi=fi, pg=pg, N=N: e.tensor_tensor(out=F5[fi][:, 0:N], in0=F5[fi][:, 0:N], in1=PS[pg + 1][:, 0:N], op=ALU.mult),
                                        r=[b_F5[fi], b_PS[pg + 1]], w=[b_F5[fi]])
                                    dve(lambda e, fi=fi, N=N, c=c, cols=cols: e.tensor_tensor(out=mT[:, c, cols], in0=F5[fi][:, 0:N], in1=mT[:, c, cols], op=ALU.add),
                                        r=[b_F5[fi], b_mT], w=[b_mT])

                for cbi in range(4):
                    slot = next_slot()
                    wv = load_w_block(slot, w_out[l, :, cbi * 512:(cbi + 1) * 512], 512)
                    for ti, tile in enumerate(tiles):
                        pi = zps[0] % 2
                        zps[0] += 1
                        fi = nxt("F5", NF512)
                        dma("sp", [(F5[fi][:], x_src(l, tile)[:, cbi * 512:(cbi + 1) * 512])], w=[b_F5[fi]], key="F5_%d" % fi)
                        pe(dense_tm(slot, wv, mT, ti, pi), r=[b_mT, b_W[slot]], w=[b_PS[pi]])
                        dve(lambda e, fi=fi, pi=pi: e.tensor_tensor(out=F5[fi][:], in0=F5[fi][:], in1=PS[pi][:, :], op=ALU.add),
                            r=[b_F5[fi], b_PS[pi]], w=[b_F5[fi]])
                        dma("sp", [(y_dst(tile)[:, cbi * 512:(cbi + 1) * 512], F5[fi][:])], r=[b_F5[fi]], key="F5_%d" % fi)
                S.barrier()

                if cfg.dbg_skip_ffn:
                    continue
                norm_stage(tiles, lambda t: y_dst(t), ffn_g[l, :], hT, b_hT)
                if has_s:
                    for f in range(NF):
                        if f % 4 == 0:
                            nf4 = min(4, NF - f)
                            fi = nxt("F5", NF512)
                            dma("sp", [(F5[fi][0:8, 0:nf4 * 128], scv[l, :, f * 128:(f + nf4) * 128])], w=[b_F5[fi]], key="F5_%d" % fi)
                            pi = 2 + (zps[0] % 2)
                            zps[0] += 1

                            def trs(e, fi=fi, pi=pi, nf4=nf4):
                                ins = None
                                for k in range(nf4):
                                    ins = e.transpose(out=PS[pi][:, k * 8:(k + 1) * 8], in_=F5[fi][0:8, k * 128:(k + 1) * 128], identity=idf[0:8, 0:8])
                                return ins
                            pe(trs, r=[b_F5[fi], b_const], w=[b_PS[pi]])
                            act(lambda e, pi=pi, f=f, nf4=nf4: e.activation(out=shalo[:, f:f + nf4, :], in_=PS[pi][:, 0:nf4 * 8].rearrange("p (k t) -> p k t", k=nf4),
                                                                            func=AF.Copy), r=[b_PS[pi]], w=[b_shalo])
                S.barrier()
                for f in range(NF):
                    slot = next_slot()
                    wu = W[:, slot * 8192:slot * 8192 + 4096].rearrange("p (k n) -> p k n", k=16)
                    dma("pool", [(wu[:, :, 0:128], w_up[l, :, f * 128:(f + 1) * 128].rearrange("(k p) n -> p k n", p=128)),
                                 (wu[:, :, 128:256], w_up[l, :, DFF + f * 128:DFF + (f + 1) * 128].rearrange("(k p) n -> p k n", p=128))],
                        w=[b_W[slot]], key="W%d" % slot)
                    for (t0, ntb) in tbs:
                        segs = []
                        nprompt = sum(1 for t in tiles[t0:t0 + ntb] if t[0] == "p")
                        if nprompt:
                            segs.append(("p", t0, nprompt))
                        if nprompt < ntb:
                            segs.append(("s", t0 + nprompt, 1))
                        for (kind, ts, ntk) in segs:
                            N = ntk * 128
                            cols = slice(ts * 128, ts * 128 + N)
                            pg = 4 + (zps[0] % 2) * 2
                            zps[0] += 1

                            def mma(e, wu=wu, cols=cols, pg=pg, N=N):
                                ins = None
                                for kc in range(16):
                                    ins = e.matmul(PS[pg][:, 0:N], lhsT=wu[:, kc, 0:128], rhs=hT[:, kc, cols], start=(kc == 0), stop=(kc == 15))
                                return ins

                            def mmb(e, wu=wu, cols=cols, pg=pg, N=N):
                                ins = None
                                for kc in range(16):
                                    ins = e.matmul(PS[pg + 1][:, 0:N], lhsT=wu[:, kc, 128:256], rhs=hT[:, kc, cols], start=(kc == 0), stop=(kc == 15))
                                return ins
                            pe(mma, r=[b_hT, b_W[slot]], w=[b_PS[pg]])
                            pe(mmb, r=[b_hT, b_W[slot]], w=[b_PS[pg + 1]])
                            fi = nxt("F5", NF512)
                            f2 = nxt("F5", NF512)
                            if kind == "p":
                                first = (tiles[ts][1] == 0)
                                if first:
                                    dve(lambda e: e.memset(abuf[:, 0:2], 0.0), w=[b_abuf])
                                else:
                                    dve(lambda e, f=f: e.tensor_copy(out=abuf[:, 0:2], in_=acar[:, f, :]), r=[b_acar], w=[b_abuf])
                                act(lambda e, pg=pg, N=N: e.activation(out=abuf[:, 2:2 + N], in_=PS[pg][:, 0:N], func=AF.Copy), r=[b_PS[pg]], w=[b_abuf])
                                act(lambda e, pg=pg, N=N, fi=fi, f=f: e.activation(out=F5[fi][:, 0:N], in_=PS[pg][:, 0:N], func=AF.Identity,
                                                                                  bias=cb[:, f:f + 1], scale=cw[:, 2, f:f + 1]), r=[b_PS[pg], b_lay], w=[b_F5[fi]])
                                dve(lambda e, fi=fi, N=N, f=f: e.scalar_tensor_tensor(out=F5[fi][:, 0:N], in0=abuf[:, 1:1 + N], scalar=cw[:, 1, f:f + 1],
                                                                                      in1=F5[fi][:, 0:N], op0=ALU.mult, op1=ALU.add), r=[b_abuf, b_F5[fi], b_lay], w=[b_F5[fi]])
                                dve(lambda e, fi=fi, N=N, f=f: e.scalar_tensor_tensor(out=F5[fi][:, 0:N], in0=abuf[:, 0:N], scalar=cw[:, 0, f:f + 1],
                                                                                      in1=F5[fi][:, 0:N], op0=ALU.mult, op1=ALU.add), r=[b_abuf, b_F5[fi], b_lay], w=[b_F5[fi]])
                                dve(lambda e, f=f, N=N: e.tensor_copy(out=acar[:, f, :], in_=abuf[:, N:N + 2]), r=[b_abuf], w=[b_acar])
                                last_tile = tiles[ts + ntk - 1]
                                if last_tile[1] == NPT - 1:
                                    dve(lambda e, f=f, N=N: e.tensor_copy(out=alast[:, f, 0:2], in_=abuf[:, N:N + 2]), r=[b_abuf], w=[b_alast])
                            else:
                                av = abuf[:, 0:136].rearrange("p (b t) -> p b t", b=4)
                                dve(lambda e, f=f, av=av: e.tensor_copy(out=av[:, :, 0:2], in_=shalo[:, f, :].rearrange("p (b t) -> p b t", b=4)),
                                    r=[b_shalo], w=[b_abuf])
                                act(lambda e, pg=pg, av=av: e.activation(out=av[:, :, 2:34], in_=PS[pg][:, 0:128].rearrange("p (b t) -> p b t", b=4), func=AF.Copy),
                                    r=[b_PS[pg]], w=[b_abuf])
                                act(lambda e, pg=pg, fi=fi, f=f: e.activation(out=F5[fi][:, 0:128], in_=PS[pg][:, 0:128], func=AF.Identity,
                                                                             bias=cb[:, f:f + 1], scale=cw[:, 2, f:f + 1]), r=[b_PS[pg], b_lay], w=[b_F5[fi]])
                                fv = F5[fi][:, 0:128].rearrange("p (b t) -> p b t", b=4)
                                dve(lambda e, fv=fv, av=av, f=f: e.scalar_tensor_tensor(out=fv, in0=av[:, :, 1:33], scalar=cw[:, 1, f:f + 1], in1=fv,
                                                                                        op0=ALU.mult, op1=ALU.add), r=[b_abuf, b_F5[fi], b_lay], w=[b_F5[fi]])
                                dve(lambda e, fv=fv, av=av, f=f: e.scalar_tensor_tensor(out=fv, in0=av[:, :, 0:32], scalar=cw[:, 0, f:f + 1], in1=fv,
                                                                                        op0=ALU.mult, op1=ALU.add), r=[b_abuf, b_F5[fi], b_lay], w=[b_F5[fi]])
                                dve(lambda e, f=f, av=av: e.tensor_copy(out=alast[:, f, 2:10].rearrange("p (b t) -> p b t", b=4), in_=av[:, :, 32:34]),
                                    r=[b_abuf], w=[b_alast])
                            act(lambda e, fi=fi, f2=f2, N=N: e.activation(out=F5[f2][:, 0:N], in_=F5[fi][:, 0:N], func=AF.Gelu), r=[b_F5[fi]], w=[b_F5[f2]])
                            dve(lambda e, f2=f2, pg=pg, N=N, f=f, cols=cols: e.tensor_tensor(out=uT[:, f, cols], in0=F5[f2][:, 0:N], in1=PS[pg + 1][:, 0:N], op=ALU.mult),
                                r=[b_F5[f2], b_PS[pg + 1]], w=[b_uT])
                last_group = (gi == len(cfg.groups) - 1)
                if has_s or last_group:
                    for f in range(NF):
                        pi = 2 + (zps[0] % 2)
                        zps[0] += 1
                        pe(lambda e, f=f, pi=pi: e.transpose(out=PS[pi][0:10, 0:128], in_=alast[:, f, :], identity=idf[:]), r=[b_alast, b_const], w=[b_PS[pi]])
                        act(lambda e, pi=pi: e.activation(out=o10[0:10, :], in_=PS[pi][0:10, 0:128], func=AF.Copy), r=[b_PS[pi]], w=[b_o10])
                        outs = []
                        if last_group:
                            outs.append((cvp[l, :, f * 128:(f + 1) * 128], o10[0:2, :]))
                        if has_s:
                            outs.append((cvs[l, :, f * 128:(f + 1) * 128], o10[2:10, :]))
                        dma("sp", outs, r=[b_o10], key="o10")
                S.barrier()
                for cb8 in range(8):
                    slot = (cb8 % 2)
                    wd = W[:, slot * 11008:slot * 11008 + NF * 256].rearrange("p (k n) -> p k n", k=NF)
                    dma("pool", [(wd, w_dn[l, :, cb8 * 256:(cb8 + 1) * 256].rearrange("(k p) n -> p k n", p=128))], w=[b_W[slot]], key="W%d" % slot)
                    for ti, tile in enumerate(tiles):
                        pi = zps[0] % 2
                        zps[0] += 1
                        fi = nxt("F5", NF512)
                        dma("sp", [(F5[fi][:, 0:256], y_dst(tile)[:, cb8 * 256:(cb8 + 1) * 256])], w=[b_F5[fi]], key="F5_%d" % fi)

                        def mmd(e, wd=wd, ti=ti, pi=pi):
                            ins = None
                            for kf in range(NF):
                                ins = e.matmul(PS[pi][:, 0:256], lhsT=uT[:, kf, ti * 128:(ti + 1) * 128], rhs=wd[:, kf, :], start=(kf == 0), stop=(kf == NF - 1))
                            return ins
                        pe(mmd, r=[b_uT, b_W[slot]], w=[b_PS[pi]])
                        dve(lambda e, fi=fi, pi=pi: e.tensor_tensor(out=F5[fi][:, 0:256], in0=F5[fi][:, 0:256], in1=PS[pi][:, 0:256], op=ALU.add),
                            r=[b_F5[fi], b_PS[pi]], w=[b_F5[fi]])
                        dma("sp", [(y_dst(tile)[:, cb8 * 256:(cb8 + 1) * 256], F5[fi][:, 0:256])], r=[b_F5[fi]], key="F5_%d" % fi)
                S.barrier(new_epoch=(gi == len(cfg.groups) - 1))


    except _Stop:
        pass
    S.in_loop = False
    print('total ops recorded', S.nops, flush=True)

    sem_keys = []
    seen = set()
    for e in ENGS:
        for o in S.q[e]:
            if o.dma:
                k = ("d", o.key)
            elif o.signal:
                k = ("e", (e, o.epoch))
            else:
                continue
            if k not in seen:
                seen.add(k)
                sem_keys.append(k)
    semmap = {}
    for i, k in enumerate(sem_keys):
        semmap[k] = es.enter_context(nc.semaphore("sm%d" % i))
    run_engine, handles = S.emit(nc, None, None)
    S._semmap = semmap
    with nc.Block() as block:
        @block.tensor
        def _(eng):
            run_engine("pe", eng)

        @block.scalar
        def _(eng):
            run_engine("act", eng)

        @block.vector
        def _(eng):
            run_engine("dve", eng)

        @block.gpsimd
        def _(eng):
            run_engine("pool", eng)

        @block.sync
        def _(eng):
            run_engine("sp", eng)
    es.close()
    return nc, S


OUT_NAMES = ["yp", "ys", "akp", "avp", "bkp", "bvp", "mkp", "mvp", "cvp", "aks", "avs", "bks", "bvs", "cvs"]
W_NAMES = ["norm_mix_g", "w_in", "a_q_norm_g", "a_k_norm_g", "a_rel_bias", "b_q_norm_g", "b_k_norm_g", "b_lam_q1", "b_lam_k1",
           "b_lam_q2", "b_lam_k2", "b_subln_g", "m_q_norm_g", "m_k_norm_g", "mem_norm_g", "w_mem_kv", "gate_b", "w_branch", "w_out",
           "norm_ffn_g", "w_ffn_up", "ffn_conv_w", "ffn_conv_b", "w_ffn_down"]


def host_consts(cfg):
    idf = np.eye(128, dtype=np.float32)
    idb = np.eye(128, dtype=np.float32).astype(ml_dtypes.bfloat16)
    jb = np.eye(128, dtype=np.float32)[::-1].copy().astype(ml_dtypes.bfloat16)
    k = np.arange(128)[:, None]
    q = np.arange(128)[None, :]
    msk = ((k // 32) == ((127 - q) // 32)).astype(np.float32)
    mskb = ((k // 32) == (q // 32)).astype(np.float32)
    half = 8
    inv_freq = np.exp(np.arange(half, dtype=np.float32) * np.float32(-2.0 * math.log(THETA) / 16)).astype(np.float32)
    pos = np.concatenate([np.arange(cfg.seq), np.tile(cfg.past + np.arange(32), 4)]).astype(np.float32)
    ang = (pos[:, None] * inv_freq[None, :]).astype(np.float32)
    cos = np.tile(np.cos(ang).astype(np.float32), (1, 8))
    sin = np.tile(np.sin(ang).astype(np.float32), (1, 8))
    return {"c_idf": idf, "c_idb": idb, "c_jb": jb, "c_msk": msk, "c_mskb": mskb,
            "c_cos": np.ascontiguousarray(cos), "c_sin": np.ascontiguousarray(sin)}


_CACHE = {}


def run(inputs, cfg, n_cores):
    key = (cfg.seq, cfg.past, cfg.depth, len(cfg.groups), cfg.dbg_skip_ffn, cfg.dbg_br, tuple(cfg.dbg_layers) if cfg.dbg_layers else None, getattr(cfg, 'dbg_stop_at', None))
    if key not in _CACHE:
        _CACHE[key] = build(cfg)
    nc, S = _CACHE[key]
    L = cfg.depth
    consts = host_consts(cfg)
    f = lambda a: np.ascontiguousarray(np.asarray(a, dtype=np.float32))
    wts = {n: f(inputs[n]) for n in W_NAMES}
    in_maps = []
    for c in range(n_cores):
        sb_ = slice(4 * c, 4 * c + 4)
        m = dict(wts)
        m.update(consts)
        m["xp"] = f(inputs["x_prompt"][c])
        m["xs"] = f(inputs["x_sample"][sb_]).reshape(128, D)
        m["cak"] = f(inputs["cache_a_k"][:, sb_]).reshape(L, 4, 512, BR)
        m["cav"] = f(inputs["cache_a_v"][:, sb_]).reshape(L, 4, 512, BR)
        m["cbk"] = f(inputs["cache_b_k"][:, sb_]).reshape(L, 4, cfg.past, BR)
        m["cbv"] = f(inputs["cache_b_v"][:, sb_]).reshape(L, 4, cfg.past, BR)
        m["cmk"] = f(inputs["cache_mem_k"][:, sb_]).reshape(L, 4, 256, BR)
        m["cmv"] = f(inputs["cache_mem_v"][:, sb_]).reshape(L, 4, 256, BR)
        m["scv"] = f(inputs["state_ffn_conv"][:, sb_]).reshape(L, 8, DFF)
        m["mem"] = f(inputs["mem_prompt"][c])
        in_maps.append(m)
    res = run_bass_kernel_spmd(nc, in_maps, core_ids=list(range(n_cores)))
    R = res.results
    g = lambda n: np.stack([np.asarray(R[c][n], dtype=np.float32) for c in range(n_cores)])
    B = n_cores
    yp = g("yp")
    ys = g("ys").reshape(B * 4, 32, D)
    AKp = cfg.akeep

    def pk(n, rows, hd):
        a = g(n)
        return np.ascontiguousarray(a.transpose(1, 0, 2, 3)).reshape(L, B, rows, BR // hd, hd)

    def sk(n, hd):
        a = g(n).reshape(B, L, 4, 32, BR)
        return np.ascontiguousarray(a.transpose(1, 0, 2, 3, 4)).reshape(L, B * 4, 32, BR // hd, hd)
    cvp = np.ascontiguousarray(g("cvp").transpose(1, 0, 2, 3))
    cvs = np.ascontiguousarray(g("cvs").reshape(B, L, 4, 2, DFF).transpose(1, 0, 2, 3, 4)).reshape(L, B * 4, 2, DFF)
    return (yp, ys, pk("akp", AKp, 128), pk("avp", AKp, 128), pk("bkp", cfg.seq, 128), pk("bvp", cfg.seq, 128),
            pk("mkp", 256, 256), pk("mvp", 256, 256), cvp, sk("aks", 128), sk("avs", 128), sk("bks", 128), sk("bvs", 128), cvs)


def kernel(**inputs):
    cfg = Cfg(seq=2048, past=4096, depth=2, ngroups=3)
    return run(inputs, cfg, 8)
```

```python
import math
import numpy as np
import ml_dtypes
import concourse.bass as bass
import concourse.mybir as mybir
from concourse.bass_utils import run_bass_kernel_spmd

F32 = mybir.dt.float32
BF16 = mybir.dt.bfloat16
AF = mybir.ActivationFunctionType
ALU = mybir.AluOpType
AX = mybir.AxisListType

D = 2048
BR = 1024
NIN = 13312
DFF = 5504
NF = 43
EPS = 1e-6
THETA = 500000.0
ENGS = ("pe", "act", "dve", "pool", "sp")
SAME_SYNC = ("act", "dve", "pool")


class Buf:
    __slots__ = ("name", "w", "rs", "rd")

    def __init__(self, name):
        self.name = name
        self.w = None
        self.rs = {}
        self.rd = []


class Op:
    __slots__ = ("eng", "fn", "waits", "signal", "sigval", "dma", "key", "dval", "epoch")


class Sched:
    def __init__(self):
        self.q = {e: [] for e in ENGS}
        self.pending = {e: [] for e in ENGS}
        self.bufs = []
        self.dma_cnt = {}
        self.last_dma = {}
        self.epoch = 0

    def buf(self, name):
        b = Buf(name)
        self.bufs.append(b)
        return b

    def bufs_n(self, name, n):
        return [self.buf("%s%d" % (name, i)) for i in range(n)]

    def add(self, eng, fn, r=(), w=(), key=None, ndma=1):
        self.nops = getattr(self, "nops", 0) + 1
        if getattr(self, "stop_at", None) is not None and self.nops > self.stop_at and getattr(self, "in_loop", False):
            raise _Stop()
        op = Op()
        op.eng = eng
        op.fn = fn
        op.signal = False
        op.sigval = 0
        op.dma = key is not None
        op.key = (key, self.epoch) if key is not None else None
        op.epoch = self.epoch
        op.waits = list(self.pending[eng])
        self.pending[eng] = []
        deps = []
        for b in r:
            if b.w is not None:
                deps.append(b.w)
        for b in w:
            if b.w is not None:
                deps.append(b.w)
            deps.extend(b.rs.values())
            deps.extend(b.rd)
        for d in deps:
            if d.dma:
                op.waits.append(d)
            elif d.eng == eng:
                if eng in SAME_SYNC:
                    d.signal = True
                    op.waits.append(d)
            else:
                d.signal = True
                op.waits.append(d)
        if op.dma:
            c = self.dma_cnt.get(op.key, 0) + 16 * ndma
            self.dma_cnt[op.key] = c
            op.dval = c
            self.last_dma[op.key] = op
        for b in r:
            if op.dma:
                b.rd.append(op)
            else:
                b.rs[eng] = op
        for b in w:
            b.w = op
            b.rs = {}
            b.rd = []
        self.q[eng].append(op)
        return op

    def barrier(self, new_epoch=False):
        lasts = []
        for e in ENGS:
            if self.q[e]:
                o = self.q[e][-1]
                if not o.dma:
                    o.signal = True
                    lasts.append(o)
        lasts.extend(self.last_dma.values())
        self.last_dma = {}
        for e in ENGS:
            for o in lasts:
                if (not o.dma) and o.eng == e and e == "pe":
                    continue
                self.pending[e].append(o)
        for b in self.bufs:
            b.w = None
            b.rs = {}
            b.rd = []
        if new_epoch:
            self.epoch += 1

    def emit(self, nc, block_ctx, sem_ctx):
        for e in ENGS:
            cnt = {}
            for o in self.q[e]:
                if o.signal and not o.dma:
                    c = cnt.get(o.epoch, 0) + 1
                    cnt[o.epoch] = c
                    o.sigval = c
                    assert c < 32000, "semaphore value too large"
        for k, v in self.dma_cnt.items():
            assert v < 32000, ("dma sem too large", k, v)
        semcache = {}

        def sem_for(kind, key):
            return self._semmap[(kind, key)]

        handles = {"pe": "tensor", "act": "scalar", "dve": "vector", "pool": "gpsimd", "sp": "sync"}
        final_keys = list(self.dma_cnt.items())

        def run_engine(e, eng):
            waited = {}
            for o in self.q[e]:
                for d in o.waits:
                    if d.dma:
                        sk = ("d", d.key)
                        val = d.dval
                    else:
                        sk = ("e", (d.eng, d.epoch))
                        val = d.sigval
                    if waited.get(sk, 0) >= val:
                        continue
                    eng.wait_ge(sem_for(*sk), val)
                    waited[sk] = val
                if o.dma:
                    o.fn(eng, sem_for("d", o.key))
                else:
                    inst = o.fn(eng)
                    if o.signal:
                        inst.then_inc(sem_for("e", (e, o.epoch)), 1)
            if e == "sp":
                for k, v in final_keys:
                    if waited.get(("d", k), 0) < v:
                        eng.wait_ge(sem_for("d", k), v)

        return run_engine, handles


class _Stop(Exception):
    pass


class Cfg:
    def __init__(self, seq=2048, past=4096, depth=2, ngroups=2, dbg_skip_ffn=False, dbg_br=(0, 1, 2), dbg_layers=None):
        self.dbg_layers = dbg_layers
        self.dbg_skip_ffn = dbg_skip_ffn
        self.dbg_br = tuple(dbg_br)
        self.seq = seq
        self.past = past
        self.depth = depth
        self.npt = seq // 128
        self.akeep = min(512, seq)
        self.nct = past // 128
        base = self.npt // ngroups
        rem = self.npt % ngroups
        sizes = [base] * ngroups
        for i in range(rem):
            sizes[(1 + i) % ngroups if ngroups > 1 else 0] += 1
        self.groups = []
        p0 = 0
        for g in range(ngroups):
            tl = [("p", p) for p in range(p0, p0 + sizes[g])]
            p0 += sizes[g]
            if g == 0:
                tl.append(("s", 0))
            self.groups.append(tl)
        self.ntmax = max(len(g) for g in self.groups)
        self.tokmax = self.ntmax * 128


def tok_blocks(nt):
    out = []
    t = 0
    while t < nt:
        n = min(4, nt - t)
        out.append((t, n))
        t += n
    return out


def build(cfg):
    nc = bass.Bass("TRN2", target_bir_lowering=False)
    L = cfg.depth
    SEQ = cfg.seq
    NPT = cfg.npt
    NCT = cfg.nct
    AK = cfg.akeep
    TOKM = cfg.tokmax
    NTM = cfg.ntmax

    def din(name, shape, dt=F32):
        return nc.dram_tensor(name, list(shape), dt, kind="ExternalInput").ap()

    def dout(name, shape, dt=F32):
        return nc.dram_tensor(name, list(shape), dt, kind="ExternalOutput").ap()

    def dscr(name, shape, dt=F32):
        return nc.dram_tensor(name, list(shape), dt, kind="Internal").ap()

    xp = din("xp", [SEQ, D])
    xs = din("xs", [128, D])
    cak = din("cak", [L, 4, 512, BR])
    cav = din("cav", [L, 4, 512, BR])
    cbk = din("cbk", [L, 4, cfg.past, BR])
    cbv = din("cbv", [L, 4, cfg.past, BR])
    cmk = din("cmk", [L, 4, 256, BR])
    cmv = din("cmv", [L, 4, 256, BR])
    scv = din("scv", [L, 8, DFF])
    mem = din("mem", [256, D])
    norm_mix_g = din("norm_mix_g", [L, D])
    w_in = din("w_in", [L, D, NIN])
    a_q_g = din("a_q_norm_g", [L, 128])
    a_k_g = din("a_k_norm_g", [L, 128])
    a_rel = din("a_rel_bias", [L, 8, 257])
    b_q_g = din("b_q_norm_g", [L, 64])
    b_k_g = din("b_k_norm_g", [L, 64])
    lq1 = din("b_lam_q1", [L, 64])
    lk1 = din("b_lam_k1", [L, 64])
    lq2 = din("b_lam_q2", [L, 64])
    lk2 = din("b_lam_k2", [L, 64])
    subln_g = din("b_subln_g", [L, 128])
    m_q_g = din("m_q_norm_g", [L, 256])
    m_k_g = din("m_k_norm_g", [L, 256])
    mem_g = din("mem_norm_g", [L, D])
    w_mem = din("w_mem_kv", [L, D, 2 * BR])
    gate_b = din("gate_b", [L, 3, D])
    w_br = din("w_branch", [L, 3, BR, D])
    w_out = din("w_out", [L, D, D])
    ffn_g = din("norm_ffn_g", [L, D])
    w_up = din("w_ffn_up", [L, D, 2 * DFF])
    conv_w = din("ffn_conv_w", [L, 3, DFF])
    conv_b = din("ffn_conv_b", [L, DFF])
    w_dn = din("w_ffn_down", [L, DFF, D])
    c_idf = din("c_idf", [128, 128])
    c_idb = din("c_idb", [128, 128], BF16)
    c_jb = din("c_jb", [128, 128], BF16)
    c_cos = din("c_cos", [SEQ + 128, 64])
    c_sin = din("c_sin", [SEQ + 128, 64])

    yp = dout("yp", [SEQ, D])
    ys = dout("ys", [128, D])
    akp = dout("akp", [L, AK, BR])
    avp = dout("avp", [L, AK, BR])
    bkp = dout("bkp", [L, SEQ, BR])
    bvp = dout("bvp", [L, SEQ, BR])
    mkp = dout("mkp", [L, 256, BR])
    mvp = dout("mvp", [L, 256, BR])
    cvp = dout("cvp", [L, 2, DFF])
    aks = dout("aks", [L, 128, BR])
    avs = dout("avs", [L, 128, BR])
    bks = dout("bks", [L, 128, BR])
    bvs = dout("bvs", [L, 128, BR])
    cvs = dout("cvs", [L, 8, DFF])
    dbgx = dout("dbgx", [128, D]) if getattr(cfg, 'dbg_dump', False) else None
    dbg_state = {'n': 0}

    s_kTa = dscr("s_kTa", [128, 8, SEQ], BF16)
    s_va = dscr("s_va", [128, NPT, 8, 129], BF16)
    s_kTb = dscr("s_kTb", [128, 8, SEQ], BF16)
    s_vb = dscr("s_vb", [128, NPT, 8, 129], BF16)
    s_mkT = dscr("s_mkT", [128, 8, 256], BF16)
    s_mv = dscr("s_mv", [128, 2, 4, 257], BF16)
    s_ext = dscr("s_ext", [8, 384], F32)

    S = Sched()
    from contextlib import ExitStack
    es = ExitStack()

    def sb(name, shape, dt):
        return es.enter_context(nc.sbuf_tensor(name, list(shape), dt))

    def pst(name, shape, dt):
        return es.enter_context(nc.psum_tensor(name, list(shape), dt))

    hT = sb("hT", [128, 16, TOKM], BF16)
    A1SZ = max(NF * TOKM, 24576, 24 * TOKM + 4 * NPT * 128 + NPT * 4 * 129)
    A1 = sb("A1", [128, A1SZ], BF16)
    W = sb("W", [128, 24576], BF16)
    a1o = 0
    oT = A1[:, a1o:a1o + 8 * TOKM].rearrange("p (k t) -> p k t", k=8)
    a1o += 8 * TOKM
    mT = A1[:, a1o:a1o + 16 * TOKM].rearrange("p (k t) -> p k t", k=16)
    a1o += 16 * TOKM
    NKT = NPT
    wkT = A1[:, a1o:a1o + 4 * NKT * 128].rearrange("p (h t) -> p h t", h=4)
    a1o += 4 * NKT * 128
    wV = A1[:, a1o:a1o + NKT * 4 * 129].rearrange("p (t h e) -> p t h e", t=NKT, h=4)
    wVflat = A1[:, a1o:a1o + NKT * 4 * 129]
    a1o += NKT * 4 * 129
    assert a1o <= A1SZ, (a1o, A1SZ)
    xts = [A1[:, i * 4096:(i + 1) * 4096].bitcast(F32) for i in range(2)]
    gtn = A1[:, 8192:12288].bitcast(F32)
    hbs = [A1[:, 12288 + i * 2048:12288 + (i + 1) * 2048] for i in range(2)]
    uT = A1[:, 0:NF * TOKM].rearrange("p (f t) -> p f t", f=NF)

    NF512 = 6
    F5 = [sb("F5_%d" % i, [128, 512], F32) for i in range(NF512)]
    B5 = [sb("B5_%d" % i, [128, 512], BF16) for i in range(4)]
    idf = sb("idf", [128, 128], F32)
    idb = sb("idb", [128, 128], BF16)
    jb = sb("jb", [128, 128], BF16)
    hank = sb("hank", [128, 4, 2, 128], F32)
    c0t = sb("c0t", [128, 8], F32)
    gka = sb("gka", [128, 128], F32)
    gqa = sb("gqa", [128, 1], F32)
    gqb = sb("gqb", [128, 64], F32)
    gkb = sb("gkb", [128, 64], F32)
    gkm = sb("gkm", [128, 256], F32)
    gqm = sb("gqm", [128, 2], F32)
    gsub = sb("gsub", [128, 128], F32)
    gtb = sb("gtb", [128, 3, 16], F32)
    cw = sb("cw", [128, 3, NF], F32)
    cb = sb("cb", [128, NF], F32)
    lamv = sb("lamv", [128, 4, 64], F32)
    lams = sb("lams", [128, 8], F32)
    st = [sb("st%d" % i, [128, 16], F32) for i in range(4)]
    cst = [sb("cs%d" % i, [128, 64], F32) for i in range(2)]
    snt = [sb("sn%d" % i, [128, 64], F32) for i in range(2)]
    ET = [sb("ET%d" % i, [128, 256], BF16) for i in range(3)]
    EPP = [sb("EPP%d" % i, [128, 256], BF16) for i in range(4)]
    EP = [EPP[i][:, 0:128] for i in range(4)]
    SP_ = [sb("SPf%d" % i, [128, 128], F32) for i in range(2)]
    ON = [sb("ON%d" % i, [128, 256], F32) for i in range(2)]
    OB = [sb("OB%d" % i, [128, 256], BF16) for i in range(2)]
    qTt = [sb("qT%d" % i, [128, 4, 128], BF16) for i in range(2)]
    qTz = [sb("qTz%d" % i, [128, 4, 128], BF16) for i in range(2)]
    cV = [sb("cV%d" % i, [128, 4, 129], BF16) for i in range(2)]
    cVm = sb("cVm", [128, 2, 4, 257], BF16)
    mkT = sb("mkT", [128, 8, 256], BF16)
    skT = sb("skT", [128, 4, 128], BF16)
    sVn = sb("sVn", [128, 4, 129], BF16)
    acar = sb("acar", [128, NF, 2], F32)
    alast = sb("alast", [128, NF, 10], F32)
    abuf = sb("abuf", [128, 516], F32)
    shalo = sb("shalo", [128, NF, 8], F32)
    s8 = sb("s8", [8, 128], F32)
    o10 = sb("o10", [16, 128], F32)
    epsc = sb("epsc", [128, 1], F32)

    PS = [pst("ps%d" % i, [128, 512], F32) for i in range(8)]

    b_hT = S.buf("hT")
    b_oT = S.buf("oT")
    b_mT = S.buf("mT")
    b_wkT = S.buf("wkT")
    b_wV = S.buf("wV")
    b_uT = S.buf("uT")
    b_xt = S.bufs_n("xt", 2)
    b_gtn = S.buf("gtn")
    b_hb = S.bufs_n("hb", 2)
    b_F5 = S.bufs_n("F5", NF512)
    b_B5 = S.bufs_n("B5", 4)
    b_const = S.buf("const")
    b_lay = S.buf("layerparams")
    b_hank = S.buf("hank")
    b_st = S.bufs_n("st", 4)
    b_cs = S.bufs_n("cs", 2)
    b_ET = S.bufs_n("ET", 3)
    b_EP = S.bufs_n("EP", 4)
    b_EPb = S.bufs_n("EPb", 4)
    b_SPf = S.bufs_n("SPf", 2)
    b_ON = S.bufs_n("ON", 2)
    b_OB = S.bufs_n("OB", 2)
    b_qT = S.bufs_n("qT", 2)
    b_cV = S.bufs_n("cV", 2)
    b_cVm = S.buf("cVm")
    b_mkT = S.buf("mkT")
    b_skT = S.buf("skT")
    b_sVn = S.buf("sVn")
    b_acar = S.buf("acar")
    b_alast = S.buf("alast")
    b_abuf = S.buf("abuf")
    b_shalo = S.buf("shalo")
    b_s8 = S.buf("s8")
    b_o10 = S.buf("o10")
    b_PS = S.bufs_n("PS", 8)
    b_W = S.bufs_n("W", 3)
    b_dram = S.buf("dram_scr")
    b_dext = S.buf("dram_ext")

    rr = {"F5": 0, "B5": 0, "st": 0, "ET": 0, "SPf": 0, "ON": 0, "OB": 0, "qT": 0, "cV": 0, "cs": 0}

    def nxt(name, n):
        i = rr[name]
        rr[name] = (i + 1) % n
        return i

    def act(fn, r=(), w=()):
        return S.add("act", fn, r, w)

    def dve(fn, r=(), w=()):
        return S.add("dve", fn, r, w)

    def pe(fn, r=(), w=()):
        return S.add("pe", fn, r, w)

    def dma(q, outs_ins, r=(), w=(), key=None, **kw):
        pairs = list(outs_ins)

        def fn(eng, sem, pairs=pairs, kw=kw):
            for (o, i) in pairs:
                eng.dma_start(out=o, in_=i, **kw).then_inc(sem, 16)
        assert key is not None
        return S.add(q, fn, r, w, key=key, ndma=len(pairs))

    def A_exp(out, in_, bias=None, scale=1.0):
        if bias is None:
            return lambda e: e.activation(out=out, in_=in_, func=AF.Exp, scale=scale)
        return lambda e: e.activation(out=out, in_=in_, func=AF.Exp, bias=bias, scale=scale)

    dma("sp", [(idf[:], c_idf[:, :]), (idb[:], c_idb[:, :]), (jb[:], c_jb[:, :])], w=[b_const], key="const")
    dve(lambda e: e.memset(epsc[:], EPS), w=[b_const])
    for i in range(4):
        dve(lambda e, i=i: e.memset(EPP[i][:], 0.0), w=[b_EP[i]])
    dve(lambda e: e.memset(acar[:], 0.0), w=[b_acar])
    dve(lambda e: e.memset(alast[:], 0.0), w=[b_alast])

    LFIRST = cfg.dbg_layers[0] if cfg.dbg_layers is not None else 0

    def x_src(l, tile):
        if tile[0] == "p":
            base = xp if l == LFIRST else yp
            return base[tile[1] * 128:(tile[1] + 1) * 128, :]
        return (xs if l == LFIRST else ys)[:, :]

    def y_dst(tile):
        if tile[0] == "p":
            return yp[tile[1] * 128:(tile[1] + 1) * 128, :]
        return ys[:, :]

    def pos_row(tile):
        return tile[1] * 128 if tile[0] == "p" else SEQ

    def norm_stage(tiles, src_fn, g_dram_row, dstT, b_dst, psbase=0):
        dma("sp", [(gtn, g_dram_row.partition_broadcast(128))], w=[b_gtn], key="gtn")
        for ti, tile in enumerate(tiles):
            i = ti % 2
            dma("sp", [(xts[i], src_fn(tile))], w=[b_xt[i]], key="xt%d" % i)
            if dbgx is not None and dstT is hT:
                dbg_state['n'] += 1
                if dbg_state['n'] == cfg.dbg_dump:
                    dma("sp", [(dbgx[:, :], xts[i])], r=[b_xt[i]], key="dbgx")
                    if getattr(cfg, 'dbg_stop', False):
                        raise _Stop()
            si = nxt("st", 4)
            act(lambda e, i=i, si=si: e.activation(out=hbs[i], in_=xts[i], func=AF.Square, accum_out=st[si][:, 0:1]),
                r=[b_xt[i]], w=[b_hb[i], b_st[si]])
            act(lambda e, si=si: e.activation(out=st[si][:, 1:2], in_=st[si][:, 0:1], func=AF.Ln, bias=epsc[:], scale=1.0 / D),
                r=[b_st[si], b_const], w=[b_st[si]])
            act(lambda e, si=si: e.activation(out=st[si][:, 2:3], in_=st[si][:, 1:2], func=AF.Exp, scale=-0.5),
                r=[b_st[si]], w=[b_st[si]])
            dve(lambda e, i=i, si=si: e.scalar_tensor_tensor(out=hbs[i], in0=xts[i], scalar=st[si][:, 2:3], in1=gtn,
                                                            op0=ALU.mult, op1=ALU.mult),
                r=[b_xt[i], b_st[si], b_gtn], w=[b_hb[i]])
            for half in range(2):
                pi = psbase + (ti % 2) * 2 + half
                pv = PS[pi][:].bitcast(BF16)

                def tr(e, i=i, half=half, pv=pv):
                    ins = None
                    for k in range(8):
                        kc = half * 8 + k
                        ins = e.transpose(out=pv[:, k * 128:(k + 1) * 128], in_=hbs[i][:, kc * 128:(kc + 1) * 128], identity=idb[:])
                    return ins
                pe(tr, r=[b_hb[i], b_const], w=[b_PS[pi]])
                act(lambda e, half=half, pv=pv, ti=ti: e.activation(
                    out=dstT[:, half * 8:(half + 1) * 8, ti * 128:(ti + 1) * 128],
                    in_=pv[:, 0:1024].rearrange("p (k t) -> p k t", k=8), func=AF.Copy),
                    r=[b_PS[pi]], w=[b_dst])

    def load_w_block(slot, src_ap_3d, ncols, q="pool"):
        base = slot * 8192 if ncols == 512 else None
        dst = W[:, base:base + 16 * ncols].rearrange("p (k n) -> p k n", k=16)
        dma(q, [(dst, src_ap_3d.rearrange("(k p) n -> p k n", p=128))], w=[b_W[slot]], key="W%d" % slot)
        return dst

    wslot = [0]

    def next_slot(n=3):
        s = wslot[0]
        wslot[0] = (s + 1) % n
        return s

    zps = [0]

    def dense_tm(slot, wv, lhs_src, ti, ps_i, nk=16):
        def fn(e):
            ins = None
            for kc in range(nk):
                ins = e.matmul(PS[ps_i][:, :], lhsT=lhs_src[:, kc, ti * 128:(ti + 1) * 128], rhs=wv[:, kc, :],
                               start=(kc == 0), stop=(kc == nk - 1))
            return ins
        return fn


    rt = [sb("rt%d" % i, [128, 64], F32) for i in range(4)]
    b_rt = S.buf("rt")
    smV = [sb("smV%d" % i, [128, 2, 257], BF16) for i in range(2)]
    b_smV = S.bufs_n("smV", 2)
    mskt = sb("mskt", [128, 128], F32)
    c_msk = din("c_msk", [128, 128])
    dma("sp", [(mskt[:], c_msk[:, :])], w=[b_const], key="const")
    for i in range(2):
        dve(lambda e, i=i: e.memset(cV[i][:], 1.0), w=[b_cV[i]])
        dve(lambda e, i=i: e.memset(smV[i][:], 1.0), w=[b_smV[i]])
    ctr = {"s": 0, "o": 0, "t": 0, "ep": 0, "smv": 0}

    def rope(fi, tile):
        ci = nxt("cs", 2)
        r0 = pos_row(tile)
        dma("sp", [(cst[ci][:], c_cos[r0:r0 + 128, :]), (snt[ci][:], c_sin[r0:r0 + 128, :])], w=[b_cs[ci]], key="cs%d" % ci)
        v = F5[fi][:].rearrange("p (s d) -> p s d", s=8)
        x1 = v[:, :, 0:8]
        x2 = v[:, :, 8:16]
        C = cst[ci][:].rearrange("p (s d) -> p s d", s=8)
        Sn = snt[ci][:].rearrange("p (s d) -> p s d", s=8)
        T = [rt[i][:].rearrange("p (s d) -> p s d", s=8) for i in range(4)]
        dve(lambda e: e.tensor_tensor(out=T[0], in0=x1, in1=C, op=ALU.mult), r=[b_F5[fi], b_cs[ci]], w=[b_rt])
        dve(lambda e: e.tensor_tensor(out=T[1], in0=x2, in1=Sn, op=ALU.mult), r=[b_F5[fi], b_cs[ci]], w=[b_rt])
        dve(lambda e: e.tensor_tensor(out=T[2], in0=x2, in1=C, op=ALU.mult), r=[b_F5[fi], b_cs[ci]], w=[b_rt])
        dve(lambda e: e.tensor_tensor(out=T[3], in0=x1, in1=Sn, op=ALU.mult), r=[b_F5[fi], b_cs[ci]], w=[b_rt])
        dve(lambda e: e.tensor_tensor(out=x1, in0=T[0], in1=T[1], op=ALU.subtract), r=[b_rt], w=[b_F5[fi]])
        dve(lambda e: e.tensor_tensor(out=x2, in0=T[2], in1=T[3], op=ALU.add), r=[b_rt], w=[b_F5[fi]])

    def s_bank():
        i = (4, 5, 3)[ctr["s"] % 3]
        ctr["s"] += 1
        return i

    def t_bank():
        return 2

    def cache_group(kd, vd, nj):
        fi = nxt("F5", NF512)
        dma("sp", [(F5[fi][:, 0:nj * 128].rearrange("p (j d) -> p j d", j=nj), kd)], w=[b_F5[fi]], key="F5_%d" % fi)
        ci = nxt("cV", 2)
        dma("pool", [(cV[ci][:, 0:nj, 0:128], vd)], w=[b_cV[ci]], key="cV%d" % ci)
        pt = t_bank()

        def tr(e, fi=fi, pt=pt, nj=nj):
            ins = None
            for k in range(nj):
                ins = e.transpose(out=PS[pt][:, k * 128:(k + 1) * 128], in_=F5[fi][:, k * 128:(k + 1) * 128], identity=idf[:])
            return ins
        pe(tr, r=[b_F5[fi], b_const], w=[b_PS[pt]])
        bi = nxt("B5", 4)
        act(lambda e, bi=bi, pt=pt, nj=nj: e.activation(out=B5[bi][:, 0:nj * 128], in_=PS[pt][:, 0:nj * 128], func=AF.Copy), r=[b_PS[pt]], w=[b_B5[bi]])
        return B5[bi][:, 0:nj * 128].rearrange("p (j d) -> p j d", j=nj), b_B5[bi], cV[ci], b_cV[ci]

    def qrange(b, rev):
        return slice(96 - 32 * b, 128 - 32 * b) if rev else slice(32 * b, 32 * b + 32)

    def finish_to_oT(ob_ap, b_ob, chunk, ti, rev):
        pt = t_bank()
        if rev:
            pe(lambda e: e.matmul(PS[pt][:, 0:128], lhsT=ob_ap, rhs=jb[:], start=True, stop=True), r=[b_ob, b_const], w=[b_PS[pt]])
            act(lambda e: e.activation(out=oT[:, chunk, ti * 128:(ti + 1) * 128], in_=PS[pt][:, 0:128], func=AF.Copy), r=[b_PS[pt]], w=[b_oT])
        else:
            pvb = PS[pt][:].bitcast(BF16)
            pe(lambda e: e.transpose(out=pvb[:, 0:128], in_=ob_ap, identity=idb[:]), r=[b_ob, b_const], w=[b_PS[pt]])
            act(lambda e: e.activation(out=oT[:, chunk, ti * 128:(ti + 1) * 128], in_=pvb[:, 0:128], func=AF.Copy), r=[b_PS[pt]], w=[b_oT])


    def run_stages(stages, after_first=None):
        n = len(stages)
        if n == 0:
            if after_first:
                after_first()
            return

        def do_st(s):
            if s.get("pre"):
                s["pre"]()
            s["st"]()
        skew = 1 if any(s.get("pre") for s in stages) else 2
        do_st(stages[0])
        if after_first:
            after_first()
        for k in range(1, skew):
            if k < n:
                do_st(stages[k])
        for j in range(n):
            if j + skew < n:
                do_st(stages[j + skew])
            stages[j]["mid"]()
            stages[j]["pv"]()

    def split_cache_loads(kd, vd, nj):
        fi = nxt("F5", NF512)
        dma("sp", [(F5[fi][:, 0:nj * 128].rearrange("p (j d) -> p j d", j=nj), kd)], w=[b_F5[fi]], key="F5_%d" % fi)
        ci = nxt("cV", 2)
        dma("pool", [(cV[ci][:, 0:nj, 0:128], vd)], w=[b_cV[ci]], key="cV%d" % ci)
        return {"fi": fi, "ci": ci, "nj": nj}

    def cache_xpose(ld):
        fi, nj = ld["fi"], ld["nj"]
        pt = t_bank()

        def tr(e, fi=fi, pt=pt, nj=nj):
            ins = None
            for k in range(nj):
                ins = e.transpose(out=PS[pt][:, k * 128:(k + 1) * 128], in_=F5[fi][:, k * 128:(k + 1) * 128], identity=idf[:])
            return ins
        pe(tr, r=[b_F5[fi], b_const], w=[b_PS[pt]])
        bi = nxt("B5", 4)
        act(lambda e, bi=bi, pt=pt, nj=nj: e.activation(out=B5[bi][:, 0:nj * 128], in_=PS[pt][:, 0:nj * 128], func=AF.Copy), r=[b_PS[pt]], w=[b_B5[bi]])
        ld["kTc"] = B5[bi][:, 0:nj * 128].rearrange("p (j d) -> p j d", j=nj)
        ld["bk"] = b_B5[bi]
        ld["vc"] = cV[ld["ci"]]
        ld["bv"] = b_cV[ld["ci"]]

    def attn_a(l, hg, ti, tile, qi):
        pend = [None]
        for hh in range(4):
            h = hg * 4 + hh
            po = 6 + (ctr["o"] % 2)
            ctr["o"] += 1
            q = qTt[qi]
            stages = []
            if tile[0] == "p":
                t = tile[1]
                js = list(range(max(0, t - 4), t + 1))
                for idx, j in enumerate(js):
                    dd = t - j
                    s = {}

                    def f_st(s=s, j=j, hh=hh):
                        ps = s_bank()
                        s["ps"] = ps
                        pe(lambda e, ps=ps: e.matmul(PS[ps][:, 0:128], lhsT=wkT[:, hh, j * 128:(j + 1) * 128], rhs=q[:, hh, :], start=True, stop=True),
                           r=[b_wkT, b_qT[qi]], w=[b_PS[ps]])

                    def f_mid(s=s, dd=dd, hh=hh, h=h):
                        ps = s["ps"]
                        ei = nxt("ET", 3)
                        s["ei"] = ei
                        if dd >= 2:
                            act(A_exp(ET[ei][:, 0:128], PS[ps][:, 0:128], bias=c0t[:, h:h + 1]), r=[b_PS[ps], b_lay], w=[b_ET[ei]])
                            if dd == 4:
                                dve(lambda e, ei=ei: e.memset(ET[ei][0:64, 0:64], 0.0), w=[b_ET[ei]])
                        else:
                            sp_ = nxt("SPf", 2)
                            dve(lambda e, sp_=sp_, ps=ps: e.tensor_tensor(out=SP_[sp_][:], in0=PS[ps][:, 0:128], in1=hank[:, hh, dd, :], op=ALU.add),
                                r=[b_PS[ps], b_hank], w=[b_SPf[sp_]])
                            act(A_exp(ET[ei][:, 0:128], SP_[sp_][:]), r=[b_SPf[sp_]], w=[b_ET[ei]])
                            if dd == 0:
                                dve(lambda e, ei=ei: e.memset(ET[ei][64:128, 64:128], 0.0), w=[b_ET[ei]])

                    def f_pv(s=s, j=j, hh=hh, idx=idx, n=len(js), po=po):
                        ei = s["ei"]
                        pe(lambda e, ei=ei: e.matmul(PS[po][:, 0:129], lhsT=ET[ei][:, 0:128], rhs=wV[:, j, hh, :], start=(idx == 0), stop=(idx == n - 1)),
                           r=[b_ET[ei], b_wV], w=[b_PS[po]])
                    s.update(st=f_st, mid=f_mid, pv=f_pv)
                    stages.append(s)
            else:
                lds = [None] * 4

                def mk_ld(b, h=h):
                    return split_cache_loads(cak[l, b, :, h * 128:(h + 1) * 128].rearrange("(j p) d -> p j d", p=128),
                                             cav[l, b, :, h * 128:(h + 1) * 128].rearrange("(j p) d -> p j d", p=128), 4)
                lds[0] = mk_ld(0)
                cache_xpose(lds[0])
                cnt = [0]
                for b in range(4):
                    qr = qrange(b, True)
                    for jj in range(4):
                        s = {}
                        pre = None
                        if jj == 1 and b + 1 < 4:
                            def pre(b=b):
                                lds[b + 1] = mk_ld(b + 1)
                        elif jj == 3 and b + 1 < 4:
                            def pre(b=b):
                                cache_xpose(lds[b + 1])

                        def f_st(s=s, b=b, jj=jj, qr=qr, hh=hh):
                            ps = s_bank()
                            s["ps"] = ps
                            ld = lds[b]
                            pe(lambda e, ps=ps, kTc=ld["kTc"]: e.matmul(PS[ps][:, 0:32], lhsT=kTc[:, jj, :], rhs=q[:, hh, qr], start=True, stop=True),
                               r=[ld["bk"], b_qT[qi]], w=[b_PS[ps]])

                        def f_mid(s=s, jj=jj, qr=qr, hh=hh, h=h):
                            ps = s["ps"]
                            ep = ctr["ep"] % 4
                            ctr["ep"] += 1
                            s["ep"] = ep
                            dve(lambda e, ep=ep: e.memset(EP[ep][:], 0.0), w=[b_EP[ep]])
                            if jj < 3:
                                act(A_exp(EP[ep][:, qr], PS[ps][:, 0:32], bias=c0t[:, h:h + 1]), r=[b_PS[ps], b_lay], w=[b_EP[ep]])
                            else:
                                sp_ = nxt("SPf", 2)
                                dve(lambda e, sp_=sp_, ps=ps: e.tensor_tensor(out=SP_[sp_][:, 0:32], in0=PS[ps][:, 0:32], in1=hank[:, hh, 1, 96:128], op=ALU.add),
                                    r=[b_PS[ps], b_hank], w=[b_SPf[sp_]])
                                act(A_exp(EP[ep][:, qr], SP_[sp_][:, 0:32]), r=[b_SPf[sp_]], w=[b_EP[ep]])

                        def f_pv(s=s, b=b, jj=jj, first=(b == 0 and jj == 0), po=po):
                            ep = s["ep"]
                            ld = lds[b]
                            pe(lambda e, ep=ep, vc=ld["vc"]: e.matmul(PS[po][:, 0:129], lhsT=EP[ep][:, :], rhs=vc[:, jj, :], start=first, stop=False),
                               r=[b_EP[ep], ld["bv"]], w=[b_PS[po]])
                        s.update(st=f_st, mid=f_mid, pv=f_pv, pre=pre)
                        stages.append(s)
                s = {}

                def f_st(s=s, hh=hh):
                    ps = s_bank()
                    s["ps"] = ps
                    pe(lambda e, ps=ps: e.matmul(PS[ps][:, 0:128], lhsT=skT[:, hh, :], rhs=q[:, hh, :], start=True, stop=True),
                       r=[b_skT, b_qT[qi]], w=[b_PS[ps]])

                def f_mid(s=s, hh=hh):
                    ps = s["ps"]
                    sp_ = nxt("SPf", 2)
                    dve(lambda e, sp_=sp_, ps=ps: e.tensor_tensor(out=SP_[sp_][:], in0=PS[ps][:, 0:128], in1=hank[:, hh, 0, :], op=ALU.add),
                        r=[b_PS[ps], b_hank], w=[b_SPf[sp_]])
                    act(A_exp(SP_[sp_][:], SP_[sp_][:]), r=[b_SPf[sp_]], w=[b_SPf[sp_]])
                    ei = nxt("ET", 3)
                    s["ei"] = ei
                    dve(lambda e, sp_=sp_, ei=ei: e.tensor_tensor(out=ET[ei][:, 0:128], in0=SP_[sp_][:], in1=mskt[:], op=ALU.mult), r=[b_SPf[sp_], b_const], w=[b_ET[ei]])

                def f_pv(s=s, hh=hh, po=po):
                    ei = s["ei"]
                    pe(lambda e, ei=ei: e.matmul(PS[po][:, 0:129], lhsT=ET[ei][:, 0:128], rhs=sVn[:, hh, :], start=False, stop=True),
                       r=[b_ET[ei], b_sVn], w=[b_PS[po]])
                s.update(st=f_st, mid=f_mid, pv=f_pv)
                stages.append(s)
            run_stages(stages, after_first=pend[0])
            si = nxt("st", 4)
            oi = nxt("OB", 2)
            dve(lambda e, si=si, po=po: e.reciprocal(out=st[si][:, 0:1], in_=PS[po][:, 128:129]), r=[b_PS[po]], w=[b_st[si]])
            dve(lambda e, si=si, po=po, oi=oi: e.tensor_scalar(out=OB[oi][:, 0:128], in0=PS[po][:, 0:128], scalar1=st[si][:, 0:1], scalar2=None, op0=ALU.mult),
                r=[b_PS[po], b_st[si]], w=[b_OB[oi]])
            pend[0] = (lambda oi=oi, hh=hh: finish_to_oT(OB[oi][:, 0:128], b_OB[oi], hg * 4 + hh, ti, True))
        pend[0]()

    def attn_b(l, hg, ti, tile, qi, neg_lam, lam_init):
        q = qTt[qi]
        qz = qTz[qi]
        pend = [None]
        for hh in range(4):
            h = hg * 4 + hh
            p0, p1 = 6, 7
            stages = []

            def st_mm(kt_ap, qcols, N, ps, hh=hh):
                def fn(e):
                    e.matmul(PS[ps][:, 0:N], lhsT=kt_ap, rhs=q[:, hh, qcols], start=True, stop=True)
                    return e.matmul(PS[ps][:, 128:128 + N], lhsT=kt_ap, rhs=qz[:, hh, qcols], start=True, stop=True)
                return fn

            def pv_mm(e0, e1, v_ap, first, last):
                def fn(e):
                    e.matmul(PS[p0][:, 0:129], lhsT=e0, rhs=v_ap, start=first, stop=last)
                    return e.matmul(PS[p1][:, 0:129], lhsT=e1, rhs=v_ap, start=first, stop=last)
                return fn
            if tile[0] == "p":
                t = tile[1]
                for j in range(t + 1):
                    s = {}

                    def f_st(s=s, j=j, hh=hh, st_mm=st_mm):
                        ps = s_bank()
                        s["ps"] = ps
                        pe(st_mm(wkT[:, hh, j * 128:(j + 1) * 128], slice(0, 128), 128, ps), r=[b_wkT, b_qT[qi]], w=[b_PS[ps]])

                    def f_mid(s=s, j=j, t=t):
                        ps = s["ps"]
                        ei = nxt("ET", 3)
                        s["ei"] = ei
                        act(A_exp(ET[ei][:, 0:256], PS[ps][:, 0:256]), r=[b_PS[ps]], w=[b_ET[ei]])
                        if j == t:
                            dve(lambda e, ei=ei: e.memset(ET[ei][64:128, :].rearrange("p (m q) -> p m q", m=2)[:, :, 0:64], 0.0), w=[b_ET[ei]])

                    def f_pv(s=s, j=j, t=t, hh=hh, pv_mm=pv_mm):
                        ei = s["ei"]
                        pe(pv_mm(ET[ei][:, 0:128], ET[ei][:, 128:256], wV[:, j, hh, :], j == 0, j == t), r=[b_ET[ei], b_wV], w=[b_PS[p0], b_PS[p1]])
                    s.update(st=f_st, mid=f_mid, pv=f_pv)
                    stages.append(s)
            else:
                groups = [(b, jg) for b in range(4) for jg in range(NCT // 4)]
                lds = [None] * len(groups)

                def mk_ld(g, h=h):
                    b, jg = groups[g]
                    rows = slice(jg * 512, (jg + 1) * 512)
                    return split_cache_loads(cbk[l, b, rows, h * 128:(h + 1) * 128].rearrange("(j p) d -> p j d", p=128),
                                             cbv[l, b, rows, h * 128:(h + 1) * 128].rearrange("(j p) d -> p j d", p=128), 4)
                lds[0] = mk_ld(0)
                cache_xpose(lds[0])
                for g, (b, jg) in enumerate(groups):
                    qr = qrange(b, False)
                    for jj in range(4):
                        s = {}
                        pre = None
                        if jj == 1 and g + 1 < len(groups):
                            def pre(g=g):
                                lds[g + 1] = mk_ld(g + 1)
                        elif jj == 3 and g + 1 < len(groups):
                            def pre(g=g):
                                cache_xpose(lds[g + 1])

                        def f_st(s=s, g=g, jj=jj, qr=qr, st_mm=st_mm):
                            ps = s_bank()
                            s["ps"] = ps
                            ld = lds[g]
                            pe(st_mm(ld["kTc"][:, jj, :], qr, 32, ps), r=[ld["bk"], b_qT[qi]], w=[b_PS[ps]])

                        def f_mid(s=s, qr=qr):
                            ps = s["ps"]
                            ep = ctr["ep"] % 4
                            ctr["ep"] += 1
                            s["ep"] = ep
                            dve(lambda e, ep=ep: e.memset(EPP[ep][:], 0.0), w=[b_EP[ep]])
                            act(A_exp(EPP[ep][:].rearrange("p (m q) -> p m q", m=2)[:, :, qr],
                                      PS[ps][:, 0:256].rearrange("p (m q) -> p m q", m=2)[:, :, 0:32]), r=[b_PS[ps]], w=[b_EP[ep]])

                        def f_pv(s=s, g=g, jj=jj, first=(g == 0 and jj == 0), pv_mm=pv_mm):
                            ep = s["ep"]
                            ld = lds[g]
                            pe(pv_mm(EPP[ep][:, 0:128], EPP[ep][:, 128:256], ld["vc"][:, jj, :], first, False), r=[b_EP[ep], ld["bv"]], w=[b_PS[p0], b_PS[p1]])
                        s.update(st=f_st, mid=f_mid, pv=f_pv, pre=pre)
                        stages.append(s)
                s = {}

                def f_st(s=s, hh=hh, st_mm=st_mm):
                    ps = s_bank()
                    s["ps"] = ps
                    pe(st_mm(skT[:, hh, :], slice(0, 128), 128, ps), r=[b_skT, b_qT[qi]], w=[b_PS[ps]])

                def f_mid(s=s):
                    ps = s["ps"]
                    fi = nxt("F5", NF512)
                    act(A_exp(F5[fi][:, 0:256], PS[ps][:, 0:256]), r=[b_PS[ps]], w=[b_F5[fi]])
                    ei = nxt("ET", 3)
                    s["ei"] = ei
                    dve(lambda e, fi=fi, ei=ei: e.tensor_tensor(out=ET[ei][:, 0:256].rearrange("p (m q) -> p m q", m=2),
                                                                in0=F5[fi][:, 0:256].rearrange("p (m q) -> p m q", m=2),
                                                                in1=mskb3, op=ALU.mult), r=[b_F5[fi], b_const], w=[b_ET[ei]])

                def f_pv(s=s, hh=hh, pv_mm=pv_mm):
                    ei = s["ei"]
                    pe(pv_mm(ET[ei][:, 0:128], ET[ei][:, 128:256], sVn[:, hh, :], False, True), r=[b_ET[ei], b_sVn], w=[b_PS[p0], b_PS[p1]])
                s.update(st=f_st, mid=f_mid, pv=f_pv)
                stages.append(s)
            run_stages(stages, after_first=pend[0])
            si = nxt("st", 4)
            oi = nxt("ON", 2)
            ob = nxt("OB", 2)
            f2 = nxt("F5", NF512)
            dve(lambda e, si=si: e.reciprocal(out=st[si][:, 0:1], in_=PS[p0][:, 128:129]), r=[b_PS[p0]], w=[b_st[si]])
            dve(lambda e, si=si: e.reciprocal(out=st[si][:, 1:2], in_=PS[p1][:, 128:129]), r=[b_PS[p1]], w=[b_st[si]])
            dve(lambda e, si=si: e.tensor_tensor(out=st[si][:, 2:3], in0=st[si][:, 1:2], in1=neg_lam, op=ALU.mult), r=[b_st[si], b_lay], w=[b_st[si]])
            dve(lambda e, si=si, oi=oi: e.tensor_scalar(out=ON[oi][:, 0:128], in0=PS[p0][:, 0:128], scalar1=st[si][:, 0:1], scalar2=None, op0=ALU.mult),
                r=[b_PS[p0], b_st[si]], w=[b_ON[oi]])
            dve(lambda e, si=si, oi=oi: e.scalar_tensor_tensor(out=ON[oi][:, 0:128], in0=PS[p1][:, 0:128], scalar=st[si][:, 2:3], in1=ON[oi][:, 0:128],
                                                               op0=ALU.mult, op1=ALU.add), r=[b_PS[p1], b_st[si], b_ON[oi]], w=[b_ON[oi]])
            dve(lambda e, oi=oi, f2=f2: e.tensor_tensor(out=F5[f2][:, 0:128], in0=ON[oi][:, 0:128], in1=ON[oi][:, 0:128], op=ALU.mult), r=[b_ON[oi]], w=[b_F5[f2]])
            dve(lambda e, si=si, f2=f2: e.tensor_reduce(out=st[si][:, 3:4], in_=F5[f2][:, 0:128], axis=AX.X, op=ALU.add), r=[b_F5[f2]], w=[b_st[si]])
            act(lambda e, si=si: e.activation(out=st[si][:, 4:5], in_=st[si][:, 3:4], func=AF.Ln, bias=epsc[:], scale=1.0 / 128), r=[b_st[si], b_const], w=[b_st[si]])
            act(lambda e, si=si: e.activation(out=st[si][:, 5:6], in_=st[si][:, 4:5], func=AF.Exp, scale=-0.5), r=[b_st[si]], w=[b_st[si]])
            dve(lambda e, si=si, oi=oi, ob=ob: e.scalar_tensor_tensor(out=OB[ob][:, 0:128], in0=ON[oi][:, 0:128], scalar=st[si][:, 5:6], in1=gsub[:],
                                                                      op0=ALU.mult, op1=ALU.mult), r=[b_ON[oi], b_st[si], b_lay], w=[b_OB[ob]])
            pend[0] = (lambda ob=ob, hh=hh: finish_to_oT(OB[ob][:, 0:128], b_OB[ob], hg * 4 + hh, ti, False))
        pend[0]()

    def attn_m(l, hg, ti, tile, qi):
        q = qTt[qi]
        pend = [None]
        for hl in range(2):
            h = hg * 2 + hl
            po = 6 + (ctr["o"] % 2)
            ctr["o"] += 1
            stages = []
            if tile[0] == "p":
                for jj in range(2):
                    s = {}

                    def f_st(s=s, jj=jj, h=h, hl=hl):
                        ps = s_bank()
                        s["ps"] = ps

                        def smm(e, ps=ps):
                            e.matmul(PS[ps][:, 0:128], lhsT=mkT[:, h * 2, jj * 128:(jj + 1) * 128], rhs=q[:, hl * 2, :], start=True, stop=False)
                            return e.matmul(PS[ps][:, 0:128], lhsT=mkT[:, h * 2 + 1, jj * 128:(jj + 1) * 128], rhs=q[:, hl * 2 + 1, :], start=False, stop=True)
                        pe(smm, r=[b_mkT, b_qT[qi]], w=[b_PS[ps]])

                    def f_mid(s=s):
                        ps = s["ps"]
                        ei = nxt("ET", 3)
                        s["ei"] = ei
                        act(A_exp(ET[ei][:, 0:128], PS[ps][:, 0:128]), r=[b_PS[ps]], w=[b_ET[ei]])

                    def f_pv(s=s, jj=jj, h=h, po=po):
                        ei = s["ei"]
                        pe(lambda e, ei=ei: e.matmul(PS[po][:, 0:257], lhsT=ET[ei][:, 0:128], rhs=cVm[:, jj, h, :], start=(jj == 0), stop=(jj == 1)),
                           r=[b_ET[ei], b_cVm], w=[b_PS[po]])
                    s.update(st=f_st, mid=f_mid, pv=f_pv)
                    stages.append(s)
            else:
                lds = [None] * 4

                def mk_ld(b, h=h):
                    fi = nxt("F5", NF512)
                    dma("sp", [(F5[fi][:].rearrange("p (j d) -> p j d", j=2), cmk[l, b, :, h * 256:(h + 1) * 256].rearrange("(j p) d -> p j d", p=128))],
                        w=[b_F5[fi]], key="F5_%d" % fi)
                    mi = ctr["smv"] % 2
                    ctr["smv"] += 1
                    dma("pool", [(smV[mi][:, :, 0:256], cmv[l, b, :, h * 256:(h + 1) * 256].rearrange("(j p) d -> p j d", p=128))], w=[b_smV[mi]], key="smV%d" % mi)
                    return {"fi": fi, "mi": mi}

                def xp_ld(ld):
                    fi = ld["fi"]
                    pt = t_bank()

                    def tr(e, fi=fi, pt=pt):
                        ins = None
                        for k in range(4):
                            ins = e.transpose(out=PS[pt][:, k * 128:(k + 1) * 128], in_=F5[fi][:, k * 128:(k + 1) * 128], identity=idf[:])
                        return ins
                    pe(tr, r=[b_F5[fi], b_const], w=[b_PS[pt]])
                    bi = nxt("B5", 4)
                    act(lambda e, bi=bi, pt=pt: e.activation(out=B5[bi][:], in_=PS[pt][:, :], func=AF.Copy), r=[b_PS[pt]], w=[b_B5[bi]])
                    ld["kv"] = B5[bi][:].rearrange("p (j c t) -> p j c t", j=2, c=2)
                    ld["bk"] = b_B5[bi]
                lds[0] = mk_ld(0)
                xp_ld(lds[0])
                for b in range(4):
                    qr = qrange(b, False)
                    for jj in range(2):
                        s = {}
                        pre = None
                        if jj == 1 and b + 1 < 4:
                            def pre(b=b):
                                lds[b + 1] = mk_ld(b + 1)
                        elif jj == 0 and b > 0:
                            def pre(b=b):
                                xp_ld(lds[b])

                        def f_st(s=s, b=b, jj=jj, qr=qr, hl=hl):
                            ps = s_bank()
                            s["ps"] = ps
                            ld = lds[b]

                            def smm(e, ps=ps, kv=ld["kv"]):
                                e.matmul(PS[ps][:, 0:32], lhsT=kv[:, jj, 0, :], rhs=q[:, hl * 2, qr], start=True, stop=False)
                                return e.matmul(PS[ps][:, 0:32], lhsT=kv[:, jj, 1, :], rhs=q[:, hl * 2 + 1, qr], start=False, stop=True)
                            pe(smm, r=[ld["bk"], b_qT[qi]], w=[b_PS[ps]])

                        def f_mid(s=s, qr=qr):
                            ps = s["ps"]
                            ep = ctr["ep"] % 4
                            ctr["ep"] += 1
                            s["ep"] = ep
                            dve(lambda e, ep=ep: e.memset(EP[ep][:], 0.0), w=[b_EP[ep]])
                            act(A_exp(EP[ep][:, qr], PS[ps][:, 0:32]), r=[b_PS[ps]], w=[b_EP[ep]])

                        def f_pv(s=s, b=b, jj=jj, first=(b == 0 and jj == 0), last=(b == 3 and jj == 1), po=po):
                            ep = s["ep"]
                            mi = lds[b]["mi"]
                            pe(lambda e, ep=ep, mi=mi: e.matmul(PS[po][:, 0:257], lhsT=EP[ep][:, :], rhs=smV[mi][:, jj, :], start=first, stop=last),
                               r=[b_EP[ep], b_smV[mi]], w=[b_PS[po]])
                        s.update(st=f_st, mid=f_mid, pv=f_pv, pre=pre)
                        stages.append(s)
            run_stages(stages, after_first=pend[0])
            si = nxt("st", 4)
            oi = nxt("OB", 2)
            dve(lambda e, si=si, po=po: e.reciprocal(out=st[si][:, 0:1], in_=PS[po][:, 256:257]), r=[b_PS[po]], w=[b_st[si]])
            dve(lambda e, si=si, po=po, oi=oi: e.tensor_scalar(out=OB[oi][:, 0:256], in0=PS[po][:, 0:256], scalar1=st[si][:, 0:1], scalar2=None, op0=ALU.mult),
                r=[b_PS[po], b_st[si]], w=[b_OB[oi]])
            def fin(oi=oi, hl=hl):
                for dc in range(2):
                    finish_to_oT(OB[oi][:, dc * 128:(dc + 1) * 128], b_OB[oi], hg * 4 + hl * 2 + dc, ti, False)
            pend[0] = fin
        pend[0]()

    mskb = sb("mskb", [128, 128], F32)
    c_mskb = din("c_mskb", [128, 128])
    dma("sp", [(mskb[:], c_mskb[:, :])], w=[b_const], key="const")
    mskb3 = mskb[:].unsqueeze(1).to_broadcast([128, 2, 128])


    def skewed_dense(slot, wv, lhs_src, rbufs):
        pis = {}

        def issue(ti):
            pi = zps[0] % 2
            zps[0] += 1
            pis[ti] = pi
            pe(dense_tm(slot, wv, lhs_src, ti, pi), r=rbufs, w=[b_PS[pi]])
        return pis, issue

    inv_sqrt = {"a": 128 ** -0.5, "b": 64 ** -0.5, "m": 256 ** -0.5}

    S.stop_at = getattr(cfg, 'dbg_stop_at', None)
    S.in_loop = True
    try:
        for l in (cfg.dbg_layers if cfg.dbg_layers is not None else range(L)):
            lam_init = 0.8 - 0.6 * math.exp(-0.3 * l)
            prm = [
                (gka[:], a_k_g[l, :].partition_broadcast(128)),
                (gqb[:], b_q_g[l, :].partition_broadcast(128)),
                (gkb[:], b_k_g[l, :].partition_broadcast(128)),
                (gkm[:], m_k_g[l, :].partition_broadcast(128)),
                (gsub[:], subln_g[l, :].partition_broadcast(128)),
                (lamv[:, 0, :], lq1[l, :].partition_broadcast(128)),
                (lamv[:, 1, :], lk1[l, :].partition_broadcast(128)),
                (lamv[:, 2, :], lq2[l, :].partition_broadcast(128)),
                (lamv[:, 3, :], lk2[l, :].partition_broadcast(128)),
            ]
            dma("sp", prm, w=[b_lay], key="lay")
            dma("sp", [(gqa[:], a_q_g[l, :].rearrange("(p o) -> p o", o=1)),
                       (gqm[:, 0:1], m_q_g[l, 0:128].rearrange("(p o) -> p o", o=1)),
                       (gqm[:, 1:2], m_q_g[l, 128:256].rearrange("(p o) -> p o", o=1))]
                + [(c0t[:, h:h + 1], a_rel[l, h, 0:1].partition_broadcast(128)) for h in range(8)], w=[b_lay], key="lay2")
            fa = nxt("F5", NF512)
            fb = nxt("F5", NF512)
            dma("sp", [(F5[fa][0:48, 0:128], gate_b[l, :, :].rearrange("n (c p) -> (n c) p", p=128)),
                       (F5[fa][0:8, 128 + 127:128 + 384], a_rel[l, :, :])], w=[b_F5[fa]], key="F5_%d" % fa)
            dma("sp", [(F5[fb][0:NF, j * 128:(j + 1) * 128], conv_w[l, j, :].rearrange("(f p) -> f p", p=128)) for j in range(3)]
                + [(F5[fb][0:NF, 384:512], conv_b[l, :].rearrange("(f p) -> f p", p=128))], w=[b_F5[fb]], key="F5_%d" % fb)
            dve(lambda e, fa=fa: e.tensor_scalar(out=F5[fa][0:8, 128:128 + 127], in0=F5[fa][0:8, 128 + 128:128 + 255], scalar1=0.0,
                                                 scalar2=F5[fa][0:8, 128 + 127:128 + 128], op0=ALU.mult, op1=ALU.add), r=[b_F5[fa]], w=[b_F5[fa]])
            dma("sp", [(s_ext[:, :], F5[fa][0:8, 128:512])], r=[b_F5[fa]], w=[b_dext], key="ext")
            pe(lambda e, fa=fa: e.transpose(out=PS[2][:, 0:48], in_=F5[fa][0:48, 0:128], identity=idf[0:48, 0:48]), r=[b_F5[fa], b_const], w=[b_PS[2]])
            act(lambda e: e.activation(out=gtb[:].rearrange("p n c -> p (n c)"), in_=PS[2][:, 0:48], func=AF.Copy), r=[b_PS[2]], w=[b_lay])

            def trc(e, fb=fb):
                ins = None
                for j in range(4):
                    ins = e.transpose(out=PS[3][:, j * 64:j * 64 + NF], in_=F5[fb][0:NF, j * 128:(j + 1) * 128], identity=idf[0:NF, 0:NF])
                return ins
            pe(trc, r=[b_F5[fb], b_const], w=[b_PS[3]])
            act(lambda e: e.activation(out=cw[:], in_=PS[3][:, 0:192].rearrange("p (j f) -> p j f", j=3)[:, :, 0:NF], func=AF.Copy), r=[b_PS[3]], w=[b_lay])
            act(lambda e: e.activation(out=cb[:], in_=PS[3][:, 192:192 + NF], func=AF.Copy), r=[b_PS[3]], w=[b_lay])
            dve(lambda e: e.tensor_tensor(out=lamv[:, 0, :], in0=lamv[:, 0, :], in1=lamv[:, 1, :], op=ALU.mult), r=[b_lay], w=[b_lay])
            dve(lambda e: e.tensor_tensor(out=lamv[:, 2, :], in0=lamv[:, 2, :], in1=lamv[:, 3, :], op=ALU.mult), r=[b_lay], w=[b_lay])
            dve(lambda e: e.tensor_reduce(out=lams[:, 0:1], in_=lamv[:, 0, :], axis=AX.X, op=ALU.add), r=[b_lay], w=[b_lay])
            dve(lambda e: e.tensor_reduce(out=lams[:, 1:2], in_=lamv[:, 2, :], axis=AX.X, op=ALU.add), r=[b_lay], w=[b_lay])
            act(lambda e: e.activation(out=lams[:, 2:4], in_=lams[:, 0:2], func=AF.Exp), r=[b_lay], w=[b_lay])
            dve(lambda e, li=lam_init: e.scalar_tensor_tensor(out=lams[:, 4:5], in0=lams[:, 3:4], scalar=-li, in1=lams[:, 2:3],
                                                 op0=ALU.add, op1=ALU.subtract), r=[b_lay], w=[b_lay])
            neg_lam = lams[:, 4:5]
            dve(lambda e, li=lam_init: e.tensor_scalar(out=gsub[:], in0=gsub[:], scalar1=1.0 - li, scalar2=None, op0=ALU.mult), r=[b_lay], w=[b_lay])

            for gi, tiles in enumerate(cfg.groups):
                NT = len(tiles)
                has_s = any(t[0] == "s" for t in tiles)
                ptiles = [t for t in tiles if t[0] == "p"]
                pt0 = ptiles[0][1]
                npl = len(ptiles)
                tbs = tok_blocks(NT)

                norm_stage(tiles, lambda t, l=l: x_src(l, t), norm_mix_g[l, :], hT, b_hT)
                S.barrier()

                if gi == 0:
                    hmT = A1[:, 16384:20480].rearrange("p (k t) -> p k t", k=16)
                    norm_stage([("m", 0), ("m", 1)], lambda t: mem[t[1] * 128:(t[1] + 1) * 128, :], mem_g[l, :], hmT, b_mT, psbase=4)
                    for cbk_i in range(4):
                        slot = next_slot()
                        wv = load_w_block(slot, w_mem[l, :, cbk_i * 512:(cbk_i + 1) * 512], 512)
                        for mt in range(2):
                            pi = zps[0] % 2
                            zps[0] += 1
                            pe(dense_tm(slot, wv, hmT, mt, pi), r=[b_mT, b_W[slot]], w=[b_PS[pi]])
                            fi = nxt("F5", NF512)
                            act(lambda e, pi=pi, fi=fi: e.activation(out=F5[fi][:], in_=PS[pi][:, :], func=AF.Copy),
                                r=[b_PS[pi]], w=[b_F5[fi]])
                            rows = slice(mt * 128, (mt + 1) * 128)
                            if cbk_i < 2:
                                si = nxt("st", 4)
                                f2 = nxt("F5", NF512)
                                dve(lambda e, fi=fi, f2=f2: e.tensor_tensor(out=F5[f2][:], in0=F5[fi][:], in1=F5[fi][:], op=ALU.mult),
                                    r=[b_F5[fi]], w=[b_F5[f2]])
                                dve(lambda e, f2=f2, si=si: e.tensor_reduce(out=st[si][:, 0:2], in_=F5[f2][:].rearrange("p (h d) -> p h d", h=2),
                                                                            axis=AX.X, op=ALU.add), r=[b_F5[f2]], w=[b_st[si]])
                                act(lambda e, si=si: e.activation(out=st[si][:, 2:4], in_=st[si][:, 0:2], func=AF.Ln, bias=epsc[:], scale=1.0 / 256),
                                    r=[b_st[si], b_const], w=[b_st[si]])
                                act(lambda e, si=si: e.activation(out=st[si][:, 4:6], in_=st[si][:, 2:4], func=AF.Exp, scale=-0.5),
                                    r=[b_st[si]], w=[b_st[si]])
                                for hh in range(2):
                                    dve(lambda e, fi=fi, si=si, hh=hh: e.scalar_tensor_tensor(
                                        out=F5[fi][:, hh * 256:(hh + 1) * 256], in0=F5[fi][:, hh * 256:(hh + 1) * 256],
                                        scalar=st[si][:, 4 + hh:5 + hh], in1=gkm[:], op0=ALU.mult, op1=ALU.mult),
                                        r=[b_F5[fi], b_st[si], b_lay], w=[b_F5[fi]])
                                dma("sp", [(mkp[l, rows, cbk_i * 512:(cbk_i + 1) * 512], F5[fi][:])], r=[b_F5[fi]], key="F5_%d" % fi)
                                pi2 = 2 + (zps[0] % 2)

                                def trm(e, fi=fi, pi2=pi2):
                                    ins = None
                                    for k in range(4):
                                        ins = e.transpose(out=PS[pi2][:, k * 128:(k + 1) * 128], in_=F5[fi][:, k * 128:(k + 1) * 128], identity=idf[:])
                                    return ins
                                pe(trm, r=[b_F5[fi], b_const], w=[b_PS[pi2]])
                                act(lambda e, pi2=pi2, cbk_i=cbk_i, mt=mt: e.activation(
                                    out=mkT[:, cbk_i * 4:(cbk_i + 1) * 4, mt * 128:(mt + 1) * 128],
                                    in_=PS[pi2][:, :].rearrange("p (k t) -> p k t", k=4), func=AF.Copy),
                                    r=[b_PS[pi2]], w=[b_mkT])
                            else:
                                hb0 = (cbk_i - 2) * 2
                                dma("sp", [(mvp[l, rows, (cbk_i - 2) * 512:(cbk_i - 1) * 512], F5[fi][:])], r=[b_F5[fi]], key="F5_%d" % fi)
                                dve(lambda e, fi=fi, mt=mt, hb0=hb0: e.tensor_copy(
                                    out=cVm[:, mt, hb0:hb0 + 2, 0:256], in_=F5[fi][:].rearrange("p (h d) -> p h d", h=2)),
                                    r=[b_F5[fi]], w=[b_cVm])
                    dve(lambda e: e.memset(cVm[:, :, :, 256:257], 1.0), w=[b_cVm])
                    dma("sp", [(s_mkT[:, :, :], mkT[:]), (s_mv[:, :, :, :], cVm[:])], r=[b_mkT, b_cVm], w=[b_dram], key="carry_m")
                    S.barrier()
                else:
                    dma("sp", [(mkT[:], s_mkT[:, :, :]), (cVm[:], s_mv[:, :, :, :])], r=[b_dram], w=[b_mkT, b_cVm], key="carry_m")

                dve(lambda e: e.memset(wVflat, 1.0), w=[b_wV])
                for br in cfg.dbg_br:
                    bname = "abm"[br]
                    qoff = {"a": 0, "b": 3072, "m": 6144}[bname]
                    for hg in range(2):
                        if bname == "a":
                            dma("sp", [(hank[:, hh, x, :], bass.AP(s_ext.tensor, (hg * 4 + hh) * 384 + (128 if x == 0 else 0), [[1, 128], [1, 128]]))
                                       for hh in range(4) for x in range(2)], r=[b_dext], w=[b_hank], key="hank")
                        if bname in ("a", "b"):
                            koff = qoff + 1024 + hg * 512
                            voff = qoff + 2048 + hg * 512
                            s_kT = s_kTa if bname == "a" else s_kTb
                            s_v = s_va if bname == "a" else s_vb
                            kout_p = akp if bname == "a" else bkp
                            vout_p = avp if bname == "a" else bvp
                            kout_s = aks if bname == "a" else bks
                            vout_s = avs if bname == "a" else bvs
                            if bname == "a":
                                kt_lo = max(0, pt0 - 4)
                            else:
                                kt_lo = 0
                            if pt0 > kt_lo:
                                dma("sp", [(wkT[:, :, kt_lo * 128:pt0 * 128], s_kT[:, hg * 4:(hg + 1) * 4, kt_lo * 128:pt0 * 128]),
                                           (wV[:, kt_lo:pt0, :, :], s_v[:, kt_lo:pt0, hg * 4:(hg + 1) * 4, :])],
                                    r=[b_dram], w=[b_wkT, b_wV], key="carry_ld")
                            slot = next_slot()
                            wv = load_w_block(slot, w_in[l, :, koff:koff + 512], 512)
                            pis, issue = skewed_dense(slot, wv, hT, [b_hT, b_W[slot]])
                            issue(0)
                            for ti, tile in enumerate(tiles):
                                if ti + 1 < NT:
                                    issue(ti + 1)
                                pi = pis[ti]
                                fi = nxt("F5", NF512)
                                f2 = nxt("F5", NF512)
                                si = nxt("st", 4)
                                act(lambda e, pi=pi, fi=fi: e.activation(out=F5[fi][:], in_=PS[pi][:, :], func=AF.Copy), r=[b_PS[pi]], w=[b_F5[fi]])
                                nsub = 4 if bname == "a" else 8
                                dsub = 512 // nsub
                                dve(lambda e, fi=fi, f2=f2: e.tensor_tensor(out=F5[f2][:], in0=F5[fi][:], in1=F5[fi][:], op=ALU.mult),
                                    r=[b_F5[fi]], w=[b_F5[f2]])
                                dve(lambda e, f2=f2, si=si, nsub=nsub: e.tensor_reduce(out=st[si][:, 0:nsub], in_=F5[f2][:].rearrange("p (h d) -> p h d", h=nsub),
                                                                                       axis=AX.X, op=ALU.add), r=[b_F5[f2]], w=[b_st[si]])
                                act(lambda e, si=si, nsub=nsub, dsub=dsub: e.activation(out=st[si][:, 8:8 + nsub], in_=st[si][:, 0:nsub], func=AF.Ln,
                                                                                        bias=epsc[:], scale=1.0 / dsub), r=[b_st[si], b_const], w=[b_st[si]])
                                act(lambda e, si=si, nsub=nsub: e.activation(out=st[si][:, 0:nsub], in_=st[si][:, 8:8 + nsub], func=AF.Exp, scale=-0.5),
                                    r=[b_st[si]], w=[b_st[si]])
                                gt_ = gka if bname == "a" else gkb
                                for hh in range(nsub):
                                    dve(lambda e, fi=fi, si=si, hh=hh, dsub=dsub, gt_=gt_: e.scalar_tensor_tensor(
                                        out=F5[fi][:, hh * dsub:(hh + 1) * dsub], in0=F5[fi][:, hh * dsub:(hh + 1) * dsub],
                                        scalar=st[si][:, hh:hh + 1], in1=gt_[:], op0=ALU.mult, op1=ALU.mult),
                                        r=[b_F5[fi], b_st[si], b_lay], w=[b_F5[fi]])
                                if bname == "b":
                                    rope(fi, tile)
                                if tile[0] == "s":
                                    dma("sp", [(kout_s[l, :, hg * 512:(hg + 1) * 512], F5[fi][:])], r=[b_F5[fi]], key="F5_%d" % fi)
                                else:
                                    prow = tile[1] * 128
                                    if bname == "b":
                                        dma("sp", [(kout_p[l, prow:prow + 128, hg * 512:(hg + 1) * 512], F5[fi][:])], r=[b_F5[fi]], key="F5_%d" % fi)
                                    elif prow >= SEQ - AK:
                                        r0 = prow - (SEQ - AK)
                                        dma("sp", [(kout_p[l, r0:r0 + 128, hg * 512:(hg + 1) * 512], F5[fi][:])], r=[b_F5[fi]], key="F5_%d" % fi)
                                pi2 = 2 + (zps[0] % 2)

                                def trk(e, fi=fi, pi2=pi2):
                                    ins = None
                                    for k in range(4):
                                        ins = e.transpose(out=PS[pi2][:, k * 128:(k + 1) * 128], in_=F5[fi][:, k * 128:(k + 1) * 128], identity=idf[:])
                                    return ins
                                pe(trk, r=[b_F5[fi], b_const], w=[b_PS[pi2]])
                                if tile[0] == "s":
                                    act(lambda e, pi2=pi2: e.activation(out=skT[:], in_=PS[pi2][:, :].rearrange("p (k t) -> p k t", k=4), func=AF.Copy),
                                        r=[b_PS[pi2]], w=[b_skT])
                                else:
                                    act(lambda e, pi2=pi2, kt=tile[1]: e.activation(out=wkT[:, :, kt * 128:(kt + 1) * 128],
                                                                                    in_=PS[pi2][:, :].rearrange("p (k t) -> p k t", k=4), func=AF.Copy),
                                        r=[b_PS[pi2]], w=[b_wkT])
                            slot = next_slot()
                            wv = load_w_block(slot, w_in[l, :, voff:voff + 512], 512)
                            pis, issue = skewed_dense(slot, wv, hT, [b_hT, b_W[slot]])
                            issue(0)
                            for ti, tile in enumerate(tiles):
                                if ti + 1 < NT:
                                    issue(ti + 1)
                                pi = pis[ti]
                                fi = nxt("F5", NF512)
                                act(lambda e, pi=pi, fi=fi: e.activation(out=F5[fi][:], in_=PS[pi][:, :], func=AF.Copy), r=[b_PS[pi]], w=[b_F5[fi]])
                                if tile[0] == "s":
                                    dma("sp", [(vout_s[l, :, hg * 512:(hg + 1) * 512], F5[fi][:])], r=[b_F5[fi]], key="F5_%d" % fi)
                                    dve(lambda e, fi=fi: e.tensor_copy(out=sVn[:, :, 0:128], in_=F5[fi][:].rearrange("p (h d) -> p h d", h=4)),
                                        r=[b_F5[fi]], w=[b_sVn])
                                    dve(lambda e: e.memset(sVn[:, :, 128:129], 1.0), w=[b_sVn])
                                else:
                                    prow = tile[1] * 128
                                    if bname == "b":
                                        dma("sp", [(vout_p[l, prow:prow + 128, hg * 512:(hg + 1) * 512], F5[fi][:])], r=[b_F5[fi]], key="F5_%d" % fi)
                                    elif prow >= SEQ - AK:
                                        r0 = prow - (SEQ - AK)
                                        dma("sp", [(vout_p[l, r0:r0 + 128, hg * 512:(hg + 1) * 512], F5[fi][:])], r=[b_F5[fi]], key="F5_%d" % fi)
                                    dve(lambda e, fi=fi, kt=tile[1]: e.tensor_copy(out=wV[:, kt, :, 0:128], in_=F5[fi][:].rearrange("p (h d) -> p h d", h=4)),
                                        r=[b_F5[fi]], w=[b_wV])
                            if gi + 1 < len(cfg.groups) and npl > 0:
                                dma("sp", [(s_kT[:, hg * 4:(hg + 1) * 4, pt0 * 128:(pt0 + npl) * 128], wkT[:, :, pt0 * 128:(pt0 + npl) * 128]),
                                           (s_v[:, pt0:pt0 + npl, hg * 4:(hg + 1) * 4, :], wV[:, pt0:pt0 + npl, :, :])],
                                    r=[b_wkT, b_wV], w=[b_dram], key="carry_st")
                        slot = next_slot()
                        wv = load_w_block(slot, w_in[l, :, qoff + hg * 512:qoff + (hg + 1) * 512], 512)
                        pis, issue = skewed_dense(slot, wv, hT, [b_hT, b_W[slot]])
                        issue(0)
                        for ti, tile in enumerate(tiles):
                            if ti + 1 < NT:
                                issue(ti + 1)
                            pi = pis[ti]
                            fi = nxt("F5", NF512)
                            f2 = nxt("F5", NF512)
                            si = nxt("st", 4)
                            act(lambda e, pi=pi, fi=fi: e.activation(out=F5[fi][:], in_=PS[pi][:, :], func=AF.Copy), r=[b_PS[pi]], w=[b_F5[fi]])
                            nsub = {"a": 4, "b": 8, "m": 2}[bname]
                            dsub = 512 // nsub
                            sc = inv_sqrt[bname]
                            dve(lambda e, fi=fi, f2=f2: e.tensor_tensor(out=F5[f2][:], in0=F5[fi][:], in1=F5[fi][:], op=ALU.mult),
                                r=[b_F5[fi]], w=[b_F5[f2]])
                            dve(lambda e, f2=f2, si=si, nsub=nsub: e.tensor_reduce(out=st[si][:, 0:nsub], in_=F5[f2][:].rearrange("p (h d) -> p h d", h=nsub),
                                                                                   axis=AX.X, op=ALU.add), r=[b_F5[f2]], w=[b_st[si]])
                            act(lambda e, si=si, nsub=nsub, dsub=dsub: e.activation(out=st[si][:, 8:8 + nsub], in_=st[si][:, 0:nsub], func=AF.Ln,
                                                                                    bias=epsc[:], scale=1.0 / dsub), r=[b_st[si], b_const], w=[b_st[si]])
                            act(lambda e, si=si, nsub=nsub: e.activation(out=st[si][:, 0:nsub], in_=st[si][:, 8:8 + nsub], func=AF.Exp, scale=-0.5),
                                r=[b_st[si]], w=[b_st[si]])
                            bi = nxt("B5", 4)
                            qi = nxt("qT", 2)
                            pi2 = 2 + (zps[0] % 2)
                            if bname == "b":
                                for hh in range(nsub):
                                    dve(lambda e, fi=fi, si=si, hh=hh, dsub=dsub: e.scalar_tensor_tensor(
                                        out=F5[fi][:, hh * dsub:(hh + 1) * dsub], in0=F5[fi][:, hh * dsub:(hh + 1) * dsub],
                                        scalar=st[si][:, hh:hh + 1], in1=gqb[:], op0=ALU.mult, op1=ALU.mult),
                                        r=[b_F5[fi], b_st[si], b_lay], w=[b_F5[fi]])
                                rope(fi, tile)
                                dve(lambda e, fi=fi, bi=bi, sc=sc: e.tensor_scalar(out=B5[bi][:], in0=F5[fi][:], scalar1=sc, scalar2=None, op0=ALU.mult),
                                    r=[b_F5[fi]], w=[b_B5[bi]])
                            else:
                                for hh in range(nsub):
                                    dve(lambda e, fi=fi, bi=bi, si=si, hh=hh, dsub=dsub, sc=sc: e.tensor_scalar(
                                        out=B5[bi][:, hh * dsub:(hh + 1) * dsub], in0=F5[fi][:, hh * dsub:(hh + 1) * dsub],
                                        scalar1=st[si][:, hh:hh + 1], scalar2=sc, op0=ALU.mult, op1=ALU.mult),
                                        r=[b_F5[fi], b_st[si]], w=[b_B5[bi]])
                            if bname == "a":
                                def trq(e, bi=bi, pi2=pi2):
                                    ins = None
                                    for k in range(4):
                                        ins = e.matmul(PS[pi2][:, k * 128:(k + 1) * 128], lhsT=B5[bi][:, k * 128:(k + 1) * 128], rhs=jb[:], start=True, stop=True)
                                    return ins
                                pe(trq, r=[b_B5[bi], b_const], w=[b_PS[pi2]])
                                act(lambda e, pi2=pi2, qi=qi: e.activation(out=qTt[qi][:], in_=PS[pi2][:, :].rearrange("p (k t) -> p k t", k=4),
                                                                           func=AF.Identity, scale=gqa[:, 0:1]), r=[b_PS[pi2], b_lay], w=[b_qT[qi]])
                            else:
                                pvb = PS[pi2][:].bitcast(BF16)

                                def trq(e, bi=bi, pvb=pvb):
                                    ins = None
                                    for k in range(4):
                                        ins = e.transpose(out=pvb[:, k * 128:(k + 1) * 128], in_=B5[bi][:, k * 128:(k + 1) * 128], identity=idb[:])
                                    return ins
                                pe(trq, r=[b_B5[bi], b_const], w=[b_PS[pi2]])
                                if bname == "b":
                                    act(lambda e, pvb=pvb, qi=qi: e.activation(out=qTt[qi][0:64], in_=pvb[0:64, 0:512].rearrange("p (k t) -> p k t", k=4), func=AF.Copy),
                                        r=[b_PS[pi2]], w=[b_qT[qi]])
                                    act(lambda e, pvb=pvb, qi=qi: e.activation(out=qTz[qi][64:128], in_=pvb[64:128, 0:512].rearrange("p (k t) -> p k t", k=4), func=AF.Copy),
                                        r=[b_PS[pi2]], w=[b_qT[qi]])
                                    dve(lambda e, qi=qi: e.memset(qTt[qi][64:128], 0.0), w=[b_qT[qi]])
                                    dve(lambda e, qi=qi: e.memset(qTz[qi][0:64], 0.0), w=[b_qT[qi]])
                                else:
                                    for k in range(4):
                                        act(lambda e, pvb=pvb, qi=qi, k=k: e.activation(out=qTt[qi][:, k, :], in_=pvb[:, k * 128:(k + 1) * 128],
                                                                                       func=AF.Identity, scale=gqm[:, (k % 2):(k % 2) + 1]),
                                            r=[b_PS[pi2], b_lay], w=[b_qT[qi]])
                            if bname == "a":
                                attn_a(l, hg, ti, tile, qi)
                            elif bname == "b":
                                attn_b(l, hg, ti, tile, qi, neg_lam, lam_init)
                            else:
                                attn_m(l, hg, ti, tile, qi)
                    for cp in range(8):
                        slot = next_slot()
                        gcol = 7168 + br * 2048 + cp * 256
                        wg = W[:, slot * 8192:slot * 8192 + 4096].rearrange("p (k n) -> p k n", k=16)
                        wb = W[:, slot * 8192 + 4096:slot * 8192 + 6144].rearrange("p (k n) -> p k n", k=8)
                        dma("pool", [(wg, w_in[l, :, gcol:gcol + 256].rearrange("(k p) n -> p k n", p=128)),
                                     (wb, w_br[l, br, :, cp * 256:(cp + 1) * 256].rearrange("(k p) n -> p k n", p=128))],
                            w=[b_W[slot]], key="W%d" % slot)
                        for cc in range(2):
                            c = cp * 2 + cc
                            for (t0, ntb) in tbs:
                                N = ntb * 128
                                cols = slice(t0 * 128, t0 * 128 + N)
                                pg = 4 + (zps[0] % 2) * 2
                                zps[0] += 1

                                def mmg(e, wg=wg, cc=cc, cols=cols, pg=pg, N=N):
                                    ins = None
                                    for kc in range(16):
                                        ins = e.matmul(PS[pg][:, 0:N], lhsT=wg[:, kc, cc * 128:(cc + 1) * 128], rhs=hT[:, kc, cols], start=(kc == 0), stop=(kc == 15))
                                    return ins

                                def mmp(e, wb=wb, cc=cc, cols=cols, pg=pg, N=N):
                                    ins = None
                                    for kc in range(8):
                                        ins = e.matmul(PS[pg + 1][:, 0:N], lhsT=wb[:, kc, cc * 128:(cc + 1) * 128], rhs=oT[:, kc, cols], start=(kc == 0), stop=(kc == 7))
                                    return ins
                                pe(mmg, r=[b_hT, b_W[slot]], w=[b_PS[pg]])
                                pe(mmp, r=[b_oT, b_W[slot]], w=[b_PS[pg + 1]])
                                fi = nxt("F5", NF512)
                                act(lambda e, fi=fi, pg=pg, N=N, c=c, br=br: e.activation(out=F5[fi][:, 0:N], in_=PS[pg][:, 0:N], func=AF.Sigmoid,
                                                                                          bias=gtb[:, br, c:c + 1]), r=[b_PS[pg], b_lay], w=[b_F5[fi]])
                                if br == cfg.dbg_br[0]:
                                    dve(lambda e, fi=fi, pg=pg, N=N, c=c, cols=cols: e.tensor_tensor(out=mT[:, c, cols], in0=F5[fi][:, 0:N], in1=PS[pg + 1][:, 0:N], op=ALU.mult),
                                        r=[b_F5[fi], b_PS[pg + 1]], w=[b_mT])
                                else:
                                    dve(lambda e, fi=fi, pg=pg, N=N: e.tensor_tensor(out=F5[fi][:, 0:N], in0=F5[fi][:, 0:N], in1=PS[pg + 1][:, 0:N], op=ALU.mult),
                                        r=[b_F5[fi], b_PS[pg + 1]], w=[b_F5[fi]])
                                    dve(lambda e, fi=fi, N=N, c=c, cols=cols: e.tensor_tensor(out=mT[:, c, cols], in0=F5[fi][:, 0:N], in1=mT[:, c, cols], op=ALU.add),
                                        r=[b_F5[fi], b_mT], w=[b_mT])

                for cbi in range(4):
                    slot = next_slot()
                    wv = load_w_block(slot, w_out[l, :, cbi * 512:(cbi + 1) * 512], 512)
                    for ti, tile in enumerate(tiles):
                        pi = zps[0] % 2
                        zps[0] += 1
                        fi = nxt("F5", NF512)
                        dma("sp", [(F5[fi][:], x_src(l, tile)[:, cbi * 512:(cbi + 1) * 512])], w=[b_F5[fi]], key="F5_%d" % fi)
                        pe(dense_tm(slot, wv, mT, ti, pi), r=[b_mT, b_W[slot]], w=[b_PS[pi]])
                        dve(lambda e, fi=fi, pi=pi: e.tensor_tensor(out=F5[fi][:], in0=F5[fi][:], in1=PS[pi][:, :], op=ALU.add),
                            r=[b_F5[fi], b_PS[pi]], w=[b_F5[fi]])
                        dma("sp", [(y_dst(tile)[:, cbi * 512:(cbi + 1) * 512], F5[fi][:])], r=[b_F5[fi]], key="F5_%d" % fi)
                S.barrier()

                if cfg.dbg_skip_ffn:
                    continue
                norm_stage(tiles, lambda t: y_dst(t), ffn_g[l, :], hT, b_hT)
                if has_s:
                    for f in range(NF):
                        if f % 4 == 0:
                            nf4 = min(4, NF - f)
                            fi = nxt("F5", NF512)
                            dma("sp", [(F5[fi][0:8, 0:nf4 * 128], scv[l, :, f * 128:(f + nf4) * 128])], w=[b_F5[fi]], key="F5_%d" % fi)
                            pi = 2 + (zps[0] % 2)
                            zps[0] += 1

                            def trs(e, fi=fi, pi=pi, nf4=nf4):
                                ins = None
                                for k in range(nf4):
                                    ins = e.transpose(out=PS[pi][:, k * 8:(k + 1) * 8], in_=F5[fi][0:8, k * 128:(k + 1) * 128], identity=idf[0:8, 0:8])
                                return ins
                            pe(trs, r=[b_F5[fi], b_const], w=[b_PS[pi]])
                            act(lambda e, pi=pi, f=f, nf4=nf4: e.activation(out=shalo[:, f:f + nf4, :], in_=PS[pi][:, 0:nf4 * 8].rearrange("p (k t) -> p k t", k=nf4),
                                                                            func=AF.Copy), r=[b_PS[pi]], w=[b_shalo])
                S.barrier()
                for f in range(NF):
                    slot = next_slot()
                    wu = W[:, slot * 8192:slot * 8192 + 4096].rearrange("p (k n) -> p k n", k=16)
                    dma("pool", [(wu[:, :, 0:128], w_up[l, :, f * 128:(f + 1) * 128].rearrange("(k p) n -> p k n", p=128)),
                                 (wu[:, :, 128:256], w_up[l, :, DFF + f * 128:DFF + (f + 1) * 128].rearrange("(k p) n -> p k n", p=128))],
                        w=[b_W[slot]], key="W%d" % slot)
                    for (t0, ntb) in tbs:
                        segs = []
                        nprompt = sum(1 for t in tiles[t0:t0 + ntb] if t[0] == "p")
                        if nprompt:
                            segs.append(("p", t0, nprompt))
                        if nprompt < ntb:
                            segs.append(("s", t0 + nprompt, 1))
                        for (kind, ts, ntk) in segs:
                            N = ntk * 128
                            cols = slice(ts * 128, ts * 128 + N)
                            pg = 4 + (zps[0] % 2) * 2
                            zps[0] += 1

                            def mma(e, wu=wu, cols=cols, pg=pg, N=N):
                                ins = None
                                for kc in range(16):
                                    ins = e.matmul(PS[pg][:, 0:N], lhsT=wu[:, kc, 0:128], rhs=hT[:, kc, cols], start=(kc == 0), stop=(kc == 15))
                                return ins

                            def mmb(e, wu=wu, cols=cols, pg=pg, N=N):
                                ins = None
                                for kc in range(16):
                                    ins = e.matmul(PS[pg + 1][:, 0:N], lhsT=wu[:, kc, 128:256], rhs=hT[:, kc, cols], start=(kc == 0), stop=(kc == 15))
                                return ins
                            pe(mma, r=[b_hT, b_W[slot]], w=[b_PS[pg]])
                            pe(mmb, r=[b_hT, b_W[slot]], w=[b_PS[pg + 1]])
                            fi = nxt("F5", NF512)
                            f2 = nxt("F5", NF512)
                            if kind == "p":
                                first = (tiles[ts][1] == 0)
                                if first:
                                    dve(lambda e: e.memset(abuf[:, 0:2], 0.0), w=[b_abuf])
                                else:
                                    dve(lambda e, f=f: e.tensor_copy(out=abuf[:, 0:2], in_=acar[:, f, :]), r=[b_acar], w=[b_abuf])
                                act(lambda e, pg=pg, N=N: e.activation(out=abuf[:, 2:2 + N], in_=PS[pg][:, 0:N], func=AF.Copy), r=[b_PS[pg]], w=[b_abuf])
                                act(lambda e, pg=pg, N=N, fi=fi, f=f: e.activation(out=F5[fi][:, 0:N], in_=PS[pg][:, 0:N], func=AF.Identity,
                                                                                  bias=cb[:, f:f + 1], scale=cw[:, 2, f:f + 1]), r=[b_PS[pg], b_lay], w=[b_F5[fi]])
                                dve(lambda e, fi=fi, N=N, f=f: e.scalar_tensor_tensor(out=F5[fi][:, 0:N], in0=abuf[:, 1:1 + N], scalar=cw[:, 1, f:f + 1],
                                                                                      in1=F5[fi][:, 0:N], op0=ALU.mult, op1=ALU.add), r=[b_abuf, b_F5[fi], b_lay], w=[b_F5[fi]])
                                dve(lambda e, fi=fi, N=N, f=f: e.scalar_tensor_tensor(out=F5[fi][:, 0:N], in0=abuf[:, 0:N], scalar=cw[:, 0, f:f + 1],
                                                                                      in1=F5[fi][:, 0:N], op0=ALU.mult, op1=ALU.add), r=[b_abuf, b_F5[fi], b_lay], w=[b_F5[fi]])
                                dve(lambda e, f=f, N=N: e.tensor_copy(out=acar[:, f, :], in_=abuf[:, N:N + 2]), r=[b_abuf], w=[b_acar])
                                last_tile = tiles[ts + ntk - 1]
                                if last_tile[1] == NPT - 1:
                                    dve(lambda e, f=f, N=N: e.tensor_copy(out=alast[:, f, 0:2], in_=abuf[:, N:N + 2]), r=[b_abuf], w=[b_alast])
                            else:
                                av = abuf[:, 0:136].rearrange("p (b t) -> p b t", b=4)
                                dve(lambda e, f=f, av=av: e.tensor_copy(out=av[:, :, 0:2], in_=shalo[:, f, :].rearrange("p (b t) -> p b t", b=4)),
                                    r=[b_shalo], w=[b_abuf])
                                act(lambda e, pg=pg, av=av: e.activation(out=av[:, :, 2:34], in_=PS[pg][:, 0:128].rearrange("p (b t) -> p b t", b=4), func=AF.Copy),
                                    r=[b_PS[pg]], w=[b_abuf])
                                act(lambda e, pg=pg, fi=fi, f=f: e.activation(out=F5[fi][:, 0:128], in_=PS[pg][:, 0:128], func=AF.Identity,
                                                                             bias=cb[:, f:f + 1], scale=cw[:, 2, f:f + 1]), r=[b_PS[pg], b_lay], w=[b_F5[fi]])
                                fv = F5[fi][:, 0:128].rearrange("p (b t) -> p b t", b=4)
                                dve(lambda e, fv=fv, av=av, f=f: e.scalar_tensor_tensor(out=fv, in0=av[:, :, 1:33], scalar=cw[:, 1, f:f + 1], in1=fv,
                                                                                        op0=ALU.mult, op1=ALU.add), r=[b_abuf, b_F5[fi], b_lay], w=[b_F5[fi]])
                                dve(lambda e, fv=fv, av=av, f=f: e.scalar_tensor_tensor(out=fv, in0=av[:, :, 0:32], scalar=cw[:, 0, f:f + 1], in1=fv,
                                                                                        op0=ALU.mult, op1=ALU.add), r=[b_abuf, b_F5[fi], b_lay], w=[b_F5[fi]])
                                dve(lambda e, f=f, av=av: e.tensor_copy(out=alast[:, f, 2:10].rearrange("p (b t) -> p b t", b=4), in_=av[:, :, 32:34]),
                                    r=[b_abuf], w=[b_alast])
                            act(lambda e, fi=fi, f2=f2, N=N: e.activation(out=F5[f2][:, 0:N], in_=F5[fi][:, 0:N], func=AF.Gelu), r=[b_F5[fi]], w=[b_F5[f2]])
                            dve(lambda e, f2=f2, pg=pg, N=N, f=f, cols=cols: e.tensor_tensor(out=uT[:, f, cols], in0=F5[f2][:, 0:N], in1=PS[pg + 1][:, 0:N], op=ALU.mult),
                                r=[b_F5[f2], b_PS[pg + 1]], w=[b_uT])
                last_group = (gi == len(cfg.groups) - 1)
                if has_s or last_group:
                    for f in range(NF):
                        pi = 2 + (zps[0] % 2)
                        zps[0] += 1
                        pe(lambda e, f=f, pi=pi: e.transpose(out=PS[pi][0:10, 0:128], in_=alast[:, f, :], identity=idf[:]), r=[b_alast, b_const], w=[b_PS[pi]])
                        act(lambda e, pi=pi: e.activation(out=o10[0:10, :], in_=PS[pi][0:10, 0:128], func=AF.Copy), r=[b_PS[pi]], w=[b_o10])
                        outs = []
                        if last_group:
                            outs.append((cvp[l, :, f * 128:(f + 1) * 128], o10[0:2, :]))
                        if has_s:
                            outs.append((cvs[l, :, f * 128:(f + 1) * 128], o10[2:10, :]))
                        dma("sp", outs, r=[b_o10], key="o10")
                S.barrier()
                for cb8 in range(8):
                    slot = (cb8 % 2)
                    wd = W[:, slot * 11008:slot * 11008 + NF * 256].rearrange("p (k n) -> p k n", k=NF)
                    dma("pool", [(wd, w_dn[l, :, cb8 * 256:(cb8 + 1) * 256].rearrange("(k p) n -> p k n", p=128))], w=[b_W[slot]], key="W%d" % slot)
                    for ti, tile in enumerate(tiles):
                        pi = zps[0] % 2
                        zps[0] += 1
                        fi = nxt("F5", NF512)
                        dma("sp", [(F5[fi][:, 0:256], y_dst(tile)[:, cb8 * 256:(cb8 + 1) * 256])], w=[b_F5[fi]], key="F5_%d" % fi)

                        def mmd(e, wd=wd, ti=ti, pi=pi):
                            ins = None
                            for kf in range(NF):
                                ins = e.matmul(PS[pi][:, 0:256], lhsT=uT[:, kf, ti * 128:(ti + 1) * 128], rhs=wd[:, kf, :], start=(kf == 0), stop=(kf == NF - 1))
                            return ins
                        pe(mmd, r=[b_uT, b_W[slot]], w=[b_PS[pi]])
                        dve(lambda e, fi=fi, pi=pi: e.tensor_tensor(out=F5[fi][:, 0:256], in0=F5[fi][:, 0:256], in1=PS[pi][:, 0:256], op=ALU.add),
                            r=[b_F5[fi], b_PS[pi]], w=[b_F5[fi]])
                        dma("sp", [(y_dst(tile)[:, cb8 * 256:(cb8 + 1) * 256], F5[fi][:, 0:256])], r=[b_F5[fi]], key="F5_%d" % fi)
                S.barrier(new_epoch=(gi == len(cfg.groups) - 1))


    except _Stop:
        pass
    S.in_loop = False

    sem_keys = []
    seen = set()
    for e in ENGS:
        for o in S.q[e]:
            if o.dma:
                k = ("d", o.key)
            elif o.signal:
                k = ("e", (e, o.epoch))
            else:
                continue
            if k not in seen:
                seen.add(k)
                sem_keys.append(k)
    semmap = {}
    for i, k in enumerate(sem_keys):
        semmap[k] = es.enter_context(nc.semaphore("sm%d" % i))
    run_engine, handles = S.emit(nc, None, None)
    S._semmap = semmap
    with nc.Block() as block:
        @block.tensor
        def _(eng):
            run_engine("pe", eng)

        @block.scalar
        def _(eng):
            run_engine("act", eng)

        @block.vector
        def _(eng):
            run_engine("dve", eng)

        @block.gpsimd
        def _(eng):
            run_engine("pool", eng)

        @block.sync
        def _(eng):
            run_engine("sp", eng)
    es.close()
    return nc, S


OUT_NAMES = ["yp", "ys", "akp", "avp", "bkp", "bvp", "mkp", "mvp", "cvp", "aks", "avs", "bks", "bvs", "cvs"]
W_NAMES = ["norm_mix_g", "w_in", "a_q_norm_g", "a_k_norm_g", "a_rel_bias", "b_q_norm_g", "b_k_norm_g", "b_lam_q1", "b_lam_k1",
           "b_lam_q2", "b_lam_k2", "b_subln_g", "m_q_norm_g", "m_k_norm_g", "mem_norm_g", "w_mem_kv", "gate_b", "w_branch", "w_out",
           "norm_ffn_g", "w_ffn_up", "ffn_conv_w", "ffn_conv_b", "w_ffn_down"]


def host_consts(cfg):
    idf = np.eye(128, dtype=np.float32)
    idb = np.eye(128, dtype=np.float32).astype(ml_dtypes.bfloat16)
    jb = np.eye(128, dtype=np.float32)[::-1].copy().astype(ml_dtypes.bfloat16)
    k = np.arange(128)[:, None]
    q = np.arange(128)[None, :]
    msk = ((k // 32) == ((127 - q) // 32)).astype(np.float32)
    mskb = ((k // 32) == (q // 32)).astype(np.float32)
    half = 8
    inv_freq = np.exp(np.arange(half, dtype=np.float32) * np.float32(-2.0 * math.log(THETA) / 16)).astype(np.float32)
    pos = np.concatenate([np.arange(cfg.seq), np.tile(cfg.past + np.arange(32), 4)]).astype(np.float32)
    ang = (pos[:, None] * inv_freq[None, :]).astype(np.float32)
    cos = np.tile(np.cos(ang).astype(np.float32), (1, 8))
    sin = np.tile(np.sin(ang).astype(np.float32), (1, 8))
    return {"c_idf": idf, "c_idb": idb, "c_jb": jb, "c_msk": msk, "c_mskb": mskb,
            "c_cos": np.ascontiguousarray(cos), "c_sin": np.ascontiguousarray(sin)}


_CACHE = {}


def run(inputs, cfg, n_cores):
    key = (cfg.seq, cfg.past, cfg.depth, len(cfg.groups), cfg.dbg_skip_ffn, cfg.dbg_br, tuple(cfg.dbg_layers) if cfg.dbg_layers else None, getattr(cfg, 'dbg_stop_at', None))
    if key not in _CACHE:
        _CACHE[key] = build(cfg)
    nc, S = _CACHE[key]
    L = cfg.depth
    consts = host_consts(cfg)
    f = lambda a: np.ascontiguousarray(np.asarray(a, dtype=np.float32))
    wts = {n: f(inputs[n]) for n in W_NAMES}
    in_maps = []
    for c in range(n_cores):
        sb_ = slice(4 * c, 4 * c + 4)
        m = dict(wts)
        m.update(consts)
        m["xp"] = f(inputs["x_prompt"][c])
        m["xs"] = f(inputs["x_sample"][sb_]).reshape(128, D)
        m["cak"] = f(inputs["cache_a_k"][:, sb_]).reshape(L, 4, 512, BR)
        m["cav"] = f(inputs["cache_a_v"][:, sb_]).reshape(L, 4, 512, BR)
        m["cbk"] = f(inputs["cache_b_k"][:, sb_]).reshape(L, 4, cfg.past, BR)
        m["cbv"] = f(inputs["cache_b_v"][:, sb_]).reshape(L, 4, cfg.past, BR)
        m["cmk"] = f(inputs["cache_mem_k"][:, sb_]).reshape(L, 4, 256, BR)
        m["cmv"] = f(inputs["cache_mem_v"][:, sb_]).reshape(L, 4, 256, BR)
        m["scv"] = f(inputs["state_ffn_conv"][:, sb_]).reshape(L, 8, DFF)
        m["mem"] = f(inputs["mem_prompt"][c])
        in_maps.append(m)
    res = run_bass_kernel_spmd(nc, in_maps, core_ids=list(range(n_cores)))
    R = res.results
    g = lambda n: np.stack([np.asarray(R[c][n], dtype=np.float32) for c in range(n_cores)])
    B = n_cores
    yp = g("yp")
    ys = g("ys").reshape(B * 4, 32, D)
    AKp = cfg.akeep

    def pk(n, rows, hd):
        a = g(n)
        return np.ascontiguousarray(a.transpose(1, 0, 2, 3)).reshape(L, B, rows, BR // hd, hd)

    def sk(n, hd):
        a = g(n).reshape(B, L, 4, 32, BR)
        return np.ascontiguousarray(a.transpose(1, 0, 2, 3, 4)).reshape(L, B * 4, 32, BR // hd, hd)
    cvp = np.ascontiguousarray(g("cvp").transpose(1, 0, 2, 3))
    cvs = np.ascontiguousarray(g("cvs").reshape(B, L, 4, 2, DFF).transpose(1, 0, 2, 3, 4)).reshape(L, B * 4, 2, DFF)
    return (yp, ys, pk("akp", AKp, 128), pk("avp", AKp, 128), pk("bkp", cfg.seq, 128), pk("bvp", cfg.seq, 128),
            pk("mkp", 256, 256), pk("mvp", 256, 256), cvp, sk("aks", 128), sk("avs", 128), sk("bks", 128), sk("bvs", 128), cvs)


def kernel(**inputs):
    cfg = Cfg(seq=2048, past=4096, depth=2, ngroups=3)
    return run(inputs, cfg, 8)
```

```python
import math
import numpy as np
import ml_dtypes
import concourse.bass as bass
import concourse.mybir as mybir
from concourse.bass_utils import run_bass_kernel_spmd

F32 = mybir.dt.float32
BF16 = mybir.dt.bfloat16
AF = mybir.ActivationFunctionType
ALU = mybir.AluOpType
AX = mybir.AxisListType

D = 2048
BR = 1024
NIN = 13312
DFF = 5504
NF = 43
EPS = 1e-6
THETA = 500000.0
ENGS = ("pe", "act", "dve", "pool", "sp")
SAME_SYNC = ("act", "dve", "pool")


class Buf:
    __slots__ = ("name", "w", "rs", "rd")

    def __init__(self, name):
        self.name = name
        self.w = None
        self.rs = {}
        self.rd = []


class Op:
    __slots__ = ("eng", "fn", "waits", "signal", "sigval", "dma", "key", "dval", "epoch")


class Sched:
    def __init__(self):
        self.q = {e: [] for e in ENGS}
        self.pending = {e: [] for e in ENGS}
        self.bufs = []
        self.dma_cnt = {}
        self.last_dma = {}
        self.epoch = 0

    def buf(self, name):
        b = Buf(name)
        self.bufs.append(b)
        return b

    def bufs_n(self, name, n):
        return [self.buf("%s%d" % (name, i)) for i in range(n)]

    def add(self, eng, fn, r=(), w=(), key=None, ndma=1):
        self.nops = getattr(self, "nops", 0) + 1
        if getattr(self, "stop_at", None) is not None and self.nops > self.stop_at and getattr(self, "in_loop", False):
            raise _Stop()
        op = Op()
        op.eng = eng
        op.fn = fn
        op.signal = False
        op.sigval = 0
        op.dma = key is not None
        op.key = (key, self.epoch) if key is not None else None
        op.epoch = self.epoch
        op.waits = list(self.pending[eng])
        self.pending[eng] = []
        deps = []
        for b in r:
            if b.w is not None:
                deps.append(b.w)
        for b in w:
            if b.w is not None:
                deps.append(b.w)
            deps.extend(b.rs.values())
            deps.extend(b.rd)
        for d in deps:
            if d.dma:
                op.waits.append(d)
            elif d.eng == eng:
                if eng in SAME_SYNC:
                    d.signal = True
                    op.waits.append(d)
            else:
                d.signal = True
                op.waits.append(d)
        if op.dma:
            c = self.dma_cnt.get(op.key, 0) + 16 * ndma
            self.dma_cnt[op.key] = c
            op.dval = c
            self.last_dma[op.key] = op
        for b in r:
            if op.dma:
                b.rd.append(op)
            else:
                b.rs[eng] = op
        for b in w:
            b.w = op
            b.rs = {}
            b.rd = []
        self.q[eng].append(op)
        return op

    def barrier(self, new_epoch=False):
        lasts = []
        for e in ENGS:
            if self.q[e]:
                o = self.q[e][-1]
                if not o.dma:
                    o.signal = True
                    lasts.append(o)
        lasts.extend(self.last_dma.values())
        self.last_dma = {}
        for e in ENGS:
            for o in lasts:
                if (not o.dma) and o.eng == e and e == "pe":
                    continue
                self.pending[e].append(o)
        for b in self.bufs:
            b.w = None
            b.rs = {}
            b.rd = []
        if new_epoch:
            self.epoch += 1

    def emit(self, nc, block_ctx, sem_ctx):
        for e in ENGS:
            cnt = {}
            for o in self.q[e]:
                if o.signal and not o.dma:
                    c = cnt.get(o.epoch, 0) + 1
                    cnt[o.epoch] = c
                    o.sigval = c
                    assert c < 32000, "semaphore value too large"
        for k, v in self.dma_cnt.items():
            assert v < 32000, ("dma sem too large", k, v)
        semcache = {}

        def sem_for(kind, key):
            return self._semmap[(kind, key)]

        handles = {"pe": "tensor", "act": "scalar", "dve": "vector", "pool": "gpsimd", "sp": "sync"}
        final_keys = list(self.dma_cnt.items())

        def run_engine(e, eng):
            waited = {}
            for o in self.q[e]:
                for d in o.waits:
                    if d.dma:
                        sk = ("d", d.key)
                        val = d.dval
                    else:
                        sk = ("e", (d.eng, d.epoch))
                        val = d.sigval
                    if waited.get(sk, 0) >= val:
                        continue
                    eng.wait_ge(sem_for(*sk), val)
                    waited[sk] = val
                if o.dma:
                    o.fn(eng, sem_for("d", o.key))
                else:
                    inst = o.fn(eng)
                    if o.signal:
                        inst.then_inc(sem_for("e", (e, o.epoch)), 1)
            if e == "sp":
                for k, v in final_keys:
                    if waited.get(("d", k), 0) < v:
                        eng.wait_ge(sem_for("d", k), v)

        return run_engine, handles


class _Stop(Exception):
    pass


class Cfg:
    def __init__(self, seq=2048, past=4096, depth=2, ngroups=2, dbg_skip_ffn=False, dbg_br=(0, 1, 2), dbg_layers=None):
        self.dbg_layers = dbg_layers
        self.dbg_skip_ffn = dbg_skip_ffn
        self.dbg_br = tuple(dbg_br)
        self.seq = seq
        self.past = past
        self.depth = depth
        self.npt = seq // 128
        self.akeep = min(512, seq)
        self.nct = past // 128
        base = self.npt // ngroups
        rem = self.npt % ngroups
        sizes = [base] * ngroups
        for i in range(rem):
            sizes[(1 + i) % ngroups if ngroups > 1 else 0] += 1
        self.groups = []
        p0 = 0
        for g in range(ngroups):
            tl = [("p", p) for p in range(p0, p0 + sizes[g])]
            p0 += sizes[g]
            if g == 0:
                tl.append(("s", 0))
            self.groups.append(tl)
        self.ntmax = max(len(g) for g in self.groups)
        self.tokmax = self.ntmax * 128


def tok_blocks(nt):
    out = []
    t = 0
    while t < nt:
        n = min(4, nt - t)
        out.append((t, n))
        t += n
    return out


def build(cfg):
    nc = bass.Bass("TRN2", target_bir_lowering=False)
    L = cfg.depth
    SEQ = cfg.seq
    NPT = cfg.npt
    NCT = cfg.nct
    AK = cfg.akeep
    TOKM = cfg.tokmax
    NTM = cfg.ntmax

    def din(name, shape, dt=F32):
        return nc.dram_tensor(name, list(shape), dt, kind="ExternalInput").ap()

    def dout(name, shape, dt=F32):
        return nc.dram_tensor(name, list(shape), dt, kind="ExternalOutput").ap()

    def dscr(name, shape, dt=F32):
        return nc.dram_tensor(name, list(shape), dt, kind="Internal").ap()

    xp = din("xp", [SEQ, D])
    xs = din("xs", [128, D])
    cak = din("cak", [L, 4, 512, BR])
    cav = din("cav", [L, 4, 512, BR])
    cbk = din("cbk", [L, 4, cfg.past, BR])
    cbv = din("cbv", [L, 4, cfg.past, BR])
    cmk = din("cmk", [L, 4, 256, BR])
    cmv = din("cmv", [L, 4, 256, BR])
    scv = din("scv", [L, 8, DFF])
    mem = din("mem", [256, D])
    norm_mix_g = din("norm_mix_g", [L, D])
    w_in = din("w_in", [L, D, NIN])
    a_q_g = din("a_q_norm_g", [L, 128])
    a_k_g = din("a_k_norm_g", [L, 128])
    a_rel = din("a_rel_bias", [L, 8, 257])
    b_q_g = din("b_q_norm_g", [L, 64])
    b_k_g = din("b_k_norm_g", [L, 64])
    lq1 = din("b_lam_q1", [L, 64])
    lk1 = din("b_lam_k1", [L, 64])
    lq2 = din("b_lam_q2", [L, 64])
    lk2 = din("b_lam_k2", [L, 64])
    subln_g = din("b_subln_g", [L, 128])
    m_q_g = din("m_q_norm_g", [L, 256])
    m_k_g = din("m_k_norm_g", [L, 256])
    mem_g = din("mem_norm_g", [L, D])
    w_mem = din("w_mem_kv", [L, D, 2 * BR])
    gate_b = din("gate_b", [L, 3, D])
    w_br = din("w_branch", [L, 3, BR, D])
    w_out = din("w_out", [L, D, D])
    ffn_g = din("norm_ffn_g", [L, D])
    w_up = din("w_ffn_up", [L, D, 2 * DFF])
    conv_w = din("ffn_conv_w", [L, 3, DFF])
    conv_b = din("ffn_conv_b", [L, DFF])
    w_dn = din("w_ffn_down", [L, DFF, D])
    c_idf = din("c_idf", [128, 128])
    c_idb = din("c_idb", [128, 128], BF16)
    c_jb = din("c_jb", [128, 128], BF16)
    c_cos = din("c_cos", [SEQ + 128, 64])
    c_sin = din("c_sin", [SEQ + 128, 64])

    yp = dout("yp", [SEQ, D])
    ys = dout("ys", [128, D])
    akp = dout("akp", [L, AK, BR])
    avp = dout("avp", [L, AK, BR])
    bkp = dout("bkp", [L, SEQ, BR])
    bvp = dout("bvp", [L, SEQ, BR])
    mkp = dout("mkp", [L, 256, BR])
    mvp = dout("mvp", [L, 256, BR])
    cvp = dout("cvp", [L, 2, DFF])
    aks = dout("aks", [L, 128, BR])
    avs = dout("avs", [L, 128, BR])
    bks = dout("bks", [L, 128, BR])
    bvs = dout("bvs", [L, 128, BR])
    cvs = dout("cvs", [L, 8, DFF])
    dbgx = dout("dbgx", [128, D]) if getattr(cfg, 'dbg_dump', False) else None
    dbg_state = {'n': 0}

    s_kTa = dscr("s_kTa", [128, 8, SEQ], BF16)
    s_va = dscr("s_va", [128, NPT, 8, 129], BF16)
    s_kTb = dscr("s_kTb", [128, 8, SEQ], BF16)
    s_vb = dscr("s_vb", [128, NPT, 8, 129], BF16)
    s_mkT = dscr("s_mkT", [128, 8, 256], BF16)
    s_mv = dscr("s_mv", [128, 2, 4, 257], BF16)
    s_ext = dscr("s_ext", [8, 384], F32)

    S = Sched()
    from contextlib import ExitStack
    es = ExitStack()

    def sb(name, shape, dt):
        return es.enter_context(nc.sbuf_tensor(name, list(shape), dt))

    def pst(name, shape, dt):
        return es.enter_context(nc.psum_tensor(name, list(shape), dt))

    hT = sb("hT", [128, 16, TOKM], BF16)
    A1SZ = max(NF * TOKM, 24576, 24 * TOKM + 4 * NPT * 128 + NPT * 4 * 129)
    A1 = sb("A1", [128, A1SZ], BF16)
    W = sb("W", [128, 24576], BF16)
    a1o = 0
    oT = A1[:, a1o:a1o + 8 * TOKM].rearrange("p (k t) -> p k t", k=8)
    a1o += 8 * TOKM
    mT = A1[:, a1o:a1o + 16 * TOKM].rearrange("p (k t) -> p k t", k=16)
    a1o += 16 * TOKM
    NKT = NPT
    wkT = A1[:, a1o:a1o + 4 * NKT * 128].rearrange("p (h t) -> p h t", h=4)
    a1o += 4 * NKT * 128
    wV = A1[:, a1o:a1o + NKT * 4 * 129].rearrange("p (t h e) -> p t h e", t=NKT, h=4)
    wVflat = A1[:, a1o:a1o + NKT * 4 * 129]
    a1o += NKT * 4 * 129
    assert a1o <= A1SZ, (a1o, A1SZ)
    xts = [A1[:, i * 4096:(i + 1) * 4096].bitcast(F32) for i in range(2)]
    gtn = A1[:, 8192:12288].bitcast(F32)
    hbs = [A1[:, 12288 + i * 2048:12288 + (i + 1) * 2048] for i in range(2)]
    uT = A1[:, 0:NF * TOKM].rearrange("p (f t) -> p f t", f=NF)

    NF512 = 6
    F5 = [sb("F5_%d" % i, [128, 512], F32) for i in range(NF512)]
    B5 = [sb("B5_%d" % i, [128, 512], BF16) for i in range(4)]
    idf = sb("idf", [128, 128], F32)
    idb = sb("idb", [128, 128], BF16)
    jb = sb("jb", [128, 128], BF16)
    hank = sb("hank", [128, 4, 2, 128], F32)
    c0t = sb("c0t", [128, 8], F32)
    gka = sb("gka", [128, 128], F32)
    gqa = sb("gqa", [128, 1], F32)
    gqb = sb("gqb", [128, 64], F32)
    gkb = sb("gkb", [128, 64], F32)
    gkm = sb("gkm", [128, 256], F32)
    gqm = sb("gqm", [128, 2], F32)
    gsub = sb("gsub", [128, 128], F32)
    gtb = sb("gtb", [128, 3, 16], F32)
    cw = sb("cw", [128, 3, NF], F32)
    cb = sb("cb", [128, NF], F32)
    lamv = sb("lamv", [128, 4, 64], F32)
    lams = sb("lams", [128, 8], F32)
    st = [sb("st%d" % i, [128, 16], F32) for i in range(4)]
    cst = [sb("cs%d" % i, [128, 64], F32) for i in range(2)]
    snt = [sb("sn%d" % i, [128, 64], F32) for i in range(2)]
    ET = [sb("ET%d" % i, [128, 256], BF16) for i in range(3)]
    EPP = [sb("EPP%d" % i, [128, 256], BF16) for i in range(4)]
    EP = [EPP[i][:, 0:128] for i in range(4)]
    SP_ = [sb("SPf%d" % i, [128, 128], F32) for i in range(2)]
    ON = [sb("ON%d" % i, [128, 256], F32) for i in range(2)]
    OB = [sb("OB%d" % i, [128, 256], BF16) for i in range(2)]
    qTt = [sb("qT%d" % i, [128, 4, 128], BF16) for i in range(2)]
    qTz = [sb("qTz%d" % i, [128, 4, 128], BF16) for i in range(2)]
    cV = [sb("cV%d" % i, [128, 4, 129], BF16) for i in range(2)]
    cVm = sb("cVm", [128, 2, 4, 257], BF16)
    mkT = sb("mkT", [128, 8, 256], BF16)
    skT = sb("skT", [128, 4, 128], BF16)
    sVn = sb("sVn", [128, 4, 129], BF16)
    acar = sb("acar", [128, NF, 2], F32)
    alast = sb("alast", [128, NF, 10], F32)
    abuf = sb("abuf", [128, 516], F32)
    shalo = sb("shalo", [128, NF, 8], F32)
    s8 = sb("s8", [8, 128], F32)
    o10 = sb("o10", [16, 128], F32)
    epsc = sb("epsc", [128, 1], F32)

    PS = [pst("ps%d" % i, [128, 512], F32) for i in range(8)]

    b_hT = S.buf("hT")
    b_oT = S.buf("oT")
    b_mT = S.buf("mT")
    b_wkT = S.buf("wkT")
    b_wV = S.buf("wV")
    b_uT = S.buf("uT")
    b_xt = S.bufs_n("xt", 2)
    b_gtn = S.buf("gtn")
    b_hb = S.bufs_n("hb", 2)
    b_F5 = S.bufs_n("F5", NF512)
    b_B5 = S.bufs_n("B5", 4)
    b_const = S.buf("const")
    b_lay = S.buf("layerparams")
    b_hank = S.buf("hank")
    b_st = S.bufs_n("st", 4)
    b_cs = S.bufs_n("cs", 2)
    b_ET = S.bufs_n("ET", 3)
    b_EP = S.bufs_n("EP", 4)
    b_EPb = S.bufs_n("EPb", 4)
    b_SPf = S.bufs_n("SPf", 2)
    b_ON = S.bufs_n("ON", 2)
    b_OB = S.bufs_n("OB", 2)
    b_qT = S.bufs_n("qT", 2)
    b_cV = S.bufs_n("cV", 2)
    b_cVm = S.buf("cVm")
    b_mkT = S.buf("mkT")
    b_skT = S.buf("skT")
    b_sVn = S.buf("sVn")
    b_acar = S.buf("acar")
    b_alast = S.buf("alast")
    b_abuf = S.buf("abuf")
    b_shalo = S.buf("shalo")
    b_s8 = S.buf("s8")
    b_o10 = S.buf("o10")
    b_PS = S.bufs_n("PS", 8)
    b_W = S.bufs_n("W", 3)
    b_dram = S.buf("dram_scr")
    b_dext = S.buf("dram_ext")

    rr = {"F5": 0, "B5": 0, "st": 0, "ET": 0, "SPf": 0, "ON": 0, "OB": 0, "qT": 0, "cV": 0, "cs": 0}

    def nxt(name, n):
        i = rr[name]
        rr[name] = (i + 1) % n
        return i

    def act(fn, r=(), w=()):
        return S.add("act", fn, r, w)

    def dve(fn, r=(), w=()):
        return S.add("dve", fn, r, w)

    def pe(fn, r=(), w=()):
        return S.add("pe", fn, r, w)

    def dma(q, outs_ins, r=(), w=(), key=None, **kw):
        pairs = list(outs_ins)

        def fn(eng, sem, pairs=pairs, kw=kw):
            for (o, i) in pairs:
                eng.dma_start(out=o, in_=i, **kw).then_inc(sem, 16)
        assert key is not None
        return S.add(q, fn, r, w, key=key, ndma=len(pairs))

    def A_exp(out, in_, bias=None, scale=1.0):
        if bias is None:
            return lambda e: e.activation(out=out, in_=in_, func=AF.Exp, scale=scale)
        return lambda e: e.activation(out=out, in_=in_, func=AF.Exp, bias=bias, scale=scale)

    dma("sp", [(idf[:], c_idf[:, :]), (idb[:], c_idb[:, :]), (jb[:], c_jb[:, :])], w=[b_const], key="const")
    dve(lambda e: e.memset(epsc[:], EPS), w=[b_const])
    for i in range(4):
        dve(lambda e, i=i: e.memset(EPP[i][:], 0.0), w=[b_EP[i]])
    dve(lambda e: e.memset(acar[:], 0.0), w=[b_acar])
    dve(lambda e: e.memset(alast[:], 0.0), w=[b_alast])

    LFIRST = cfg.dbg_layers[0] if cfg.dbg_layers is not None else 0

    def x_src(l, tile):
        if tile[0] == "p":
            base = xp if l == LFIRST else yp
            return base[tile[1] * 128:(tile[1] + 1) * 128, :]
        return (xs if l == LFIRST else ys)[:, :]

    def y_dst(tile):
        if tile[0] == "p":
            return yp[tile[1] * 128:(tile[1] + 1) * 128, :]
        return ys[:, :]

    def pos_row(tile):
        return tile[1] * 128 if tile[0] == "p" else SEQ

    def norm_stage(tiles, src_fn, g_dram_row, dstT, b_dst, psbase=0):
        dma("sp", [(gtn, g_dram_row.partition_broadcast(128))], w=[b_gtn], key="gtn")
        for ti, tile in enumerate(tiles):
            i = ti % 2
            dma("sp", [(xts[i], src_fn(tile))], w=[b_xt[i]], key="xt%d" % i)
            if dbgx is not None and dstT is hT:
                dbg_state['n'] += 1
                if dbg_state['n'] == cfg.dbg_dump:
                    dma("sp", [(dbgx[:, :], xts[i])], r=[b_xt[i]], key="dbgx")
                    if getattr(cfg, 'dbg_stop', False):
                        raise _Stop()
            si = nxt("st", 4)
            act(lambda e, i=i, si=si: e.activation(out=hbs[i], in_=xts[i], func=AF.Square, accum_out=st[si][:, 0:1]),
                r=[b_xt[i]], w=[b_hb[i], b_st[si]])
            act(lambda e, si=si: e.activation(out=st[si][:, 1:2], in_=st[si][:, 0:1], func=AF.Ln, bias=epsc[:], scale=1.0 / D),
                r=[b_st[si], b_const], w=[b_st[si]])
            act(lambda e, si=si: e.activation(out=st[si][:, 2:3], in_=st[si][:, 1:2], func=AF.Exp, scale=-0.5),
                r=[b_st[si]], w=[b_st[si]])
            dve(lambda e, i=i, si=si: e.scalar_tensor_tensor(out=hbs[i], in0=xts[i], scalar=st[si][:, 2:3], in1=gtn,
                                                            op0=ALU.mult, op1=ALU.mult),
                r=[b_xt[i], b_st[si], b_gtn], w=[b_hb[i]])
            for half in range(2):
                pi = psbase + (ti % 2) * 2 + half
                pv = PS[pi][:].bitcast(BF16)

                def tr(e, i=i, half=half, pv=pv):
                    ins = None
                    for k in range(8):
                        kc = half * 8 + k
                        ins = e.transpose(out=pv[:, k * 128:(k + 1) * 128], in_=hbs[i][:, kc * 128:(kc + 1) * 128], identity=idb[:])
                    return ins
                pe(tr, r=[b_hb[i], b_const], w=[b_PS[pi]])
                act(lambda e, half=half, pv=pv, ti=ti: e.activation(
                    out=dstT[:, half * 8:(half + 1) * 8, ti * 128:(ti + 1) * 128],
                    in_=pv[:, 0:1024].rearrange("p (k t) -> p k t", k=8), func=AF.Copy),
                    r=[b_PS[pi]], w=[b_dst])

    def load_w_block(slot, src_ap_3d, ncols, q="pool"):
        base = slot * 8192 if ncols == 512 else None
        dst = W[:, base:base + 16 * ncols].rearrange("p (k n) -> p k n", k=16)
        dma(q, [(dst, src_ap_3d.rearrange("(k p) n -> p k n", p=128))], w=[b_W[slot]], key="W%d" % slot)
        return dst

    wslot = [0]

    def next_slot(n=3):
        s = wslot[0]
        wslot[0] = (s + 1) % n
        return s

    zps = [0]

    def dense_tm(slot, wv, lhs_src, ti, ps_i, nk=16):
        def fn(e):
            ins = None
            for kc in range(nk):
                ins = e.matmul(PS[ps_i][:, :], lhsT=lhs_src[:, kc, ti * 128:(ti + 1) * 128], rhs=wv[:, kc, :],
                               start=(kc == 0), stop=(kc == nk - 1))
            return ins
        return fn


    rt = [sb("rt%d" % i, [128, 64], F32) for i in range(4)]
    b_rt = S.buf("rt")
    smV = [sb("smV%d" % i, [128, 2, 257], BF16) for i in range(2)]
    b_smV = S.bufs_n("smV", 2)
    mskt = sb("mskt", [128, 128], F32)
    c_msk = din("c_msk", [128, 128])
    dma("sp", [(mskt[:], c_msk[:, :])], w=[b_const], key="const")
    for i in range(2):
        dve(lambda e, i=i: e.memset(cV[i][:], 1.0), w=[b_cV[i]])
        dve(lambda e, i=i: e.memset(smV[i][:], 1.0), w=[b_smV[i]])
    ctr = {"s": 0, "o": 0, "t": 0, "ep": 0, "smv": 0}

    def rope(fi, tile):
        ci = nxt("cs", 2)
        r0 = pos_row(tile)
        dma("sp", [(cst[ci][:], c_cos[r0:r0 + 128, :]), (snt[ci][:], c_sin[r0:r0 + 128, :])], w=[b_cs[ci]], key="cs%d" % ci)
        v = F5[fi][:].rearrange("p (s d) -> p s d", s=8)
        x1 = v[:, :, 0:8]
        x2 = v[:, :, 8:16]
        C = cst[ci][:].rearrange("p (s d) -> p s d", s=8)
        Sn = snt[ci][:].rearrange("p (s d) -> p s d", s=8)
        T = [rt[i][:].rearrange("p (s d) -> p s d", s=8) for i in range(4)]
        dve(lambda e: e.tensor_tensor(out=T[0], in0=x1, in1=C, op=ALU.mult), r=[b_F5[fi], b_cs[ci]], w=[b_rt])
        dve(lambda e: e.tensor_tensor(out=T[1], in0=x2, in1=Sn, op=ALU.mult), r=[b_F5[fi], b_cs[ci]], w=[b_rt])
        dve(lambda e: e.tensor_tensor(out=T[2], in0=x2, in1=C, op=ALU.mult), r=[b_F5[fi], b_cs[ci]], w=[b_rt])
        dve(lambda e: e.tensor_tensor(out=T[3], in0=x1, in1=Sn, op=ALU.mult), r=[b_F5[fi], b_cs[ci]], w=[b_rt])
        dve(lambda e: e.tensor_tensor(out=x1, in0=T[0], in1=T[1], op=ALU.subtract), r=[b_rt], w=[b_F5[fi]])
        dve(lambda e: e.tensor_tensor(out=x2, in0=T[2], in1=T[3], op=ALU.add), r=[b_rt], w=[b_F5[fi]])

    def s_bank():
        i = (4, 5, 3)[ctr["s"] % 3]
        ctr["s"] += 1
        return i

    def t_bank():
        return 2

    def cache_group(kd, vd, nj):
        fi = nxt("F5", NF512)
        dma("sp", [(F5[fi][:, 0:nj * 128].rearrange("p (j d) -> p j d", j=nj), kd)], w=[b_F5[fi]], key="F5_%d" % fi)
        ci = nxt("cV", 2)
        dma("pool", [(cV[ci][:, 0:nj, 0:128], vd)], w=[b_cV[ci]], key="cV%d" % ci)
        pt = t_bank()

        def tr(e, fi=fi, pt=pt, nj=nj):
            ins = None
            for k in range(nj):
                ins = e.transpose(out=PS[pt][:, k * 128:(k + 1) * 128], in_=F5[fi][:, k * 128:(k + 1) * 128], identity=idf[:])
            return ins
        pe(tr, r=[b_F5[fi], b_const], w=[b_PS[pt]])
        bi = nxt("B5", 4)
        act(lambda e, bi=bi, pt=pt, nj=nj: e.activation(out=B5[bi][:, 0:nj * 128], in_=PS[pt][:, 0:nj * 128], func=AF.Copy), r=[b_PS[pt]], w=[b_B5[bi]])
        return B5[bi][:, 0:nj * 128].rearrange("p (j d) -> p j d", j=nj), b_B5[bi], cV[ci], b_cV[ci]

    def qrange(b, rev):
        return slice(96 - 32 * b, 128 - 32 * b) if rev else slice(32 * b, 32 * b + 32)

    def finish_to_oT(ob_ap, b_ob, chunk, ti, rev):
        pt = t_bank()
        if rev:
            pe(lambda e: e.matmul(PS[pt][:, 0:128], lhsT=ob_ap, rhs=jb[:], start=True, stop=True), r=[b_ob, b_const], w=[b_PS[pt]])
            act(lambda e: e.activation(out=oT[:, chunk, ti * 128:(ti + 1) * 128], in_=PS[pt][:, 0:128], func=AF.Copy), r=[b_PS[pt]], w=[b_oT])
        else:
            pvb = PS[pt][:].bitcast(BF16)
            pe(lambda e: e.transpose(out=pvb[:, 0:128], in_=ob_ap, identity=idb[:]), r=[b_ob, b_const], w=[b_PS[pt]])
            act(lambda e: e.activation(out=oT[:, chunk, ti * 128:(ti + 1) * 128], in_=pvb[:, 0:128], func=AF.Copy), r=[b_PS[pt]], w=[b_oT])


    def run_stages(stages, after_first=None):
        n = len(stages)
        if n == 0:
            if after_first:
                after_first()
            return

        def do_st(s):
            if s.get("pre"):
                s["pre"]()
            s["st"]()
        skew = 1 if any(s.get("pre") for s in stages) else 2
        do_st(stages[0])
        if after_first:
            after_first()
        for k in range(1, skew):
            if k < n:
                do_st(stages[k])
        for j in range(n):
            if j + skew < n:
                do_st(stages[j + skew])
            stages[j]["mid"]()
            stages[j]["pv"]()

    def split_cache_loads(kd, vd, nj):
        fi = nxt("F5", NF512)
        dma("sp", [(F5[fi][:, 0:nj * 128].rearrange("p (j d) -> p j d", j=nj), kd)], w=[b_F5[fi]], key="F5_%d" % fi)
        ci = nxt("cV", 2)
        dma("pool", [(cV[ci][:, 0:nj, 0:128], vd)], w=[b_cV[ci]], key="cV%d" % ci)
        return {"fi": fi, "ci": ci, "nj": nj}

    def cache_xpose(ld):
        fi, nj = ld["fi"], ld["nj"]
        pt = t_bank()

        def tr(e, fi=fi, pt=pt, nj=nj):
            ins = None
            for k in range(nj):
                ins = e.transpose(out=PS[pt][:, k * 128:(k + 1) * 128], in_=F5[fi][:, k * 128:(k + 1) * 128], identity=idf[:])
            return ins
        pe(tr, r=[b_F5[fi], b_const], w=[b_PS[pt]])
        bi = nxt("B5", 4)
        act(lambda e, bi=bi, pt=pt, nj=nj: e.activation(out=B5[bi][:, 0:nj * 128], in_=PS[pt][:, 0:nj * 128], func=AF.Copy), r=[b_PS[pt]], w=[b_B5[bi]])
        ld["kTc"] = B5[bi][:, 0:nj * 128].rearrange("p (j d) -> p j d", j=nj)
        ld["bk"] = b_B5[bi]
        ld["vc"] = cV[ld["ci"]]
        ld["bv"] = b_cV[ld["ci"]]

    def attn_a(l, hg, ti, tile, qi):
        pend = [None]
        for hh in range(4):
            h = hg * 4 + hh
            po = 6 + (ctr["o"] % 2)
            ctr["o"] += 1
            q = qTt[qi]
            stages = []
            if tile[0] == "p":
                t = tile[1]
                js = list(range(max(0, t - 4), t + 1))
                for idx, j in enumerate(js):
                    dd = t - j
                    s = {}

                    def f_st(s=s, j=j, hh=hh):
                        ps = s_bank()
                        s["ps"] = ps
                        pe(lambda e, ps=ps: e.matmul(PS[ps][:, 0:128], lhsT=wkT[:, hh, j * 128:(j + 1) * 128], rhs=q[:, hh, :], start=True, stop=True),
                           r=[b_wkT, b_qT[qi]], w=[b_PS[ps]])

                    def f_mid(s=s, dd=dd, hh=hh, h=h):
                        ps = s["ps"]
                        ei = nxt("ET", 3)
                        s["ei"] = ei
                        if dd >= 2:
                            act(A_exp(ET[ei][:, 0:128], PS[ps][:, 0:128], bias=c0t[:, h:h + 1]), r=[b_PS[ps], b_lay], w=[b_ET[ei]])
                            if dd == 4:
                                dve(lambda e, ei=ei: e.memset(ET[ei][0:64, 0:64], 0.0), w=[b_ET[ei]])
                        else:
                            sp_ = nxt("SPf", 2)
                            dve(lambda e, sp_=sp_, ps=ps: e.tensor_tensor(out=SP_[sp_][:], in0=PS[ps][:, 0:128], in1=hank[:, hh, dd, :], op=ALU.add),
                                r=[b_PS[ps], b_hank], w=[b_SPf[sp_]])
                            act(A_exp(ET[ei][:, 0:128], SP_[sp_][:]), r=[b_SPf[sp_]], w=[b_ET[ei]])
                            if dd == 0:
                                dve(lambda e, ei=ei: e.memset(ET[ei][64:128, 64:128], 0.0), w=[b_ET[ei]])

                    def f_pv(s=s, j=j, hh=hh, idx=idx, n=len(js), po=po):
                        ei = s["ei"]
                        pe(lambda e, ei=ei: e.matmul(PS[po][:, 0:129], lhsT=ET[ei][:, 0:128], rhs=wV[:, j, hh, :], start=(idx == 0), stop=(idx == n - 1)),
                           r=[b_ET[ei], b_wV], w=[b_PS[po]])
                    s.update(st=f_st, mid=f_mid, pv=f_pv)
                    stages.append(s)
            else:
                lds = [None] * 4

                def mk_ld(b, h=h):
                    return split_cache_loads(cak[l, b, :, h * 128:(h + 1) * 128].rearrange("(j p) d -> p j d", p=128),
                                             cav[l, b, :, h * 128:(h + 1) * 128].rearrange("(j p) d -> p j d", p=128), 4)
                lds[0] = mk_ld(0)
                cache_xpose(lds[0])
                cnt = [0]
                for b in range(4):
                    qr = qrange(b, True)
                    for jj in range(4):
                        s = {}
                        pre = None
                        if jj == 1 and b + 1 < 4:
                            def pre(b=b):
                                lds[b + 1] = mk_ld(b + 1)
                        elif jj == 3 and b + 1 < 4:
                            def pre(b=b):
                                cache_xpose(lds[b + 1])

                        def f_st(s=s, b=b, jj=jj, qr=qr, hh=hh):
                            ps = s_bank()
                            s["ps"] = ps
                            ld = lds[b]
                            pe(lambda e, ps=ps, kTc=ld["kTc"]: e.matmul(PS[ps][:, 0:32], lhsT=kTc[:, jj, :], rhs=q[:, hh, qr], start=True, stop=True),
                               r=[ld["bk"], b_qT[qi]], w=[b_PS[ps]])

                        def f_mid(s=s, jj=jj, qr=qr, hh=hh, h=h):
                            ps = s["ps"]
                            ep = ctr["ep"] % 4
                            ctr["ep"] += 1
                            s["ep"] = ep
                            dve(lambda e, ep=ep: e.memset(EP[ep][:], 0.0), w=[b_EP[ep]])
                            if jj < 3:
                                act(A_exp(EP[ep][:, qr], PS[ps][:, 0:32], bias=c0t[:, h:h + 1]), r=[b_PS[ps], b_lay], w=[b_EP[ep]])
                            else:
                                sp_ = nxt("SPf", 2)
                                dve(lambda e, sp_=sp_, ps=ps: e.tensor_tensor(out=SP_[sp_][:, 0:32], in0=PS[ps][:, 0:32], in1=hank[:, hh, 1, 96:128], op=ALU.add),
                                    r=[b_PS[ps], b_hank], w=[b_SPf[sp_]])
                                act(A_exp(EP[ep][:, qr], SP_[sp_][:, 0:32]), r=[b_SPf[sp_]], w=[b_EP[ep]])

                        def f_pv(s=s, b=b, jj=jj, first=(b == 0 and jj == 0), po=po):
                            ep = s["ep"]
                            ld = lds[b]
                            pe(lambda e, ep=ep, vc=ld["vc"]: e.matmul(PS[po][:, 0:129], lhsT=EP[ep][:, :], rhs=vc[:, jj, :], start=first, stop=False),
                               r=[b_EP[ep], ld["bv"]], w=[b_PS[po]])
                        s.update(st=f_st, mid=f_mid, pv=f_pv, pre=pre)
                        stages.append(s)
                s = {}

                def f_st(s=s, hh=hh):
                    ps = s_bank()
                    s["ps"] = ps
                    pe(lambda e, ps=ps: e.matmul(PS[ps][:, 0:128], lhsT=skT[:, hh, :], rhs=q[:, hh, :], start=True, stop=True),
                       r=[b_skT, b_qT[qi]], w=[b_PS[ps]])

                def f_mid(s=s, hh=hh):
                    ps = s["ps"]
                    sp_ = nxt("SPf", 2)
                    dve(lambda e, sp_=sp_, ps=ps: e.tensor_tensor(out=SP_[sp_][:], in0=PS[ps][:, 0:128], in1=hank[:, hh, 0, :], op=ALU.add),
                        r=[b_PS[ps], b_hank], w=[b_SPf[sp_]])
                    act(A_exp(SP_[sp_][:], SP_[sp_][:]), r=[b_SPf[sp_]], w=[b_SPf[sp_]])
                    ei = nxt("ET", 3)
                    s["ei"] = ei
                    dve(lambda e, sp_=sp_, ei=ei: e.tensor_tensor(out=ET[ei][:, 0:128], in0=SP_[sp_][:], in1=mskt[:], op=ALU.mult), r=[b_SPf[sp_], b_const], w=[b_ET[ei]])

                def f_pv(s=s, hh=hh, po=po):
                    ei = s["ei"]
                    pe(lambda e, ei=ei: e.matmul(PS[po][:, 0:129], lhsT=ET[ei][:, 0:128], rhs=sVn[:, hh, :], start=False, stop=True),
                       r=[b_ET[ei], b_sVn], w=[b_PS[po]])
                s.update(st=f_st, mid=f_mid, pv=f_pv)
                stages.append(s)
            run_stages(stages, after_first=pend[0])
            si = nxt("st", 4)
            oi = nxt("OB", 2)
            dve(lambda e, si=si, po=po: e.reciprocal(out=st[si][:, 0:1], in_=PS[po][:, 128:129]), r=[b_PS[po]], w=[b_st[si]])
            dve(lambda e, si=si, po=po, oi=oi: e.tensor_scalar(out=OB[oi][:, 0:128], in0=PS[po][:, 0:128], scalar1=st[si][:, 0:1], scalar2=None, op0=ALU.mult),
                r=[b_PS[po], b_st[si]], w=[b_OB[oi]])
            pend[0] = (lambda oi=oi, hh=hh: finish_to_oT(OB[oi][:, 0:128], b_OB[oi], hg * 4 + hh, ti, True))
        pend[0]()

    def attn_b(l, hg, ti, tile, qi, neg_lam, lam_init):
        q = qTt[qi]
        qz = qTz[qi]
        pend = [None]
        for hh in range(4):
            h = hg * 4 + hh
            p0, p1 = 6, 7
            stages = []

            def st_mm(kt_ap, qcols, N, ps, hh=hh):
                def fn(e):
                    e.matmul(PS[ps][:, 0:N], lhsT=kt_ap, rhs=q[:, hh, qcols], start=True, stop=True)
                    return e.matmul(PS[ps][:, 128:128 + N], lhsT=kt_ap, rhs=qz[:, hh, qcols], start=True, stop=True)
                return fn

            def pv_mm(e0, e1, v_ap, first, last):
                def fn(e):
                    e.matmul(PS[p0][:, 0:129], lhsT=e0, rhs=v_ap, start=first, stop=last)
                    return e.matmul(PS[p1][:, 0:129], lhsT=e1, rhs=v_ap, start=first, stop=last)
                return fn
            if tile[0] == "p":
                t = tile[1]
                for j in range(t + 1):
                    s = {}

                    def f_st(s=s, j=j, hh=hh, st_mm=st_mm):
                        ps = s_bank()
                        s["ps"] = ps
                        pe(st_mm(wkT[:, hh, j * 128:(j + 1) * 128], slice(0, 128), 128, ps), r=[b_wkT, b_qT[qi]], w=[b_PS[ps]])

                    def f_mid(s=s, j=j, t=t):
                        ps = s["ps"]
                        ei = nxt("ET", 3)
                        s["ei"] = ei
                        act(A_exp(ET[ei][:, 0:256], PS[ps][:, 0:256]), r=[b_PS[ps]], w=[b_ET[ei]])
                        if j == t:
                            dve(lambda e, ei=ei: e.memset(ET[ei][64:128, :].rearrange("p (m q) -> p m q", m=2)[:, :, 0:64], 0.0), w=[b_ET[ei]])

                    def f_pv(s=s, j=j, t=t, hh=hh, pv_mm=pv_mm):
                        ei = s["ei"]
                        pe(pv_mm(ET[ei][:, 0:128], ET[ei][:, 128:256], wV[:, j, hh, :], j == 0, j == t), r=[b_ET[ei], b_wV], w=[b_PS[p0], b_PS[p1]])
                    s.update(st=f_st, mid=f_mid, pv=f_pv)
                    stages.append(s)
            else:
                groups = [(b, jg) for b in range(4) for jg in range(NCT // 4)]
                lds = [None] * len(groups)

                def mk_ld(g, h=h):
                    b, jg = groups[g]
                    rows = slice(jg * 512, (jg + 1) * 512)
                    return split_cache_loads(cbk[l, b, rows, h * 128:(h + 1) * 128].rearrange("(j p) d -> p j d", p=128),
                                             cbv[l, b, rows, h * 128:(h + 1) * 128].rearrange("(j p) d -> p j d", p=128), 4)
                lds[0] = mk_ld(0)
                cache_xpose(lds[0])
                for g, (b, jg) in enumerate(groups):
                    qr = qrange(b, False)
                    for jj in range(4):
                        s = {}
                        pre = None
                        if jj == 1 and g + 1 < len(groups):
                            def pre(g=g):
                                lds[g + 1] = mk_ld(g + 1)
                        elif jj == 3 and g + 1 < len(groups):
                            def pre(g=g):
                                cache_xpose(lds[g + 1])

                        def f_st(s=s, g=g, jj=jj, qr=qr, st_mm=st_mm):
                            ps = s_bank()
                            s["ps"] = ps
                            ld = lds[g]
                            pe(st_mm(ld["kTc"][:, jj, :], qr, 32, ps), r=[ld["bk"], b_qT[qi]], w=[b_PS[ps]])

                        def f_mid(s=s, qr=qr):
                            ps = s["ps"]
                            ep = ctr["ep"] % 4
                            ctr["ep"] += 1
                            s["ep"] = ep
                            dve(lambda e, ep=ep: e.memset(EPP[ep][:], 0.0), w=[b_EP[ep]])
                            act(A_exp(EPP[ep][:].rearrange("p (m q) -> p m q", m=2)[:, :, qr],
                                      PS[ps][:, 0:256].rearrange("p (m q) -> p m q", m=2)[:, :, 0:32]), r=[b_PS[ps]], w=[b_EP[ep]])

                        def f_pv(s=s, g=g, jj=jj, first=(g == 0 and jj == 0), pv_mm=pv_mm):
                            ep = s["ep"]
                            ld = lds[g]
                            pe(pv_mm(EPP[ep][:, 0:128], EPP[ep][:, 128:256], ld["vc"][:, jj, :], first, False), r=[b_EP[ep], ld["bv"]], w=[b_PS[p0], b_PS[p1]])
                        s.update(st=f_st, mid=f_mid, pv=f_pv, pre=pre)
                        stages.append(s)
                s = {}

                def f_st(s=s, hh=hh, st_mm=st_mm):
                    ps = s_bank()
                    s["ps"] = ps
                    pe(st_mm(skT[:, hh, :], slice(0, 128), 128, ps), r=[b_skT, b_qT[qi]], w=[b_PS[ps]])

                def f_mid(s=s):
                    ps = s["ps"]
                    fi = nxt("F5", NF512)
                    act(A_exp(F5[fi][:, 0:256], PS[ps][:, 0:256]), r=[b_PS[ps]], w=[b_F5[fi]])
                    ei = nxt("ET", 3)
                    s["ei"] = ei
                    dve(lambda e, fi=fi, ei=ei: e.tensor_tensor(out=ET[ei][:, 0:256].rearrange("p (m q) -> p m q", m=2),
                                                                in0=F5[fi][:, 0:256].rearrange("p (m q) -> p m q", m=2),
                                                                in1=mskb3, op=ALU.mult), r=[b_F5[fi], b_const], w=[b_ET[ei]])

                def f_pv(s=s, hh=hh, pv_mm=pv_mm):
                    ei = s["ei"]
                    pe(pv_mm(ET[ei][:, 0:128], ET[ei][:, 128:256], sVn[:, hh, :], False, True), r=[b_ET[ei], b_sVn], w=[b_PS[p0], b_PS[p1]])
                s.update(st=f_st, mid=f_mid, pv=f_pv)
                stages.append(s)
            run_stages(stages, after_first=pend[0])
            si = nxt("st", 4)
            oi = nxt("ON", 2)
            ob = nxt("OB", 2)
            f2 = nxt("F5", NF512)
            dve(lambda e, si=si: e.reciprocal(out=st[si][:, 0:1], in_=PS[p0][:, 128:129]), r=[b_PS[p0]], w=[b_st[si]])
            dve(lambda e, si=si: e.reciprocal(out=st[si][:, 1:2], in_=PS[p1][:, 128:129]), r=[b_PS[p1]], w=[b_st[si]])
            dve(lambda e, si=si: e.tensor_tensor(out=st[si][:, 2:3], in0=st[si][:, 1:2], in1=neg_lam, op=ALU.mult), r=[b_st[si], b_lay], w=[b_st[si]])
            dve(lambda e, si=si, oi=oi: e.tensor_scalar(out=ON[oi][:, 0:128], in0=PS[p0][:, 0:128], scalar1=st[si][:, 0:1], scalar2=None, op0=ALU.mult),
                r=[b_PS[p0], b_st[si]], w=[b_ON[oi]])
            dve(lambda e, si=si, oi=oi: e.scalar_tensor_tensor(out=ON[oi][:, 0:128], in0=PS[p1][:, 0:128], scalar=st[si][:, 2:3], in1=ON[oi][:, 0:128],
                                                               op0=ALU.mult, op1=ALU.add), r=[b_PS[p1], b_st[si], b_ON[oi]], w=[b_ON[oi]])
            dve(lambda e, oi=oi, f2=f2: e.tensor_tensor(out=F5[f2][:, 0:128], in0=ON[oi][:, 0:128], in1=ON[oi][:, 0:128], op=ALU.mult), r=[b_ON[oi]], w=[b_F5[f2]])
            dve(lambda e, si=si, f2=f2: e.tensor_reduce(out=st[si][:, 3:4], in_=F5[f2][:, 0:128], axis=AX.X, op=ALU.add), r=[b_F5[f2]], w=[b_st[si]])
            act(lambda e, si=si: e.activation(out=st[si][:, 4:5], in_=st[si][:, 3:4], func=AF.Ln, bias=epsc[:], scale=1.0 / 128), r=[b_st[si], b_const], w=[b_st[si]])
            act(lambda e, si=si: e.activation(out=st[si][:, 5:6], in_=st[si][:, 4:5], func=AF.Exp, scale=-0.5), r=[b_st[si]], w=[b_st[si]])
            dve(lambda e, si=si, oi=oi, ob=ob: e.scalar_tensor_tensor(out=OB[ob][:, 0:128], in0=ON[oi][:, 0:128], scalar=st[si][:, 5:6], in1=gsub[:],
                                                                      op0=ALU.mult, op1=ALU.mult), r=[b_ON[oi], b_st[si], b_lay], w=[b_OB[ob]])
            pend[0] = (lambda ob=ob, hh=hh: finish_to_oT(OB[ob][:, 0:128], b_OB[ob], hg * 4 + hh, ti, False))
        pend[0]()

    def attn_m(l, hg, ti, tile, qi):
        q = qTt[qi]
        pend = [None]
        for hl in range(2):
            h = hg * 2 + hl
            po = 6 + (ctr["o"] % 2)
            ctr["o"] += 1
            stages = []
            if tile[0] == "p":
                for jj in range(2):
                    s = {}

                    def f_st(s=s, jj=jj, h=h, hl=hl):
                        ps = s_bank()
                        s["ps"] = ps

                        def smm(e, ps=ps):
                            e.matmul(PS[ps][:, 0:128], lhsT=mkT[:, h * 2, jj * 128:(jj + 1) * 128], rhs=q[:, hl * 2, :], start=True, stop=False)
                            return e.matmul(PS[ps][:, 0:128], lhsT=mkT[:, h * 2 + 1, jj * 128:(jj + 1) * 128], rhs=q[:, hl * 2 + 1, :], start=False, stop=True)
                        pe(smm, r=[b_mkT, b_qT[qi]], w=[b_PS[ps]])

                    def f_mid(s=s):
                        ps = s["ps"]
                        ei = nxt("ET", 3)
                        s["ei"] = ei
                        act(A_exp(ET[ei][:, 0:128], PS[ps][:, 0:128]), r=[b_PS[ps]], w=[b_ET[ei]])

                    def f_pv(s=s, jj=jj, h=h, po=po):
                        ei = s["ei"]
                        pe(lambda e, ei=ei: e.matmul(PS[po][:, 0:257], lhsT=ET[ei][:, 0:128], rhs=cVm[:, jj, h, :], start=(jj == 0), stop=(jj == 1)),
                           r=[b_ET[ei], b_cVm], w=[b_PS[po]])
                    s.update(st=f_st, mid=f_mid, pv=f_pv)
                    stages.append(s)
            else:
                lds = [None] * 4

                def mk_ld(b, h=h):
                    fi = nxt("F5", NF512)
                    dma("sp", [(F5[fi][:].rearrange("p (j d) -> p j d", j=2), cmk[l, b, :, h * 256:(h + 1) * 256].rearrange("(j p) d -> p j d", p=128))],
                        w=[b_F5[fi]], key="F5_%d" % fi)
                    mi = ctr["smv"] % 2
                    ctr["smv"] += 1
                    dma("pool", [(smV[mi][:, :, 0:256], cmv[l, b, :, h * 256:(h + 1) * 256].rearrange("(j p) d -> p j d", p=128))], w=[b_smV[mi]], key="smV%d" % mi)
                    return {"fi": fi, "mi": mi}

                def xp_ld(ld):
                    fi = ld["fi"]
                    pt = t_bank()

                    def tr(e, fi=fi, pt=pt):
                        ins = None
                        for k in range(4):
                            ins = e.transpose(out=PS[pt][:, k * 128:(k + 1) * 128], in_=F5[fi][:, k * 128:(k + 1) * 128], identity=idf[:])
                        return ins
                    pe(tr, r=[b_F5[fi], b_const], w=[b_PS[pt]])
                    bi = nxt("B5", 4)
                    act(lambda e, bi=bi, pt=pt: e.activation(out=B5[bi][:], in_=PS[pt][:, :], func=AF.Copy), r=[b_PS[pt]], w=[b_B5[bi]])
                    ld["kv"] = B5[bi][:].rearrange("p (j c t) -> p j c t", j=2, c=2)
                    ld["bk"] = b_B5[bi]
                lds[0] = mk_ld(0)
                xp_ld(lds[0])
                for b in range(4):
                    qr = qrange(b, False)
                    for jj in range(2):
                        s = {}
                        pre = None
                        if jj == 1 and b + 1 < 4:
                            def pre(b=b):
                                lds[b + 1] = mk_ld(b + 1)
                        elif jj == 0 and b > 0:
                            def pre(b=b):
                                xp_ld(lds[b])

                        def f_st(s=s, b=b, jj=jj, qr=qr, hl=hl):
                            ps = s_bank()
                            s["ps"] = ps
                            ld = lds[b]

                            def smm(e, ps=ps, kv=ld["kv"]):
                                e.matmul(PS[ps][:, 0:32], lhsT=kv[:, jj, 0, :], rhs=q[:, hl * 2, qr], start=True, stop=False)
                                return e.matmul(PS[ps][:, 0:32], lhsT=kv[:, jj, 1, :], rhs=q[:, hl * 2 + 1, qr], start=False, stop=True)
                            pe(smm, r=[ld["bk"], b_qT[qi]], w=[b_PS[ps]])

                        def f_mid(s=s, qr=qr):
                            ps = s["ps"]
                            ep = ctr["ep"] % 4
                            ctr["ep"] += 1
                            s["ep"] = ep
                            dve(lambda e, ep=ep: e.memset(EP[ep][:], 0.0), w=[b_EP[ep]])
                            act(A_exp(EP[ep][:, qr], PS[ps][:, 0:32]), r=[b_PS[ps]], w=[b_EP[ep]])

                        def f_pv(s=s, b=b, jj=jj, first=(b == 0 and jj == 0), last=(b == 3 and jj == 1), po=po):
                            ep = s["ep"]
                            mi = lds[b]["mi"]
                            pe(lambda e, ep=ep, mi=mi: e.matmul(PS[po][:, 0:257], lhsT=EP[ep][:, :], rhs=smV[mi][:, jj, :], start=first, stop=last),
                               r=[b_EP[ep], b_smV[mi]], w=[b_PS[po]])
                        s.update(st=f_st, mid=f_mid, pv=f_pv, pre=pre)
                        stages.append(s)
            run_stages(stages, after_first=pend[0])
            si = nxt("st", 4)
            oi = nxt("OB", 2)
            dve(lambda e, si=si, po=po: e.reciprocal(out=st[si][:, 0:1], in_=PS[po][:, 256:257]), r=[b_PS[po]], w=[b_st[si]])
            dve(lambda e, si=si, po=po, oi=oi: e.tensor_scalar(out=OB[oi][:, 0:256], in0=PS[po][:, 0:256], scalar1=st[si][:, 0:1], scalar2=None, op0=ALU.mult),
                r=[b_PS[po], b_st[si]], w=[b_OB[oi]])
            def fin(oi=oi, hl=hl):
                for dc in range(2):
                    finish_to_oT(OB[oi][:, dc * 128:(dc + 1) * 128], b_OB[oi], hg * 4 + hl * 2 + dc, ti, False)
            pend[0] = fin
        pend[0]()

    mskb = sb("mskb", [128, 128], F32)
    c_mskb = din("c_mskb", [128, 128])
    dma("sp", [(mskb[:], c_mskb[:, :])], w=[b_const], key="const")
    mskb3 = mskb[:].unsqueeze(1).to_broadcast([128, 2, 128])


    def skewed_dense(slot, wv, lhs_src, rbufs):
        pis = {}

        def issue(ti):
            pi = zps[0] % 2
            zps[0] += 1
            pis[ti] = pi
            pe(dense_tm(slot, wv, lhs_src, ti, pi), r=rbufs, w=[b_PS[pi]])
        return pis, issue

    inv_sqrt = {"a": 128 ** -0.5, "b": 64 ** -0.5, "m": 256 ** -0.5}

    S.stop_at = getattr(cfg, 'dbg_stop_at', None)
    S.in_loop = True
    try:
        for l in (cfg.dbg_layers if cfg.dbg_layers is not None else range(L)):
            lam_init = 0.8 - 0.6 * math.exp(-0.3 * l)
            prm = [
                (gka[:], a_k_g[l, :].partition_broadcast(128)),
                (gqb[:], b_q_g[l, :].partition_broadcast(128)),
                (gkb[:], b_k_g[l, :].partition_broadcast(128)),
                (gkm[:], m_k_g[l, :].partition_broadcast(128)),
                (gsub[:], subln_g[l, :].partition_broadcast(128)),
                (lamv[:, 0, :], lq1[l, :].partition_broadcast(128)),
                (lamv[:, 1, :], lk1[l, :].partition_broadcast(128)),
                (lamv[:, 2, :], lq2[l, :].partition_broadcast(128)),
                (lamv[:, 3, :], lk2[l, :].partition_broadcast(128)),
            ]
            dma("sp", prm, w=[b_lay], key="lay")
            dma("sp", [(gqa[:], a_q_g[l, :].rearrange("(p o) -> p o", o=1)),
                       (gqm[:, 0:1], m_q_g[l, 0:128].rearrange("(p o) -> p o", o=1)),
                       (gqm[:, 1:2], m_q_g[l, 128:256].rearrange("(p o) -> p o", o=1))]
                + [(c0t[:, h:h + 1], a_rel[l, h, 0:1].partition_broadcast(128)) for h in range(8)], w=[b_lay], key="lay2")
            fa = nxt("F5", NF512)
            fb = nxt("F5", NF512)
            dma("sp", [(F5[fa][0:48, 0:128], gate_b[l, :, :].rearrange("n (c p) -> (n c) p", p=128)),
                       (F5[fa][0:8, 128 + 127:128 + 384], a_rel[l, :, :])], w=[b_F5[fa]], key="F5_%d" % fa)
            dma("sp", [(F5[fb][0:NF, j * 128:(j + 1) * 128], conv_w[l, j, :].rearrange("(f p) -> f p", p=128)) for j in range(3)]
                + [(F5[fb][0:NF, 384:512], conv_b[l, :].rearrange("(f p) -> f p", p=128))], w=[b_F5[fb]], key="F5_%d" % fb)
            dve(lambda e, fa=fa: e.tensor_scalar(out=F5[fa][0:8, 128:128 + 127], in0=F5[fa][0:8, 128 + 128:128 + 255], scalar1=0.0,
                                                 scalar2=F5[fa][0:8, 128 + 127:128 + 128], op0=ALU.mult, op1=ALU.add), r=[b_F5[fa]], w=[b_F5[fa]])
            dma("sp", [(s_ext[:, :], F5[fa][0:8, 128:512])], r=[b_F5[fa]], w=[b_dext], key="ext")
            pe(lambda e, fa=fa: e.transpose(out=PS[2][:, 0:48], in_=F5[fa][0:48, 0:128], identity=idf[0:48, 0:48]), r=[b_F5[fa], b_const], w=[b_PS[2]])
            act(lambda e: e.activation(out=gtb[:].rearrange("p n c -> p (n c)"), in_=PS[2][:, 0:48], func=AF.Copy), r=[b_PS[2]], w=[b_lay])

            def trc(e, fb=fb):
                ins = None
                for j in range(4):
                    ins = e.transpose(out=PS[3][:, j * 64:j * 64 + NF], in_=F5[fb][0:NF, j * 128:(j + 1) * 128], identity=idf[0:NF, 0:NF])
                return ins
            pe(trc, r=[b_F5[fb], b_const], w=[b_PS[3]])
            act(lambda e: e.activation(out=cw[:], in_=PS[3][:, 0:192].rearrange("p (j f) -> p j f", j=3)[:, :, 0:NF], func=AF.Copy), r=[b_PS[3]], w=[b_lay])
            act(lambda e: e.activation(out=cb[:], in_=PS[3][:, 192:192 + NF], func=AF.Copy), r=[b_PS[3]], w=[b_lay])
            dve(lambda e: e.tensor_tensor(out=lamv[:, 0, :], in0=lamv[:, 0, :], in1=lamv[:, 1, :], op=ALU.mult), r=[b_lay], w=[b_lay])
            dve(lambda e: e.tensor_tensor(out=lamv[:, 2, :], in0=lamv[:, 2, :], in1=lamv[:, 3, :], op=ALU.mult), r=[b_lay], w=[b_lay])
            dve(lambda e: e.tensor_reduce(out=lams[:, 0:1], in_=lamv[:, 0, :], axis=AX.X, op=ALU.add), r=[b_lay], w=[b_lay])
            dve(lambda e: e.tensor_reduce(out=lams[:, 1:2], in_=lamv[:, 2, :], axis=AX.X, op=ALU.add), r=[b_lay], w=[b_lay])
            act(lambda e: e.activation(out=lams[:, 2:4], in_=lams[:, 0:2], func=AF.Exp), r=[b_lay], w=[b_lay])
            dve(lambda e, li=lam_init: e.scalar_tensor_tensor(out=lams[:, 4:5], in0=lams[:, 3:4], scalar=-li, in1=lams[:, 2:3],
                                                 op0=ALU.add, op1=ALU.subtract), r=[b_lay], w=[b_lay])
            neg_lam = lams[:, 4:5]
            dve(lambda e, li=lam_init: e.tensor_scalar(out=gsub[:], in0=gsub[:], scalar1=1.0 - li, scalar2=None, op0=ALU.mult), r=[b_lay], w=[b_lay])

            for gi, tiles in enumerate(cfg.groups):
                NT = len(tiles)
                has_s = any(t[0] == "s" for t in tiles)
                ptiles = [t for t in tiles if t[0] == "p"]
                pt0 = ptiles[0][1]
                npl = len(ptiles)
                tbs = tok_blocks(NT)

                norm_stage(tiles, lambda t, l=l: x_src(l, t), norm_mix_g[l, :], hT, b_hT)
                S.barrier()

                if gi == 0:
                    hmT = A1[:, 16384:20480].rearrange("p (k t) -> p k t", k=16)
                    norm_stage([("m", 0), ("m", 1)], lambda t: mem[t[1] * 128:(t[1] + 1) * 128, :], mem_g[l, :], hmT, b_mT, psbase=4)
                    for cbk_i in range(4):
                        slot = next_slot()
                        wv = load_w_block(slot, w_mem[l, :, cbk_i * 512:(cbk_i + 1) * 512], 512)
                        for mt in range(2):
                            pi = zps[0] % 2
                            zps[0] += 1
                            pe(dense_tm(slot, wv, hmT, mt, pi), r=[b_mT, b_W[slot]], w=[b_PS[pi]])
                            fi = nxt("F5", NF512)
                            act(lambda e, pi=pi, fi=fi: e.activation(out=F5[fi][:], in_=PS[pi][:, :], func=AF.Copy),
                                r=[b_PS[pi]], w=[b_F5[fi]])
                            rows = slice(mt * 128, (mt + 1) * 128)
                            if cbk_i < 2:
                                si = nxt("st", 4)
                                f2 = nxt("F5", NF512)
                                dve(lambda e, fi=fi, f2=f2: e.tensor_tensor(out=F5[f2][:], in0=F5[fi][:], in1=F5[fi][:], op=ALU.mult),
                                    r=[b_F5[fi]], w=[b_F5[f2]])
                                dve(lambda e, f2=f2, si=si: e.tensor_reduce(out=st[si][:, 0:2], in_=F5[f2][:].rearrange("p (h d) -> p h d", h=2),
                                                                            axis=AX.X, op=ALU.add), r=[b_F5[f2]], w=[b_st[si]])
                                act(lambda e, si=si: e.activation(out=st[si][:, 2:4], in_=st[si][:, 0:2], func=AF.Ln, bias=epsc[:], scale=1.0 / 256),
                                    r=[b_st[si], b_const], w=[b_st[si]])
                                act(lambda e, si=si: e.activation(out=st[si][:, 4:6], in_=st[si][:, 2:4], func=AF.Exp, scale=-0.5),
                                    r=[b_st[si]], w=[b_st[si]])
                                for hh in range(2):
                                    dve(lambda e, fi=fi, si=si, hh=hh: e.scalar_tensor_tensor(
                                        out=F5[fi][:, hh * 256:(hh + 1) * 256], in0=F5[fi][:, hh * 256:(hh + 1) * 256],
                                        scalar=st[si][:, 4 + hh:5 + hh], in1=gkm[:], op0=ALU.mult, op1=ALU.mult),
                                        r=[b_F5[fi], b_st[si], b_lay], w=[b_F5[fi]])
                                dma("sp", [(mkp[l, rows, cbk_i * 512:(cbk_i + 1) * 512], F5[fi][:])], r=[b_F5[fi]], key="F5_%d" % fi)
                                pi2 = 2 + (zps[0] % 2)

                                def trm(e, fi=fi, pi2=pi2):
                                    ins = None
                                    for k in range(4):
                                        ins = e.transpose(out=PS[pi2][:, k * 128:(k + 1) * 128], in_=F5[fi][:, k * 128:(k + 1) * 128], identity=idf[:])
                                    return ins
                                pe(trm, r=[b_F5[fi], b_const], w=[b_PS[pi2]])
                                act(lambda e, pi2=pi2, cbk_i=cbk_i, mt=mt: e.activation(
                                    out=mkT[:, cbk_i * 4:(cbk_i + 1) * 4, mt * 128:(mt + 1) * 128],
                                    in_=PS[pi2][:, :].rearrange("p (k t) -> p k t", k=4), func=AF.Copy),
                                    r=[b_PS[pi2]], w=[b_mkT])
                            else:
                                hb0 = (cbk_i - 2) * 2
                                dma("sp", [(mvp[l, rows, (cbk_i - 2) * 512:(cbk_i - 1) * 512], F5[fi][:])], r=[b_F5[fi]], key="F5_%d" % fi)
                                dve(lambda e, fi=fi, mt=mt, hb0=hb0: e.tensor_copy(
                                    out=cVm[:, mt, hb0:hb0 + 2, 0:256], in_=F5[fi][:].rearrange("p (h d) -> p h d", h=2)),
                                    r=[b_F5[fi]], w=[b_cVm])
                    dve(lambda e: e.memset(cVm[:, :, :, 256:257], 1.0), w=[b_cVm])
                    dma("sp", [(s_mkT[:, :, :], mkT[:]), (s_mv[:, :, :, :], cVm[:])], r=[b_mkT, b_cVm], w=[b_dram], key="carry_m")
                    S.barrier()
                else:
                    dma("sp", [(mkT[:], s_mkT[:, :, :]), (cVm[:], s_mv[:, :, :, :])], r=[b_dram], w=[b_mkT, b_cVm], key="carry_m")

                dve(lambda e: e.memset(wVflat, 1.0), w=[b_wV])
                for br in cfg.dbg_br:
                    bname = "abm"[br]
                    qoff = {"a": 0, "b": 3072, "m": 6144}[bname]
                    for hg in range(2):
                        if bname == "a":
                            dma("sp", [(hank[:, hh, x, :], bass.AP(s_ext.tensor, (hg * 4 + hh) * 384 + (128 if x == 0 else 0), [[1, 128], [1, 128]]))
                                       for hh in range(4) for x in range(2)], r=[b_dext], w=[b_hank], key="hank")
                        if bname in ("a", "b"):
                            koff = qoff + 1024 + hg * 512
                            voff = qoff + 2048 + hg * 512
                            s_kT = s_kTa if bname == "a" else s_kTb
                            s_v = s_va if bname == "a" else s_vb
                            kout_p = akp if bname == "a" else bkp
                            vout_p = avp if bname == "a" else bvp
                            kout_s = aks if bname == "a" else bks
                            vout_s = avs if bname == "a" else bvs
                            if bname == "a":
                                kt_lo = max(0, pt0 - 4)
                            else:
                                kt_lo = 0
                            if pt0 > kt_lo:
                                dma("sp", [(wkT[:, :, kt_lo * 128:pt0 * 128], s_kT[:, hg * 4:(hg + 1) * 4, kt_lo * 128:pt0 * 128]),
                                           (wV[:, kt_lo:pt0, :, :], s_v[:, kt_lo:pt0, hg * 4:(hg + 1) * 4, :])],
                                    r=[b_dram], w=[b_wkT, b_wV], key="carry_ld")
                            slot = next_slot()
                            wv = load_w_block(slot, w_in[l, :, koff:koff + 512], 512)
                            pis, issue = skewed_dense(slot, wv, hT, [b_hT, b_W[slot]])
                            issue(0)
                            for ti, tile in enumerate(tiles):
                                if ti + 1 < NT:
                                    issue(ti + 1)
                                pi = pis[ti]
                                fi = nxt("F5", NF512)
                                f2 = nxt("F5", NF512)
                                si = nxt("st", 4)
                                act(lambda e, pi=pi, fi=fi: e.activation(out=F5[fi][:], in_=PS[pi][:, :], func=AF.Copy), r=[b_PS[pi]], w=[b_F5[fi]])
                                nsub = 4 if bname == "a" else 8
                                dsub = 512 // nsub
                                dve(lambda e, fi=fi, f2=f2: e.tensor_tensor(out=F5[f2][:], in0=F5[fi][:], in1=F5[fi][:], op=ALU.mult),
                                    r=[b_F5[fi]], w=[b_F5[f2]])
                                dve(lambda e, f2=f2, si=si, nsub=nsub: e.tensor_reduce(out=st[si][:, 0:nsub], in_=F5[f2][:].rearrange("p (h d) -> p h d", h=nsub),
                                                                                       axis=AX.X, op=ALU.add), r=[b_F5[f2]], w=[b_st[si]])
                                act(lambda e, si=si, nsub=nsub, dsub=dsub: e.activation(out=st[si][:, 8:8 + nsub], in_=st[si][:, 0:nsub], func=AF.Ln,
                                                                                        bias=epsc[:], scale=1.0 / dsub), r=[b_st[si], b_const], w=[b_st[si]])
                                act(lambda e, si=si, nsub=nsub: e.activation(out=st[si][:, 0:nsub], in_=st[si][:, 8:8 + nsub], func=AF.Exp, scale=-0.5),
                                    r=[b_st[si]], w=[b_st[si]])
                                gt_ = gka if bname == "a" else gkb
                                for hh in range(nsub):
                                    dve(lambda e, fi=fi, si=si, hh=hh, dsub=dsub, gt_=gt_: e.scalar_tensor_tensor(
                                        out=F5[fi][:, hh * dsub:(hh + 1) * dsub], in0=F5[fi][:, hh * dsub:(hh + 1) * dsub],
                                        scalar=st[si][:, hh:hh + 1], in1=gt_[:], op0=ALU.mult, op1=ALU.mult),
                                        r=[b_F5[fi], b_st[si], b_lay], w=[b_F5[fi]])
                                if bname == "b":
                                    rope(fi, tile)
                                if tile[0] == "s":
                                    dma("sp", [(kout_s[l, :, hg * 512:(hg + 1) * 512], F5[fi][:])], r=[b_F5[fi]], key="F5_%d" % fi)
                                else:
                                    prow = tile[1] * 128
                                    if bname == "b":
                                        dma("sp", [(kout_p[l, prow:prow + 128, hg * 512:(hg + 1) * 512], F5[fi][:])], r=[b_F5[fi]], key="F5_%d" % fi)
                                    elif prow >= SEQ - AK:
                                        r0 = prow - (SEQ - AK)
                                        dma("sp", [(kout_p[l, r0:r0 + 128, hg * 512:(hg + 1) * 512], F5[fi][:])], r=[b_F5[fi]], key="F5_%d" % fi)
                                pi2 = 2 + (zps[0] % 2)

                                def trk(e, fi=fi, pi2=pi2):
                                    ins = None
                                    for k in range(4):
                                        ins = e.transpose(out=PS[pi2][:, k * 128:(k + 1) * 128], in_=F5[fi][:, k * 128:(k + 1) * 128], identity=idf[:])
                                    return ins
                                pe(trk, r=[b_F5[fi], b_const], w=[b_PS[pi2]])
                                if tile[0] == "s":
                                    act(lambda e, pi2=pi2: e.activation(out=skT[:], in_=PS[pi2][:, :].rearrange("p (k t) -> p k t", k=4), func=AF.Copy),
                                        r=[b_PS[pi2]], w=[b_skT])
                                else:
                                    act(lambda e, pi2=pi2, kt=tile[1]: e.activation(out=wkT[:, :, kt * 128:(kt + 1) * 128],
                                                                                    in_=PS[pi2][:, :].rearrange("p (k t) -> p k t", k=4), func=AF.Copy),
                                        r=[b_PS[pi2]], w=[b_wkT])
                            slot = next_slot()
                            wv = load_w_block(slot, w_in[l, :, voff:voff + 512], 512)
                            pis, issue = skewed_dense(slot, wv, hT, [b_hT, b_W[slot]])
                            issue(0)
                            for ti, tile in enumerate(tiles):
                                if ti + 1 < NT:
                                    issue(ti + 1)
                                pi = pis[ti]
                                fi = nxt("F5", NF512)
                                act(lambda e, pi=pi, fi=fi: e.activation(out=F5[fi][:], in_=PS[pi][:, :], func=AF.Copy), r=[b_PS[pi]], w=[b_F5[fi]])
                                if tile[0] == "s":
                                    dma("sp", [(vout_s[l, :, hg * 512:(hg + 1) * 512], F5[fi][:])], r=[b_F5[fi]], key="F5_%d" % fi)
                                    dve(lambda e, fi=fi: e.tensor_copy(out=sVn[:, :, 0:128], in_=F5[fi][:].rearrange("p (h d) -> p h d", h=4)),
                                        r=[b_F5[fi]], w=[b_sVn])
                                    dve(lambda e: e.memset(sVn[:, :, 128:129], 1.0), w=[b_sVn])
                                else:
                                    prow = tile[1] * 128
                                    if bname == "b":
                                        dma("sp", [(vout_p[l, prow:prow + 128, hg * 512:(hg + 1) * 512], F5[fi][:])], r=[b_F5[fi]], key="F5_%d" % fi)
                                    elif prow >= SEQ - AK:
                                        r0 = prow - (SEQ - AK)
                                        dma("sp", [(vout_p[l, r0:r0 + 128, hg * 512:(hg + 1) * 512], F5[fi][:])], r=[b_F5[fi]], key="F5_%d" % fi)
                                    dve(lambda e, fi=fi, kt=tile[1]: e.tensor_copy(out=wV[:, kt, :, 0:128], in_=F5[fi][:].rearrange("p (h d) -> p h d", h=4)),
                                        r=[b_F5[fi]], w=[b_wV])
                            if gi + 1 < len(cfg.groups) and npl > 0:
                                dma("sp", [(s_kT[:, hg * 4:(hg + 1) * 4, pt0 * 128:(pt0 + npl) * 128], wkT[:, :, pt0 * 128:(pt0 + npl) * 128]),
                                           (s_v[:, pt0:pt0 + npl, hg * 4:(hg + 1) * 4, :], wV[:, pt0:pt0 + npl, :, :])],
                                    r=[b_wkT, b_wV], w=[b_dram], key="carry_st")
                        slot = next_slot()
                        wv = load_w_block(slot, w_in[l, :, qoff + hg * 512:qoff + (hg + 1) * 512], 512)
                        pis, issue = skewed_dense(slot, wv, hT, [b_hT, b_W[slot]])

                        def q_epi(ti, tile):
                            pi = pis[ti]
                            fi = nxt("F5", NF512)
                            f2 = nxt("F5", NF512)
                            si = nxt("st", 4)
                            act(lambda e, pi=pi, fi=fi: e.activation(out=F5[fi][:], in_=PS[pi][:, :], func=AF.Copy), r=[b_PS[pi]], w=[b_F5[fi]])
                            nsub = {"a": 4, "b": 8, "m": 2}[bname]
                            dsub = 512 // nsub
                            sc = inv_sqrt[bname]
                            dve(lambda e, fi=fi, f2=f2: e.tensor_tensor(out=F5[f2][:], in0=F5[fi][:], in1=F5[fi][:], op=ALU.mult),
                                r=[b_F5[fi]], w=[b_F5[f2]])
                            dve(lambda e, f2=f2, si=si, nsub=nsub: e.tensor_reduce(out=st[si][:, 0:nsub], in_=F5[f2][:].rearrange("p (h d) -> p h d", h=nsub),
                                                                                   axis=AX.X, op=ALU.add), r=[b_F5[f2]], w=[b_st[si]])
                            act(lambda e, si=si, nsub=nsub, dsub=dsub: e.activation(out=st[si][:, 8:8 + nsub], in_=st[si][:, 0:nsub], func=AF.Ln,
                                                                                    bias=epsc[:], scale=1.0 / dsub), r=[b_st[si], b_const], w=[b_st[si]])
                            act(lambda e, si=si, nsub=nsub: e.activation(out=st[si][:, 0:nsub], in_=st[si][:, 8:8 + nsub], func=AF.Exp, scale=-0.5),
                                r=[b_st[si]], w=[b_st[si]])
                            bi = nxt("B5", 4)
                            if bname == "b":
                                for hh in range(nsub):
                                    dve(lambda e, fi=fi, si=si, hh=hh, dsub=dsub: e.scalar_tensor_tensor(
                                        out=F5[fi][:, hh * dsub:(hh + 1) * dsub], in0=F5[fi][:, hh * dsub:(hh + 1) * dsub],
                                        scalar=st[si][:, hh:hh + 1], in1=gqb[:], op0=ALU.mult, op1=ALU.mult),
                                        r=[b_F5[fi], b_st[si], b_lay], w=[b_F5[fi]])
                                rope(fi, tile)
                                dve(lambda e, fi=fi, bi=bi, sc=sc: e.tensor_scalar(out=B5[bi][:], in0=F5[fi][:], scalar1=sc, scalar2=None, op0=ALU.mult),
                                    r=[b_F5[fi]], w=[b_B5[bi]])
                            else:
                                for hh in range(nsub):
                                    dve(lambda e, fi=fi, bi=bi, si=si, hh=hh, dsub=dsub, sc=sc: e.tensor_scalar(
                                        out=B5[bi][:, hh * dsub:(hh + 1) * dsub], in0=F5[fi][:, hh * dsub:(hh + 1) * dsub],
                                        scalar1=st[si][:, hh:hh + 1], scalar2=sc, op0=ALU.mult, op1=ALU.mult),
                                        r=[b_F5[fi], b_st[si]], w=[b_B5[bi]])
                            return bi

                        def q_fin(ti, tile, bi):
                            qi = nxt("qT", 2)
                            pi2 = 2 + (zps[0] % 2)
                            if bname == "a":
                                def trq(e, bi=bi, pi2=pi2):
                                    ins = None
                                    for k in range(4):
                                        ins = e.matmul(PS[pi2][:, k * 128:(k + 1) * 128], lhsT=B5[bi][:, k * 128:(k + 1) * 128], rhs=jb[:], start=True, stop=True)
                                    return ins
                                pe(trq, r=[b_B5[bi], b_const], w=[b_PS[pi2]])
                                act(lambda e, pi2=pi2, qi=qi: e.activation(out=qTt[qi][:], in_=PS[pi2][:, :].rearrange("p (k t) -> p k t", k=4),
                                                                           func=AF.Identity, scale=gqa[:, 0:1]), r=[b_PS[pi2], b_lay], w=[b_qT[qi]])
                            else:
                                pvb = PS[pi2][:].bitcast(BF16)

                                def trq(e, bi=bi, pvb=pvb):
                                    ins = None
                                    for k in range(4):
                                        ins = e.transpose(out=pvb[:, k * 128:(k + 1) * 128], in_=B5[bi][:, k * 128:(k + 1) * 128], identity=idb[:])
                                    return ins
                                pe(trq, r=[b_B5[bi], b_const], w=[b_PS[pi2]])
                                if bname == "b":
                                    act(lambda e, pvb=pvb, qi=qi: e.activation(out=qTt[qi][0:64], in_=pvb[0:64, 0:512].rearrange("p (k t) -> p k t", k=4), func=AF.Copy),
                                        r=[b_PS[pi2]], w=[b_qT[qi]])
                                    act(lambda e, pvb=pvb, qi=qi: e.activation(out=qTz[qi][64:128], in_=pvb[64:128, 0:512].rearrange("p (k t) -> p k t", k=4), func=AF.Copy),
                                        r=[b_PS[pi2]], w=[b_qT[qi]])
                                    dve(lambda e, qi=qi: e.memset(qTt[qi][64:128], 0.0), w=[b_qT[qi]])
                                    dve(lambda e, qi=qi: e.memset(qTz[qi][0:64], 0.0), w=[b_qT[qi]])
                                else:
                                    for k in range(4):
                                        act(lambda e, pvb=pvb, qi=qi, k=k: e.activation(out=qTt[qi][:, k, :], in_=pvb[:, k * 128:(k + 1) * 128],
                                                                                       func=AF.Identity, scale=gqm[:, (k % 2):(k % 2) + 1]),
                                            r=[b_PS[pi2], b_lay], w=[b_qT[qi]])
                            if bname == "a":
                                attn_a(l, hg, ti, tile, qi)
                            elif bname == "b":
                                attn_b(l, hg, ti, tile, qi, neg_lam, lam_init)
                            else:
                                attn_m(l, hg, ti, tile, qi)
                        issue(0)
                        if NT > 1:
                            issue(1)
                        bis = {0: q_epi(0, tiles[0])}
                        for ti, tile in enumerate(tiles):
                            if ti + 1 < NT:
                                bis[ti + 1] = q_epi(ti + 1, tiles[ti + 1])
                            if ti + 2 < NT:
                                issue(ti + 2)
                            q_fin(ti, tile, bis[ti])
                    for cp in range(8):
                        slot = next_slot()
                        gcol = 7168 + br * 2048 + cp * 256
                        wg = W[:, slot * 8192:slot * 8192 + 4096].rearrange("p (k n) -> p k n", k=16)
                        wb = W[:, slot * 8192 + 4096:slot * 8192 + 6144].rearrange("p (k n) -> p k n", k=8)
                        dma("pool", [(wg, w_in[l, :, gcol:gcol + 256].rearrange("(k p) n -> p k n", p=128)),
                                     (wb, w_br[l, br, :, cp * 256:(cp + 1) * 256].rearrange("(k p) n -> p k n", p=128))],
                            w=[b_W[slot]], key="W%d" % slot)
                        for cc in range(2):
                            c = cp * 2 + cc
                            for (t0, ntb) in tbs:
                                N = ntb * 128
                                cols = slice(t0 * 128, t0 * 128 + N)
                                pg = 4 + (zps[0] % 2) * 2
                                zps[0] += 1

                                def mmg(e, wg=wg, cc=cc, cols=cols, pg=pg, N=N):
                                    ins = None
                                    for kc in range(16):
                                        ins = e.matmul(PS[pg][:, 0:N], lhsT=wg[:, kc, cc * 128:(cc + 1) * 128], rhs=hT[:, kc, cols], start=(kc == 0), stop=(kc == 15))
                                    return ins

                                def mmp(e, wb=wb, cc=cc, cols=cols, pg=pg, N=N):
                                    ins = None
                                    for kc in range(8):
                                        ins = e.matmul(PS[pg + 1][:, 0:N], lhsT=wb[:, kc, cc * 128:(cc + 1) * 128], rhs=oT[:, kc, cols], start=(kc == 0), stop=(kc == 7))
                                    return ins
                                pe(mmg, r=[b_hT, b_W[slot]], w=[b_PS[pg]])
                                pe(mmp, r=[b_oT, b_W[slot]], w=[b_PS[pg + 1]])
                                fi = nxt("F5", NF512)
                                act(lambda e, fi=fi, pg=pg, N=N, c=c, br=br: e.activation(out=F5[fi][:, 0:N], in_=PS[pg][:, 0:N], func=AF.Sigmoid,
                                                                                          bias=gtb[:, br, c:c + 1]), r=[b_PS[pg], b_lay], w=[b_F5[fi]])
                                if br == cfg.dbg_br[0]:
                                    dve(lambda e, fi=fi, pg=pg, N=N, c=c, cols=cols: e.tensor_tensor(out=mT[:, c, cols], in0=F5[fi][:, 0:N], in1=PS[pg + 1][:, 0:N], op=ALU.mult),
                                        r=[b_F5[fi], b_PS[pg + 1]], w=[b_mT])
                                else:
                                    dve(lambda e, fi=fi, pg=pg, N=N: e.tensor_tensor(out=F5[fi][:, 0:N], in0=F5[fi][:, 0:N], in1=PS[pg + 1][:, 0:N], op=ALU.mult),
                                        r=[b_F5[fi], b_PS[pg + 1]], w=[b_F5[fi]])
                                    dve(lambda e, fi=fi, N=N, c=c, cols=cols: e.tensor_tensor(out=mT[:, c, cols], in0=F5[fi][:, 0:N], in1=mT[:, c, cols], op=ALU.add),
                                        r=[b_F5[fi], b_mT], w=[b_mT])

                for cbi in range(4):
                    slot = next_slot()
                    wv = load_w_block(slot, w_out[l, :, cbi * 512:(cbi + 1) * 512], 512)
                    for ti, tile in enumerate(tiles):
                        pi = zps[0] % 2
                        zps[0] += 1
                        fi = nxt("F5", NF512)
                        dma("sp", [(F5[fi][:], x_src(l, tile)[:, cbi * 512:(cbi + 1) * 512])], w=[b_F5[fi]], key="F5_%d" % fi)
                        pe(dense_tm(slot, wv, mT, ti, pi), r=[b_mT, b_W[slot]], w=[b_PS[pi]])
                        dve(lambda e, fi=fi, pi=pi: e.tensor_tensor(out=F5[fi][:], in0=F5[fi][:], in1=PS[pi][:, :], op=ALU.add),
                            r=[b_F5[fi], b_PS[pi]], w=[b_F5[fi]])
                        dma("sp", [(y_dst(tile)[:, cbi * 512:(cbi + 1) * 512], F5[fi][:])], r=[b_F5[fi]], key="F5_%d" % fi)
                S.barrier()

                if cfg.dbg_skip_ffn:
                    continue
                norm_stage(tiles, lambda t: y_dst(t), ffn_g[l, :], hT, b_hT)
                if has_s:
                    for f in range(NF):
                        if f % 4 == 0:
                            nf4 = min(4, NF - f)
                            fi = nxt("F5", NF512)
                            dma("sp", [(F5[fi][0:8, 0:nf4 * 128], scv[l, :, f * 128:(f + nf4) * 128])], w=[b_F5[fi]], key="F5_%d" % fi)
                            pi = 2 + (zps[0] % 2)
                            zps[0] += 1

                            def trs(e, fi=fi, pi=pi, nf4=nf4):
                                ins = None
                                for k in range(nf4):
                                    ins = e.transpose(out=PS[pi][:, k * 8:(k + 1) * 8], in_=F5[fi][0:8, k * 128:(k + 1) * 128], identity=idf[0:8, 0:8])
                                return ins
                            pe(trs, r=[b_F5[fi], b_const], w=[b_PS[pi]])
                            act(lambda e, pi=pi, f=f, nf4=nf4: e.activation(out=shalo[:, f:f + nf4, :], in_=PS[pi][:, 0:nf4 * 8].rearrange("p (k t) -> p k t", k=nf4),
                                                                            func=AF.Copy), r=[b_PS[pi]], w=[b_shalo])
                S.barrier()
                for f in range(NF):
                    slot = next_slot()
                    wu = W[:, slot * 8192:slot * 8192 + 4096].rearrange("p (k n) -> p k n", k=16)
                    dma("pool", [(wu[:, :, 0:128], w_up[l, :, f * 128:(f + 1) * 128].rearrange("(k p) n -> p k n", p=128)),
                                 (wu[:, :, 128:256], w_up[l, :, DFF + f * 128:DFF + (f + 1) * 128].rearrange("(k p) n -> p k n", p=128))],
                        w=[b_W[slot]], key="W%d" % slot)
                    for (t0, ntb) in tbs:
                        segs = []
                        nprompt = sum(1 for t in tiles[t0:t0 + ntb] if t[0] == "p")
                        if nprompt:
                            segs.append(("p", t0, nprompt))
                        if nprompt < ntb:
                            segs.append(("s", t0 + nprompt, 1))
                        for (kind, ts, ntk) in segs:
                            N = ntk * 128
                            cols = slice(ts * 128, ts * 128 + N)
                            pg = 4 + (zps[0] % 2) * 2
                            zps[0] += 1

                            def mma(e, wu=wu, cols=cols, pg=pg, N=N):
                                ins = None
                                for kc in range(16):
                                    ins = e.matmul(PS[pg][:, 0:N], lhsT=wu[:, kc, 0:128], rhs=hT[:, kc, cols], start=(kc == 0), stop=(kc == 15))
                                return ins

                            def mmb(e, wu=wu, cols=cols, pg=pg, N=N):
                                ins = None
                                for kc in range(16):
                                    ins = e.matmul(PS[pg + 1][:, 0:N], lhsT=wu[:, kc, 128:256], rhs=hT[:, kc, cols], start=(kc == 0), stop=(kc == 15))
                                return ins
                            pe(mma, r=[b_hT, b_W[slot]], w=[b_PS[pg]])
                            pe(mmb, r=[b_hT, b_W[slot]], w=[b_PS[pg + 1]])
                            fi = nxt("F5", NF512)
                            f2 = nxt("F5", NF512)
                            if kind == "p":
                                first = (tiles[ts][1] == 0)
                                if first:
                                    dve(lambda e: e.memset(abuf[:, 0:2], 0.0), w=[b_abuf])
                                else:
                                    dve(lambda e, f=f: e.tensor_copy(out=abuf[:, 0:2], in_=acar[:, f, :]), r=[b_acar], w=[b_abuf])
                                act(lambda e, pg=pg, N=N: e.activation(out=abuf[:, 2:2 + N], in_=PS[pg][:, 0:N], func=AF.Copy), r=[b_PS[pg]], w=[b_abuf])
                                act(lambda e, pg=pg, N=N, fi=fi, f=f: e.activation(out=F5[fi][:, 0:N], in_=PS[pg][:, 0:N], func=AF.Identity,
                                                                                  bias=cb[:, f:f + 1], scale=cw[:, 2, f:f + 1]), r=[b_PS[pg], b_lay], w=[b_F5[fi]])
                                dve(lambda e, fi=fi, N=N, f=f: e.scalar_tensor_tensor(out=F5[fi][:, 0:N], in0=abuf[:, 1:1 + N], scalar=cw[:, 1, f:f + 1],
                                                                                      in1=F5[fi][:, 0:N], op0=ALU.mult, op1=ALU.add), r=[b_abuf, b_F5[fi], b_lay], w=[b_F5[fi]])
                                dve(lambda e, fi=fi, N=N, f=f: e.scalar_tensor_tensor(out=F5[fi][:, 0:N], in0=abuf[:, 0:N], scalar=cw[:, 0, f:f + 1],
                                                                                      in1=F5[fi][:, 0:N], op0=ALU.mult, op1=ALU.add), r=[b_abuf, b_F5[fi], b_lay], w=[b_F5[fi]])
                                dve(lambda e, f=f, N=N: e.tensor_copy(out=acar[:, f, :], in_=abuf[:, N:N + 2]), r=[b_abuf], w=[b_acar])
                                last_tile = tiles[ts + ntk - 1]
                                if last_tile[1] == NPT - 1:
                                    dve(lambda e, f=f, N=N: e.tensor_copy(out=alast[:, f, 0:2], in_=abuf[:, N:N + 2]), r=[b_abuf], w=[b_alast])
                            else:
                                av = abuf[:, 0:136].rearrange("p (b t) -> p b t", b=4)
                                dve(lambda e, f=f, av=av: e.tensor_copy(out=av[:, :, 0:2], in_=shalo[:, f, :].rearrange("p (b t) -> p b t", b=4)),
                                    r=[b_shalo], w=[b_abuf])
                                act(lambda e, pg=pg, av=av: e.activation(out=av[:, :, 2:34], in_=PS[pg][:, 0:128].rearrange("p (b t) -> p b t", b=4), func=AF.Copy),
                                    r=[b_PS[pg]], w=[b_abuf])
                                act(lambda e, pg=pg, fi=fi, f=f: e.activation(out=F5[fi][:, 0:128], in_=PS[pg][:, 0:128], func=AF.Identity,
                                                                             bias=cb[:, f:f + 1], scale=cw[:, 2, f:f + 1]), r=[b_PS[pg], b_lay], w=[b_F5[fi]])
                                fv = F5[fi][:, 0:128].rearrange("p (b t) -> p b t", b=4)
                                dve(lambda e, fv=fv, av=av, f=f: e.scalar_tensor_tensor(out=fv, in0=av[:, :, 1:33], scalar=cw[:, 1, f:f + 1], in1=fv,
                                                                                        op0=ALU.mult, op1=ALU.add), r=[b_abuf, b_F5[fi], b_lay], w=[b_F5[fi]])
                                dve(lambda e, fv=fv, av=av, f=f: e.scalar_tensor_tensor(out=fv, in0=av[:, :, 0:32], scalar=cw[:, 0, f:f + 1], in1=fv,
                                                                                        op0=ALU.mult, op1=ALU.add), r=[b_abuf, b_F5[fi], b_lay], w=[b_F5[fi]])
                                dve(lambda e, f=f, av=av: e.tensor_copy(out=alast[:, f, 2:10].rearrange("p (b t) -> p b t", b=4), in_=av[:, :, 32:34]),
                                    r=[b_abuf], w=[b_alast])
                            act(lambda e, fi=fi, f2=f2, N=N: e.activation(out=F5[f2][:, 0:N], in_=F5[fi][:, 0:N], func=AF.Gelu), r=[b_F5[fi]], w=[b_F5[f2]])
                            dve(lambda e, f2=f2, pg=pg, N=N, f=f, cols=cols: e.tensor_tensor(out=uT[:, f, cols], in0=F5[f2][:, 0:N], in1=PS[pg + 1][:, 0:N], op=ALU.mult),
                                r=[b_F5[f2], b_PS[pg + 1]], w=[b_uT])
                last_group = (gi == len(cfg.groups) - 1)
                if has_s or last_group:
                    for f in range(NF):
                        pi = 2 + (zps[0] % 2)
                        zps[0] += 1
                        pe(lambda e, f=f, pi=pi: e.transpose(out=PS[pi][0:10, 0:128], in_=alast[:, f, :], identity=idf[:]), r=[b_alast, b_const], w=[b_PS[pi]])
                        act(lambda e, pi=pi: e.activation(out=o10[0:10, :], in_=PS[pi][0:10, 0:128], func=AF.Copy), r=[b_PS[pi]], w=[b_o10])
                        outs = []
                        if last_group:
                            outs.append((cvp[l, :, f * 128:(f + 1) * 128], o10[0:2, :]))
                        if has_s:
                            outs.append((cvs[l, :, f * 128:(f + 1) * 128], o10[2:10, :]))
                        dma("sp", outs, r=[b_o10], key="o10")
                S.barrier()
                for cb8 in range(8):
                    slot = (cb8 % 2)
                    wd = W[:, slot * 11008:slot * 11008 + NF * 256].rearrange("p (k n) -> p k n", k=NF)
                    dma("pool", [(wd, w_dn[l, :, cb8 * 256:(cb8 + 1) * 256].rearrange("(k p) n -> p k n", p=128))], w=[b_W[slot]], key="W%d" % slot)
                    for ti, tile in enumerate(tiles):
                        pi = zps[0] % 2
                        zps[0] += 1
                        fi = nxt("F5", NF512)
                        dma("sp", [(F5[fi][:, 0:256], y_dst(tile)[:, cb8 * 256:(cb8 + 1) * 256])], w=[b_F5[fi]], key="F5_%d" % fi)

                        def mmd(e, wd=wd, ti=ti, pi=pi):
                            ins = None
                            for kf in range(NF):
                                ins = e.matmul(PS[pi][:, 0:256], lhsT=uT[:, kf, ti * 128:(ti + 1) * 128], rhs=wd[:, kf, :], start=(kf == 0), stop=(kf == NF - 1))
                            return ins
                        pe(mmd, r=[b_uT, b_W[slot]], w=[b_PS[pi]])
                        dve(lambda e, fi=fi, pi=pi: e.tensor_tensor(out=F5[fi][:, 0:256], in0=F5[fi][:, 0:256], in1=PS[pi][:, 0:256], op=ALU.add),
                            r=[b_F5[fi], b_PS[pi]], w=[b_F5[fi]])
                        dma("sp", [(y_dst(tile)[:, cb8 * 256:(cb8 + 1) * 256], F5[fi][:, 0:256])], r=[b_F5[fi]], key="F5_%d" % fi)
                S.barrier(new_epoch=(gi == len(cfg.groups) - 1))


    except _Stop:
        pass
    S.in_loop = False

    sem_keys = []
    seen = set()
    for e in ENGS:
        for o in S.q[e]:
            if o.dma:
                k = ("d", o.key)
            elif o.signal:
                k = ("e", (e, o.epoch))
            else:
                continue
            if k not in seen:
                seen.add(k)
                sem_keys.append(k)
    semmap = {}
    for i, k in enumerate(sem_keys):
        semmap[k] = es.enter_context(nc.semaphore("sm%d" % i))
    run_engine, handles = S.emit(nc, None, None)
    S._semmap = semmap
    with nc.Block() as block:
        @block.tensor
        def _(eng):
            run_engine("pe", eng)

        @block.scalar
        def _(eng):
            run_engine("act", eng)

        @block.vector
        def _(eng):
            run_engine("dve", eng)

        @block.gpsimd
        def _(eng):
            run_engine("pool", eng)

        @block.sync
        def _(eng):
            run_engine("sp", eng)
    es.close()
    return nc, S


OUT_NAMES = ["yp", "ys", "akp", "avp", "bkp", "bvp", "mkp", "mvp", "cvp", "aks", "avs", "bks", "bvs", "cvs"]
W_NAMES = ["norm_mix_g", "w_in", "a_q_norm_g", "a_k_norm_g", "a_rel_bias", "b_q_norm_g", "b_k_norm_g", "b_lam_q1", "b_lam_k1",
           "b_lam_q2", "b_lam_k2", "b_subln_g", "m_q_norm_g", "m_k_norm_g", "mem_norm_g", "w_mem_kv", "gate_b", "w_branch", "w_out",
           "norm_ffn_g", "w_ffn_up", "ffn_conv_w", "ffn_conv_b", "w_ffn_down"]


def host_consts(cfg):
    idf = np.eye(128, dtype=np.float32)
    idb = np.eye(128, dtype=np.float32).astype(ml_dtypes.bfloat16)
    jb = np.eye(128, dtype=np.float32)[::-1].copy().astype(ml_dtypes.bfloat16)
    k = np.arange(128)[:, None]
    q = np.arange(128)[None, :]
    msk = ((k // 32) == ((127 - q) // 32)).astype(np.float32)
    mskb = ((k // 32) == (q // 32)).astype(np.float32)
    half = 8
    inv_freq = np.exp(np.arange(half, dtype=np.float32) * np.float32(-2.0 * math.log(THETA) / 16)).astype(np.float32)
    pos = np.concatenate([np.arange(cfg.seq), np.tile(cfg.past + np.arange(32), 4)]).astype(np.float32)
    ang = (pos[:, None] * inv_freq[None, :]).astype(np.float32)
    cos = np.tile(np.cos(ang).astype(np.float32), (1, 8))
    sin = np.tile(np.sin(ang).astype(np.float32), (1, 8))
    return {"c_idf": idf, "c_idb": idb, "c_jb": jb, "c_msk": msk, "c_mskb": mskb,
            "c_cos": np.ascontiguousarray(cos), "c_sin": np.ascontiguousarray(sin)}


_CACHE = {}


def run(inputs, cfg, n_cores):
    key = (cfg.seq, cfg.past, cfg.depth, len(cfg.groups), cfg.dbg_skip_ffn, cfg.dbg_br, tuple(cfg.dbg_layers) if cfg.dbg_layers else None, getattr(cfg, 'dbg_stop_at', None))
    if key not in _CACHE:
        _CACHE[key] = build(cfg)
    nc, S = _CACHE[key]
    L = cfg.depth
    consts = host_consts(cfg)
    f = lambda a: np.ascontiguousarray(np.asarray(a, dtype=np.float32))
    wts = {n: f(inputs[n]) for n in W_NAMES}
    in_maps = []
    for c in range(n_cores):
        sb_ = slice(4 * c, 4 * c + 4)
        m = dict(wts)
        m.update(consts)
        m["xp"] = f(inputs["x_prompt"][c])
        m["xs"] = f(inputs["x_sample"][sb_]).reshape(128, D)
        m["cak"] = f(inputs["cache_a_k"][:, sb_]).reshape(L, 4, 512, BR)
        m["cav"] = f(inputs["cache_a_v"][:, sb_]).reshape(L, 4, 512, BR)
        m["cbk"] = f(inputs["cache_b_k"][:, sb_]).reshape(L, 4, cfg.past, BR)
        m["cbv"] = f(inputs["cache_b_v"][:, sb_]).reshape(L, 4, cfg.past, BR)
        m["cmk"] = f(inputs["cache_mem_k"][:, sb_]).reshape(L, 4, 256, BR)
        m["cmv"] = f(inputs["cache_mem_v"][:, sb_]).reshape(L, 4, 256, BR)
        m["scv"] = f(inputs["state_ffn_conv"][:, sb_]).reshape(L, 8, DFF)
        m["mem"] = f(inputs["mem_prompt"][c])
        in_maps.append(m)
    res = run_bass_kernel_spmd(nc, in_maps, core_ids=list(range(n_cores)))
    R = res.results
    g = lambda n: np.stack([np.asarray(R[c][n], dtype=np.float32) for c in range(n_cores)])
    B = n_cores
    yp = g("yp")
    ys = g("ys").reshape(B * 4, 32, D)
    AKp = cfg.akeep

    def pk(n, rows, hd):
        a = g(n)
        return np.ascontiguousarray(a.transpose(1, 0, 2, 3)).reshape(L, B, rows, BR // hd, hd)

    def sk(n, hd):
        a = g(n).reshape(B, L, 4, 32, BR)
        return np.ascontiguousarray(a.transpose(1, 0, 2, 3, 4)).reshape(L, B * 4, 32, BR // hd, hd)
    cvp = np.ascontiguousarray(g("cvp").transpose(1, 0, 2, 3))
    cvs = np.ascontiguousarray(g("cvs").reshape(B, L, 4, 2, DFF).transpose(1, 0, 2, 3, 4)).reshape(L, B * 4, 2, DFF)
    return (yp, ys, pk("akp", AKp, 128), pk("avp", AKp, 128), pk("bkp", cfg.seq, 128), pk("bvp", cfg.seq, 128),
            pk("mkp", 256, 256), pk("mvp", 256, 256), cvp, sk("aks", 128), sk("avs", 128), sk("bks", 128), sk("bvs", 128), cvs)


def kernel(**inputs):
    cfg = Cfg(seq=2048, past=4096, depth=2, ngroups=3)
    return run(inputs, cfg, 8)
```
